# Optimizing a Trainium2 kernel written in Bass

```python
import jax
import jax.numpy as jnp
from jax import lax
import numpy as np

D_MODEL = 1024
BATCH = 16
SEQ = 256
DEPTH = 4
DEC_BATCH = 8
DEC_SEQ = 2048
PAST_LEN = 512

GRID_W = 64
EPS = 1e-6
NEG_INF = -1e30
BRANCH_W = D_MODEL // 2
N_BRANCH = 3
CONV_K = 3
NA_HEADS = 8
NA_HD = BRANCH_W // NA_HEADS
WIN_R = 8
WIN_C = 16
QCB = 16
KC = 2 * WIN_C
NCB = GRID_W // QCB
ATT_BLOCK = 128
GLA_HEADS = 4
GLA_DV = BRANCH_W // GLA_HEADS
GLA_DK = GLA_DV // 2
GLA_KW = GLA_HEADS * GLA_DK
GLA_RANK = 16
GLA_TAU = 16.0
GLA_CHUNK = 64
ROPE_BASE = 10000.0
SPLIT_SIZES = (BRANCH_W, BRANCH_W, BRANCH_W, BRANCH_W,
               BRANCH_W, BRANCH_W, BRANCH_W, BRANCH_W,
               GLA_KW, GLA_KW, BRANCH_W, BRANCH_W, GLA_RANK, GLA_RANK)
D_IN = sum(SPLIT_SIZES)
SPLIT_IDX = tuple(int(i) for i in np.cumsum(SPLIT_SIZES)[:-1])

kernel_name = 'hybrid_conv_natten_gla_diffusion_step'


def rms_norm(x, g):
    xf = x.astype(jnp.float32)
    y = xf * lax.rsqrt(jnp.mean(xf * xf, axis=-1, keepdims=True) + EPS)
    return (y * g.astype(jnp.float32)).astype(x.dtype)


def to_heads(u, n_heads):
    b, t, _ = u.shape
    return u.reshape(b, t, n_heads, -1).transpose(0, 2, 1, 3)


def from_heads(u):
    b, h, t, d = u.shape
    return u.transpose(0, 2, 1, 3).reshape(b, t, h * d)


def adaln(cond, w_ada, b_ada):
    m = jax.nn.silu(cond) @ w_ada + b_ada
    if m.ndim == 2:
        m = m[:, None, :]
    return jnp.split(m, 3, axis=-1)


def short_conv(u, w):
    t = u.shape[1]
    pad = CONV_K // 2
    up = jnp.pad(u, ((0, 0), (pad, pad), (0, 0)))
    y = up[:, 0:t] * w[0]
    for i in range(1, CONV_K):
        y = y + up[:, i:i + t] * w[i]
    return y


def rope_2d(x):
    t = x.shape[2]
    pos = np.arange(t)
    n_f = GLA_DK // 4
    inv = ROPE_BASE ** (-np.arange(n_f) / n_f)
    ang_r = jnp.asarray((pos // GRID_W)[:, None] * inv, jnp.float32)
    ang_c = jnp.asarray((pos % GRID_W)[:, None] * inv, jnp.float32)

    def rot(u, ang):
        a, b = jnp.split(u, 2, axis=-1)
        cos, sin = jnp.cos(ang), jnp.sin(ang)
        return jnp.concatenate([a * cos - b * sin, a * sin + b * cos], axis=-1)

    xf = x.astype(jnp.float32)
    half = GLA_DK // 2
    return jnp.concatenate([rot(xf[..., :half], ang_r), rot(xf[..., half:], ang_c)], axis=-1).astype(x.dtype)


def na_context(q, k, v):
    b, h, t, d = q.shape
    nb = t // ATT_BLOCK
    qb = q.reshape(b, h, nb, ATT_BLOCK, d).transpose(2, 0, 1, 3, 4)

    def block(qi):
        s = jnp.einsum('bhqd,bhkd->bhqk', qi, k).astype(jnp.float32) * NA_HD ** -0.5
        p = jax.nn.softmax(s, axis=-1).astype(v.dtype)
        return jnp.einsum('bhqk,bhkd->bhqd', p, v)

    o = lax.map(block, qb)
    return o.transpose(1, 2, 0, 3, 4).reshape(b, h, t, d)


def na_latent(q, k, v, k_ctx, v_ctx, rpb):
    b, h, t, d = q.shape
    rows = t // GRID_W
    kr = min(WIN_R, rows)
    r = np.arange(rows)
    rs = np.clip(r - kr // 2, 0, rows - kr)
    row_idx = rs[:, None] + np.arange(kr)
    dr = row_idx - r[:, None]
    kcs = np.clip(np.arange(NCB) * QCB - WIN_C // 2, 0, GRID_W - KC)
    col_idx = kcs[:, None] + np.arange(KC)
    qcol = np.arange(GRID_W).reshape(NCB, QCB)
    cs = np.clip(qcol - WIN_C // 2, 0, GRID_W - WIN_C)
    kcol = col_idx[:, None, :]
    mask = (kcol >= cs[..., None]) & (kcol < cs[..., None] + WIN_C)
    dc = np.clip(kcol - qcol[..., None] + WIN_C - 1, 0, 2 * WIN_C - 2)
    key_idx = (row_idx[:, None, :, None] * GRID_W + col_idx[None, :, None, :]).reshape(rows, NCB, kr * KC)
    nw = kr * KC

    bias = rpb[:, (dr + WIN_R - 1)[:, None, None, :, None], dc[None, :, :, None, :]].astype(jnp.float32)
    bias = jnp.where(mask[None, None, :, :, None, :], bias, NEG_INF).reshape(h, rows, NCB, QCB, nw)

    qg = q.reshape(b, h, rows, NCB, QCB, d)
    kw = jnp.take(k, key_idx, axis=2)
    vw = jnp.take(v, key_idx, axis=2)
    scale = NA_HD ** -0.5
    s_win = jnp.einsum('bhrjid,bhrjmd->bhrjim', qg, kw).astype(jnp.float32) * scale + bias
    s_ctx = jnp.einsum('bhrjid,bhnd->bhrjin', qg, k_ctx).astype(jnp.float32) * scale
    p = jax.nn.softmax(jnp.concatenate([s_win, s_ctx], axis=-1), axis=-1).astype(v.dtype)
    o = (jnp.einsum('bhrjim,bhrjmd->bhrjid', p[..., :nw], vw)
         + jnp.einsum('bhrjin,bhnd->bhrjid', p[..., nw:], v_ctx))
    return o.reshape(b, h, t, d)


def gla_chunked(q, k, v, log_a, s0):
    b, h, t, dk = q.shape
    dv = v.shape[-1]
    n = t // GLA_CHUNK

    def chunks(u):
        return u.astype(jnp.float32).reshape(b, h, n, GLA_CHUNK, u.shape[-1]).transpose(2, 0, 1, 3, 4)

    lower = jnp.tril(jnp.ones((GLA_CHUNK, GLA_CHUNK), dtype=bool))[:, :, None]

    def step(s, inp):
        qc, kc, vc, lc = inp
        bc = jnp.cumsum(lc, axis=2)
        o_inter = jnp.einsum('bhtk,bhkv->bhtv', qc * jnp.exp(bc), s)
        diff = bc[:, :, :, None, :] - bc[:, :, None, :, :]
        decay = jnp.where(lower, jnp.exp(jnp.where(lower, diff, 0.0)), 0.0)
        att = jnp.einsum('bhtk,bhsk,bhtsk->bhts', qc, kc, decay)
        o = o_inter + jnp.einsum('bhts,bhsv->bhtv', att, vc)
        b_last = bc[:, :, -1, :]
        s_new = (jnp.exp(b_last)[..., None] * s
                 + jnp.einsum('bhsk,bhsv->bhkv', kc * jnp.exp(b_last[:, :, None, :] - bc), vc))
        return s_new, o

    s_fin, o = lax.scan(step, s0.astype(jnp.float32), (chunks(q), chunks(k), chunks(v), chunks(log_a)))
    return o.transpose(1, 2, 0, 3, 4).reshape(b, h, t, dv), s_fin


def gla_bidir(q, k, v, la_f, la_b, s_f0, s_b0):
    o_f, s_f = gla_chunked(q, k, v, la_f, s_f0)
    flip = lambda u: jnp.flip(u, axis=2)
    o_b, s_b = gla_chunked(flip(q), flip(k), flip(v), flip(la_b), s_b0)
    return o_f + flip(o_b), s_f, s_b


def log_decay(lr, w, b):
    z = (lr @ w + b).astype(jnp.float32)
    return to_heads(jax.nn.log_sigmoid(z) / GLA_TAU, GLA_HEADS)


def trunk_layer(x, mod, ctx, P):
    shift, scale, gate = mod
    h = rms_norm(x, P['norm_w']) * (1.0 + scale) + shift
    (xa, ba, ca, ga, qn, kn, vn, gn, qg, kg, vg, gg, lrf, lrb) = jnp.split(h @ P['w_in'], SPLIT_IDX, axis=-1)

    y_conv = ba * short_conv(ca * xa, P['conv_w']) * jax.nn.silu(ga)

    q = rms_norm(to_heads(qn, NA_HEADS), P['q_norm_w'])
    k = rms_norm(to_heads(kn, NA_HEADS), P['k_norm_w'])
    v = to_heads(vn, NA_HEADS)

    qc = to_heads(qg, GLA_HEADS)
    kc = to_heads(kg, GLA_HEADS)
    vc = to_heads(vg, GLA_HEADS)
    la_f = log_decay(lrf, P['w_alpha'][0], P['b_alpha'][0])
    la_b = log_decay(lrb, P['w_alpha'][1], P['b_alpha'][1])

    if ctx is None:
        o_na = na_context(q, k, v)
        zeros = jnp.zeros((x.shape[0], GLA_HEADS, GLA_DK, GLA_DV), jnp.float32)
        s_f0, s_b0 = zeros, zeros
    else:
        k_ctx, v_ctx, s_f0, s_b0 = ctx
        o_na = na_latent(q, k, v, k_ctx, v_ctx, P['rpb'])
        qc = rope_2d(qc)
        kc = rope_2d(kc)

    o_gla, s_f, s_b = gla_bidir(qc * GLA_DK ** -0.5, kc, vc, la_f, la_b, s_f0, s_b0)

    y_na = from_heads(o_na) * jax.nn.silu(gn)
    y_gla = from_heads(rms_norm(o_gla, P['gla_norm_w'])).astype(x.dtype) * jax.nn.silu(gg)

    g_conv, g_na, g_gla = jnp.split(jax.nn.sigmoid(h @ P['w_gate'] + P['b_gate']), N_BRANCH, axis=-1)
    wb = P['w_branch']
    merged = g_conv * (y_conv @ wb[0]) + g_na * (y_na @ wb[1]) + g_gla * (y_gla @ wb[2])
    x_new = x + gate * (merged @ P['w_out'])
    return x_new, (k, v, s_f, s_b)


def setup_inputs(seed: int = 0) -> dict:
    key = jax.random.key(seed)
    ks = jax.random.split(key, 24)
    f32 = jnp.float32

    def nrm(k, shape, s):
        return jax.random.normal(k, shape, f32) * s

    return {
        'x_prompt': nrm(ks[0], (BATCH, SEQ, D_MODEL), 1.0),
        'x_sample': nrm(ks[1], (DEC_BATCH, DEC_SEQ, D_MODEL), 1.0),
        'c': nrm(ks[2], (DEC_BATCH, D_MODEL), 1.0),
        'cache_k': nrm(ks[3], (DEC_BATCH, DEPTH, NA_HEADS, PAST_LEN, NA_HD), 1.0),
        'cache_v': nrm(ks[4], (DEC_BATCH, DEPTH, NA_HEADS, PAST_LEN, NA_HD), 1.0),
        'state_fwd': nrm(ks[5], (DEC_BATCH, DEPTH, GLA_HEADS, GLA_DK, GLA_DV), 0.5),
        'state_bwd': nrm(ks[6], (DEC_BATCH, DEPTH, GLA_HEADS, GLA_DK, GLA_DV), 0.5),
        'c_ctx': nrm(ks[7], (D_MODEL,), 1.0),
        'norm_w': 1.0 + nrm(ks[8], (DEPTH, D_MODEL), 0.01),
        'w_ada': nrm(ks[9], (DEPTH, D_MODEL, 3 * D_MODEL), D_MODEL ** -0.5),
        'b_ada': nrm(ks[10], (DEPTH, 3 * D_MODEL), 0.02),
        'w_in': nrm(ks[11], (DEPTH, D_MODEL, D_IN), D_MODEL ** -0.5),
        'conv_w': nrm(ks[12], (DEPTH, CONV_K, BRANCH_W), CONV_K ** -0.5),
        'q_norm_w': 1.0 + nrm(ks[13], (DEPTH, NA_HD), 0.01),
        'k_norm_w': 1.0 + nrm(ks[14], (DEPTH, NA_HD), 0.01),
        'rpb': nrm(ks[15], (DEPTH, NA_HEADS, 2 * WIN_R - 1, 2 * WIN_C - 1), 0.1),
        'w_alpha': nrm(ks[16], (DEPTH, 2, GLA_RANK, GLA_KW), GLA_RANK ** -0.5),
        'b_alpha': nrm(ks[17], (DEPTH, 2, GLA_KW), 0.1),
        'gla_norm_w': 1.0 + nrm(ks[18], (DEPTH, GLA_DV), 0.01),
        'w_branch': nrm(ks[19], (DEPTH, N_BRANCH, BRANCH_W, D_MODEL), BRANCH_W ** -0.5),
        'w_gate': nrm(ks[20], (DEPTH, D_MODEL, N_BRANCH * D_MODEL), D_MODEL ** -0.5),
        'b_gate': nrm(ks[21], (DEPTH, N_BRANCH * D_MODEL), 0.02),
        'w_out': nrm(ks[22], (DEPTH, D_MODEL, D_MODEL), D_MODEL ** -0.5),
    }


def reference(x_prompt, x_sample, c, cache_k, cache_v, state_fwd, state_bwd, c_ctx,
              norm_w, w_ada, b_ada, w_in, conv_w, q_norm_w, k_norm_w, rpb,
              w_alpha, b_alpha, gla_norm_w, w_branch, w_gate, b_gate, w_out):
    y_p = x_prompt
    y_s = x_sample
    ks, vs, sfs, sbs = [], [], [], []
    for l in range(DEPTH):
        P = {
            'norm_w': norm_w[l], 'w_in': w_in[l], 'conv_w': conv_w[l],
            'q_norm_w': q_norm_w[l], 'k_norm_w': k_norm_w[l], 'rpb': rpb[l],
            'w_alpha': w_alpha[l], 'b_alpha': b_alpha[l], 'gla_norm_w': gla_norm_w[l],
            'w_branch': w_branch[l], 'w_gate': w_gate[l], 'b_gate': b_gate[l], 'w_out': w_out[l],
        }
        mod_ctx = adaln(c_ctx, w_ada[l], b_ada[l])
        y_p, (k_l, v_l, sf_l, sb_l) = trunk_layer(y_p, mod_ctx, None, P)
        ks.append(k_l)
        vs.append(v_l)
        sfs.append(sf_l)
        sbs.append(sb_l)
        mod_lat = adaln(c, w_ada[l], b_ada[l])
        ctx = (cache_k[:, l], cache_v[:, l], state_fwd[:, l], state_bwd[:, l])
        y_s, _ = trunk_layer(y_s, mod_lat, ctx, P)
    return (y_p, y_s, jnp.stack(ks, axis=1), jnp.stack(vs, axis=1), jnp.stack(sfs, axis=1), jnp.stack(sbs, axis=1))
```

```python
import numpy as np
from contextlib import ExitStack
import concourse.bass as bass
import concourse.mybir as mybir
from concourse.bass_utils import run_bass_kernel_spmd

F32 = mybir.dt.float32
BF16 = mybir.dt.bfloat16
AF = mybir.ActivationFunctionType
ALU = mybir.AluOpType

NL = 4
D = 1024
NT = 2560
NTT = 20
NG = 5
WEXT = 6176
EPS = 1e-6
NEG = -1e30
ENG = ['pe', 'act', 'dve', 'pool', 'sp']


class Sch:
    def __init__(self):
        self.ops = {e: [] for e in ENG}
        self.lastw = {}
        self.readers = {}
        self.dma_count = {}
        self.regions = set()
        self.barrier_op = None

    def add(self, eng, fn, R=(), W=(), dma_key=None):
        op = dict(eng=eng, fn=fn, deps=[], signal=False, dma_key=dma_key, dma_ord=0)
        if dma_key is not None:
            self.dma_count[dma_key] = self.dma_count.get(dma_key, 0) + 1
            op['dma_ord'] = self.dma_count[dma_key]
        deps = {}

        def dep(o, raw):
            if o is None:
                return
            if o['dma_key'] is None and o['eng'] == eng and dma_key is None:
                if eng == 'pe':
                    return
            deps[id(o)] = o

        for r in R:
            self.regions.add(r)
            dep(self.lastw.get(r, self.barrier_op), True)
            if r.startswith('pb') or r.startswith('pt'):
                for o in self.readers.get(r, {}).values():
                    if o['eng'] != eng:
                        dep(o, True)
        for w in W:
            self.regions.add(w)
            dep(self.lastw.get(w, self.barrier_op), False)
            for o in self.readers.get(w, {}).values():
                dep(o, False)
        for r in R:
            self.readers.setdefault(r, {})[(eng, dma_key)] = op
        for w in W:
            self.lastw[w] = op
            self.readers[w] = {}
        op['deps'] = list(deps.values())
        for o in op['deps']:
            if o['dma_key'] is None:
                o['signal'] = True
        self.ops[eng].append(op)
        return op

    def emit(self, nc):
        for e in ENG:
            c = 0
            for op in self.ops[e]:
                if op['signal']:
                    c += 1
                op['ord'] = c
        with ExitStack() as es:
            sems = {e: es.enter_context(nc.semaphore('s_' + e)) for e in ENG}
            dsems = {k: es.enter_context(nc.semaphore('d%d' % i)) for i, k in enumerate(self.dma_count)}
            block = es.enter_context(nc.Block())

            def run(e, h):
                seen = {}
                for op in self.ops[e]:
                    need = {}
                    for o in op['deps']:
                        if o['dma_key'] is not None:
                            key = ('d', o['dma_key'])
                            val = 16 * o['dma_ord']
                        else:
                            key = ('e', o['eng'])
                            val = o['ord']
                        need[key] = max(need.get(key, 0), val)
                    for key, val in need.items():
                        if seen.get(key, 0) >= val:
                            continue
                        seen[key] = val
                        h.wait_ge(dsems[key[1]] if key[0] == 'd' else sems[key[1]], val)
                    ins = op['fn'](h)
                    if op['dma_key'] is not None:
                        ins.then_inc(dsems[op['dma_key']], 16)
                    elif op['signal']:
                        ins.then_inc(sems[e], 1)
                if e == 'sp':
                    for k, c in self.dma_count.items():
                        h.wait_ge(dsems[k], 16 * c)

            @block.tensor
            def _(h):
                run('pe', h)

            @block.scalar
            def _(h):
                run('act', h)

            @block.vector
            def _(h):
                run('dve', h)

            @block.gpsimd
            def _(h):
                run('pool', h)

            @block.sync
            def _(h):
                run('sp', h)


class _Stop(Exception):
    pass


def build_nc(stop=None):
    nc = bass.Bass("TRN2", target_bir_lowering=False)
    S = Sch()

    def din(name, shape):
        return nc.dram_tensor(name, list(shape), F32, kind="ExternalInput").ap()

    def dout(name, shape):
        return nc.dram_tensor(name, list(shape), F32, kind="ExternalOutput").ap()

    xin = din("xin", [NT, D])
    w_in = din("w_in", [NL, D, WEXT])
    w_ada = din("w_ada", [NL, D, 3 * D])
    w_gate = din("w_gate", [NL, D, 3 * D])
    w_br = din("w_br", [NL, 3, 512, D])
    w_out = din("w_out", [NL, D, D])
    cond = din("cond", [128, 16])
    pvec = din("pvec", [NL, 128, 72])
    walpha = din("walpha", [NL, 33, 512])
    b2tab = din("b2tab", [NL, 128, 8 * 1024])
    b2tabi = din("b2tabi", [NL, 128, 8 * 640])
    ck = din("ck", [NL, 8, 512, 64])
    cv = din("cv", [NL, 8, 512, 64])
    sf = din("sf", [NL, 4, 64, 128])
    sb = din("sb", [NL, 4, 64, 128])
    cst = din("cst", [128, 8 * 128])
    rope = din("rope", [2, 128, 2048])
    yout = dout("y", [NT, D])
    nk = dout("nk", [2, NL, 8, 256, 64])
    nv = dout("nv", [2, NL, 8, 256, 64])
    nsf = dout("nsf", [2, NL, 4, 64, 128])
    nsb = dout("nsb", [2, NL, 4, 64, 128])
    xsA = nc.dram_tensor("xsA", [NT, D], F32).ap()
    xsB = nc.dram_tensor("xsB", [NT, D], F32).ap()

    ARENA = 74 * 1024
    es = ExitStack()
    sb_ = lambda n, sh, dt: es.enter_context(nc.sbuf_tensor(n, sh, dt))
    hT = sb_("hT", [128, 8, NT], BF16)
    yT = sb_("yT", [128, 12, NT], BF16)
    NWS = 6
    wsl = sb_("wsl", [128, NWS, 8, 128], BF16)
    arena = sb_("arena", [128, ARENA // 4], F32)
    cstf = sb_("cstf", [128, 5, 128], F32)
    cstb = sb_("cstb", [128, 5, 128], BF16)
    condt = sb_("condt", [128, 16], F32)
    scb = sb_("scb", [128, 16], BF16)
    pv = sb_("pv", [128, 72], F32)
    modTs = [sb_("modT%d" % i, [128, 24, 2], F32) for i in range(2)]
    Amods = [sb_("Amod%d" % i, [128, 8, 2], F32) for i in range(2)]
    pvn = sb_("pvn", [128, 32], F32)
    wq8 = sb_("wq8", [128, 1], F32)
    wal = sb_("wal", [33, 512], BF16)
    tf = sb_("tf", [128, 4, 512], F32)
    tb = sb_("tb", [128, 4, 512], BF16)
    sml = sb_("sml", [128, 64], F32)
    NPB = 8
    pbs = [es.enter_context(nc.psum_tensor("pb%d" % i, [128, 512], F32)) for i in range(NPB)]

    identf, Uf, Ub, SU, SL = [cstf[:, i, :] for i in range(5)]
    identb, Mf, Mb, bd64, ones1k = [cstb[:, i, :] for i in range(5)]
    Umat = [Uf, Ub]
    SUL = [SU, SL]
    Mmask = [Mf, Mb]

    cnt = dict(pb=0, pt=0, ws=0, tf=0, tb=0, P=0)

    def nbank(idx=None):
        if idx is None:
            idx = cnt['pb'] % 6
            cnt['pb'] += 1
        return pbs[idx][:], "pb%d" % idx

    def ntbank(idx=None):
        if idx is None:
            idx = 6 + cnt['pt'] % 2
            cnt['pt'] += 1
        return pbs[idx][:].bitcast(BF16), "pb%d" % idx

    def ntf():
        i = cnt['tf'] % 4
        cnt['tf'] += 1
        return tf[:, i, :], "tf%d" % i

    def ntb():
        i = cnt['tb'] % 4
        cnt['tb'] += 1
        return tb[:, i, :], "tb%d" % i

    def MM(out, lhsT, rhs, start, stop, R, W):
        S.add('pe', lambda h: h.matmul(out, lhsT, rhs, start=start, stop=stop), R, W)

    def TR(out, in_, ident, R, W):
        S.add('pe', lambda h: h.transpose(out, in_, ident), R, W)

    def ACT(out, in_, func, R, W, bias=None, scale=None, accum=None):
        kw = {}
        if bias is not None:
            kw['bias'] = bias
        if scale is not None:
            kw['scale'] = scale
        if accum is not None:
            kw['accum_out'] = accum
        S.add('act', lambda h: h.activation(out, in_, func, **kw), R, W)

    def engname(e):
        return e

    def TT(e, out, in0, in1, op, R, W):
        S.add(e, lambda h: h.tensor_tensor(out, in0, in1, op), R, W)

    def TS(e, out, in0, s1, s2, op0, op1, R, W):
        if s2 is None:
            S.add(e, lambda h: h.tensor_scalar(out, in0, s1, None, op0), R, W)
        else:
            S.add(e, lambda h: h.tensor_scalar(out, in0, s1, s2, op0, op1), R, W)

    def STT(out, in0, sc, in1, op0, op1, R, W):
        S.add('dve', lambda h: h.scalar_tensor_tensor(out, in0, sc, in1, op0, op1), R, W)

    def CP(e, out, in_, R, W):
        if e == 'act':
            S.add('act', lambda h: h.copy(out, in_), R, W)
        else:
            S.add(e, lambda h: h.tensor_copy(out, in_), R, W)

    def MSET(e, ap, val, W):
        S.add(e, lambda h: h.memset(ap, val), (), W)

    def RCP(out, in_, R, W):
        S.add('dve', lambda h: h.reciprocal(out, in_), R, W)

    def DMA(q, out, in_, R, W, key):
        S.add(q, lambda h: h.dma_start(out=out, in_=in_), R, W, dma_key=key)

    def barrier():
        regs = sorted(S.regions)
        MSET('pool', sml[:, 63:64], 0.0, regs + ['bar'])
        S.barrier_op = S.ops['pool'][-1]

    pre = {}

    def wpre(tag, src, nk_=8, ncols=128):
        pre[tag] = wload(src, nk_, ncols)

    def wload(src, nk_=8, ncols=128, tag=None):
        if tag is not None and tag in pre:
            return pre.pop(tag)
        i = cnt['ws'] % NWS
        cnt['ws'] += 1
        DMA('pool', wsl[:, i, 0:nk_, 0:ncols], src.rearrange("(kc p) n -> p kc n", p=128), (), ["ws%d" % i], "ws%d" % i)
        return i

    def hreg(g):
        return ["hT%d" % t for t in range(4 * g, 4 * g + 4)]

    def proj(slot, g, ncols=128):
        bank, br = nbank()
        for kc in range(8):
            MM(bank[0:ncols, :], wsl[:, slot, kc, 0:ncols], hT[:, kc, g * 512:(g + 1) * 512], kc == 0, kc == 7,
               ["ws%d" % slot] + hreg(g), [br])
        return bank, br

    class Carve:
        def __init__(self):
            self.off = 0

        def get(self, shape, dt):
            n = int(np.prod(shape[1:])) * (4 if dt == F32 else 2)
            n = (n + 31) // 32 * 32
            a = self.off // 4
            self.off += n
            assert self.off <= ARENA, ("arena overflow", self.off)
            v = arena[0:shape[0], a:a + n // 4]
            if dt == BF16:
                v = v.bitcast(BF16)
            ne = int(np.prod(shape[1:]))
            v = v[:, 0:ne]
            if len(shape) == 3:
                v = v.rearrange("p (a b) -> p a b", a=shape[1])
            elif len(shape) == 4:
                v = v.rearrange("p (a b c) -> p a b c", a=shape[1], b=shape[2])
            return v

    DMA('sp', cstf[:], cst[:, 0:640].rearrange("p (a b) -> p a b", a=5), (), ['cstf'], 'cstf')
    DMA('pool', cstb[:, 0, :], cst[:, 0:128], (), ['cstb'], 'cstb')
    DMA('pool', cstb[:, 1:4, :], cst[:, 640:1024].rearrange("p (a b) -> p a b", a=3), (), ['cstb'], 'cstb')
    MSET('pool', cstb[:, 4, :], 1.0 / 1024.0, ['cstb'])
    DMA('sp', condt[:], cond, (), ['condt'], 'condt')
    ACT(scb[:], condt[:], AF.Silu, ['condt'], ['scb'])

    xbufs = [(xin, 'xin'), (xsA, 'xsA'), (xsB, 'xsB'), (xsA, 'xsA'), (yout, 'yo')]
    if stop is not None:
        MSET('pool', yT[:].rearrange("p a b -> p (a b)"), 0.0, ['yTinit'])
    SEQS = [(0, 256, 0), (256, 256, 0), (512, 2048, 1)]

    def ucol(tok):
        return tok + 1 + 2 * (0 if tok < 256 else (1 if tok < 512 else 2))

    def mod_begin(lm):
        DMA('sp', pvn[:, 0:24], pvec[lm][:, 0:24], (), ['pvn'], 'pvn')
        DMA('sp', pvn[:, 24:32], pvec[lm][:, 48:56], (), ['pvn'], 'pvn')

    def mod_chunk(lm, j, pm, pmr):
        sl = wload(w_ada[lm][:, j * 128:(j + 1) * 128])
        for kc in range(8):
            MM(pm[:, 2 * j:2 * j + 2], wsl[:, sl, kc, :], scb[:, 2 * kc:2 * kc + 2], kc == 0, kc == 7,
               ["ws%d" % sl, 'scb'], [pmr])

    def mod_finish(lm, pm, pmr):
        mT, mr = modTs[lm % 2], "modT%d" % (lm % 2)
        aT, ar = Amods[lm % 2], "Amod%d" % (lm % 2)
        TT('dve', mT[:], pm[:, 0:48].rearrange("p (a b) -> p a b", b=2),
           pvn[:, 0:24].unsqueeze(2).broadcast_to([128, 24, 2]), ALU.add, [pmr, 'pvn'], [mr])
        TS('dve', aT[:], mT[:, 8:16, :], 1.0, None, ALU.add, None, [mr], [ar])
        TT('dve', aT[:], aT[:], pvn[:, 24:32].unsqueeze(2).broadcast_to([128, 8, 2]), ALU.mult, [ar, 'pvn'], [ar])

    mod_begin(0)
    pm0, pm0r = nbank(7)
    for j0 in range(24):
        mod_chunk(0, j0, pm0, pm0r)
    mod_finish(0, pm0, pm0r)

    def layer(l):
        modT, modTr = modTs[l % 2], "modT%d" % (l % 2)
        Amod, Amodr = Amods[l % 2], "Amod%d" % (l % 2)
        xcur, xcn = xbufs[l]
        xnxt, xnn = xbufs[l + 1]
        barrier()
        DMA('sp', pv[:], pvec[l], (), ['pv'], 'pv')
        DMA('pool', wal[:], walpha[l], (), ['wal'], 'wal')
        TS('dve', wq8[:], pv[:, 68:69], 0.125, None, ALU.mult, None, ['pv'], ['wq8'])

        cv_ = Carve()
        xt = [cv_.get([128, 1024], F32) for _ in range(4)]
        xn = [cv_.get([128, 1024], BF16) for _ in range(3)]
        sqj = cv_.get([128, 1024], BF16)

        def X1(tt):
            b = tt % 4
            DMA('sp', xt[b], xcur[tt * 128:(tt + 1) * 128, :], ["%s%d" % (xcn, tt)], ["xt%d" % b], "xt%d" % b)

        def X2(tt):
            b = tt % 4
            ACT(sqj, xt[b], AF.Square, ["xt%d" % b], ['sqj', "ss%d" % b], accum=sml[:, b:b + 1])
            ACT(sml[:, 4 + b:5 + b], sml[:, b:b + 1], AF.Ln, ["ss%d" % b], ["ln%d" % b], bias=EPS, scale=1.0 / D)
            ACT(sml[:, 8 + b:9 + b], sml[:, 4 + b:5 + b], AF.Exp, ["ln%d" % b], ["rs%d" % b], scale=-0.5)

        def X3(tt):
            b = tt % 4
            n3 = tt % 3
            TS('dve', xn[n3], xt[b], sml[:, 8 + b:9 + b], None, ALU.mult, None, ["xt%d" % b, "rs%d" % b], ["xn%d" % n3])
            pt, ptr = ntbank(6 + tt % 2)
            for c in range(8):
                TR(pt[:, c * 128:(c + 1) * 128], xn[n3][:, c * 128:(c + 1) * 128], identb, ["xn%d" % n3, 'cstb'], [ptr])

        def X4(tt):
            s = 0 if tt < 4 else 1
            pt, ptr = ntbank(6 + tt % 2)
            for c in range(8):
                o = hT[:, c, tt * 128:(tt + 1) * 128]
                i = pt[:, c * 128:(c + 1) * 128]
                if tt % 2 == 0:
                    ACT(o, i, AF.Identity, [ptr, Amodr, modTr], ["hT%d" % tt], bias=modT[:, c, s:s + 1],
                        scale=Amod[:, c, s:s + 1])
                else:
                    TS('dve', o, i, Amod[:, c, s:s + 1], modT[:, c, s:s + 1], ALU.mult, ALU.add,
                       [ptr, Amodr, modTr], ["hT%d" % tt])

        for j in range(-2, NTT + 1):
            for fn_, off in ((X4, -1), (X3, 0), (X2, 1), (X1, 2)):
                if 0 <= j + off < NTT:
                    fn_(j + off)
        wpre(('A', l), w_in[l][:, 0:128])
        barrier()
        if stop == (l, 'X'):
            raise _Stop()

        def gate_apply(col0, ych):
            sl = wload(w_in[l][:, col0:col0 + 128])
            for g in range(NG):
                bank, br = proj(sl, g)
                t, tr = ntb()
                ACT(t, bank[:], AF.Silu, [br], [tr])
                yv = yT[:, ych, g * 512:(g + 1) * 512]
                TT('dve', yv, yv, t, ALU.mult, [tr, "y%d_%d" % (ych, g)], ["y%d_%d" % (ych, g)])

        cv_ = Carve()
        ubuf = cv_.get([128, NT + 6], F32)
        MSET('pool', ubuf, 0.0, ["u%d" % g for g in range(NG)])
        for cc in range(4):
            s_xa = wload(w_in[l][:, cc * 128:(cc + 1) * 128], tag=('A', l) if cc == 0 else None)
            s_ca = wload(w_in[l][:, 1024 + cc * 128:1024 + (cc + 1) * 128])
            for g in range(NG):
                b1, r1 = proj(s_xa, g)
                b2, r2 = proj(s_ca, g)
                t, tr = ntf()
                CP('act', t, b1[:], [r1], [tr])
                if g == 0:
                    for hh in range(2):
                        TT('dve', ubuf[:, ucol(hh * 256):ucol(hh * 256) + 256], b2[:, hh * 256:(hh + 1) * 256],
                           t[:, hh * 256:(hh + 1) * 256], ALU.mult, [r2, tr], ["u0"])
                else:
                    TT('dve', ubuf[:, ucol(g * 512):ucol(g * 512) + 512], b2[:], t, ALU.mult, [r2, tr], ["u%d" % g])
            s_ba = wload(w_in[l][:, 512 + cc * 128:512 + (cc + 1) * 128])
            for g in range(NG):
                b3, r3 = proj(s_ba, g)
                t, tr = ntf()
                segs = [(0, 256), (256, 256)] if g == 0 else [(g * 512, 512)]
                ur = ["u%d" % gg for gg in range(max(0, g - 1), min(NG, g + 2))]
                for (t0, n) in segs:
                    u0 = ucol(t0)
                    o = t[:, t0 - g * 512:t0 - g * 512 + n]
                    TS('dve', o, ubuf[:, u0 - 1:u0 - 1 + n], pv[:, 56 + cc * 3:57 + cc * 3], None, ALU.mult, None,
                       ur + ['pv'], [tr])
                    STT(o, ubuf[:, u0:u0 + n], pv[:, 57 + cc * 3:58 + cc * 3], o, ALU.mult, ALU.add, ur + ['pv', tr], [tr])
                    STT(o, ubuf[:, u0 + 1:u0 + 1 + n], pv[:, 58 + cc * 3:59 + cc * 3], o, ALU.mult, ALU.add,
                        ur + ['pv', tr], [tr])
                TT('dve', yT[:, cc, g * 512:(g + 1) * 512], b3[:], t, ALU.mult, [r3, tr], ["y%d_%d" % (cc, g)])
            gate_apply(1536 + cc * 128, cc)
        wpre(('B', l), w_in[l][:, 2048:2048 + 128])
        barrier()
        if stop == (l, 'A'):
            raise _Stop()

        cv_ = Carve()
        qT = cv_.get([128, NT], BF16)
        kT = cv_.get([128, NT], BF16)
        vaug = cv_.get([128, NTT, 2, 65], BF16)
        vodd = cv_.get([128, 15, 2, 65], BF16)
        kcfs = [cv_.get([128, 4, 2, 64], F32) for _ in range(2)]
        kctxTs = [cv_.get([128, 512], BF16) for _ in range(2)]
        vctxs = [cv_.get([128, 4, 2, 65], BF16) for _ in range(2)]
        B2s = [cv_.get([128, 2, 16, 64], BF16) for _ in range(2)]
        B2is = [cv_.get([128, 2, 640], BF16) for _ in range(2)]
        Pbig = [cv_.get([128, 1152], BF16) for _ in range(6)]
        onb = [cv_.get([128, 128], BF16) for _ in range(2)]
        kst = cv_.get([128, 4, 128], F32)
        knb = cv_.get([128, 512], F32)
        vst = cv_.get([128, 4, 128], F32)
        rcb = [cv_.get([128, 2], F32) for _ in range(2)]
        for i2 in range(2):
            MSET('pool', vctxs[i2][:, :, :, 64:65], 1.0, ["vctx%d" % i2])

        def na_prefetch(hp_):
            i2 = hp_ % 2
            for e in range(2):
                DMA('sp', kcfs[i2][:, :, e, :], ck[l, 2 * hp_ + e].rearrange("(kt p) d -> p kt d", p=128), (), ["kcf%d" % i2],
                    'kcf%d_%d' % (i2, e))
                DMA('pool', vctxs[i2][:, :, e, 0:64], cv[l, 2 * hp_ + e].rearrange("(kt p) d -> p kt d", p=128), (),
                    ["vctx%d" % i2], 'vctx%d_%d' % (i2, e))
            DMA('pool', B2s[i2][:], b2tab[l][:, 2 * hp_ * 1024:(2 * hp_ + 2) * 1024].rearrange("p (e u q) -> p e u q", e=2, u=16),
                (), ["B2%d" % i2], 'B2%d' % i2)
            DMA('pool', B2is[i2][:], b2tabi[l][:, 2 * hp_ * 640:(2 * hp_ + 2) * 640].rearrange("p (e q) -> p e q", e=2),
                (), ["B2i%d" % i2], 'B2i%d' % i2)
            v2 = B2s[i2][:].rearrange("p e u q -> p (e u q)")
            ACT(v2, v2, AF.Exp, ["B2%d" % i2], ["B2%d" % i2])
            v2 = B2is[i2][:].rearrange("p e q -> p (e q)")
            ACT(v2, v2, AF.Exp, ["B2i%d" % i2], ["B2i%d" % i2])

        na_prefetch(0)
        for hp in range(4):
            i2 = hp % 2
            kcf, kctxT, vctx, B2, B2i = kcfs[i2], kctxTs[i2], vctxs[i2], B2s[i2], B2is[i2]
            rkc, rvc, rb2, rb2i = "kctxT%d" % i2, "vctx%d" % i2, "B2%d" % i2, "B2i%d" % i2
            MSET('pool', vaug[:, :, :, 64:65], 1.0, ['vaug'])
            bk, bkr = nbank()
            for kt in range(4):
                TR(bk[:, kt * 128:(kt + 1) * 128], kcf[:, kt].rearrange("p e d -> p (e d)"), identf, ["kcf%d" % i2, 'cstf'], [bkr])
            CP('dve', kctxT, bk[:], [bkr], [rkc])
            for which in range(2):
                sl = wload(w_in[l][:, 2048 + which * 512 + hp * 128:2048 + which * 512 + (hp + 1) * 128],
                           tag=('B', l) if (which == 0 and hp == 0) else None)
                dst = qT if which == 0 else kT
                dn = 'qT' if which == 0 else 'kT'
                wcol = wq8[:, 0:1] if which == 0 else pv[:, 69:70]
                def qk1(g):
                    bank, br = proj(sl, g)
                    f, fr = ntf()
                    CP('act', f, bank[:], [br], [fr])
                    sq, sqr = ntb()
                    ACT(sq, bank[:], AF.Square, [br], [sqr])
                    return (f, fr, sq, sqr)

                def qk2(g, st_):
                    f, fr, sq, sqr = st_
                    b2_, b2r = nbank()
                    MM(b2_[:], bd64, sq, True, True, [sqr, 'cstb'], [b2r])
                    rs, rsr = ntf()
                    ACT(rs, b2_[:], AF.Ln, [b2r], [rsr], bias=EPS)
                    ACT(rs, rs, AF.Exp, [rsr], [rsr], scale=-0.5)
                    STT(dst[:, g * 512:(g + 1) * 512], f, wcol, rs, ALU.mult, ALU.mult, [fr, rsr, 'pv', 'wq8'],
                        [dn])
                    if which == 1 and g == 0:
                        kn, knr = knb, 'knb'
                        STT(kn, f, wcol, rs, ALU.mult, ALU.mult, [fr, rsr, 'pv'], [knr])
                        b3, b3r = nbank()
                        for j in range(4):
                            TR(b3[:, j * 128:(j + 1) * 128], kn[:, j * 128:(j + 1) * 128], identf, [knr, 'cstf'], [b3r])
                        CP('dve', kst[:].rearrange("p a b -> p (a b)"), b3[:], [b3r], ['kst'])
                        for j in range(4):
                            for e in range(2):
                                DMA('sp', nk[j // 2, l, 2 * hp + e, (j % 2) * 128:(j % 2) * 128 + 128, :],
                                    kst[:, j, e * 64:(e + 1) * 64], ['kst'], ['nk'], 'kst')

                st_ = qk1(0)
                for g in range(NG):
                    nx_ = qk1(g + 1) if g + 1 < NG else None
                    qk2(g, st_)
                    st_ = nx_
            sl = wload(w_in[l][:, 3072 + hp * 128:3072 + (hp + 1) * 128])
            for g in range(NG):
                bank, br = nbank()
                for j in range(4):
                    tt = 4 * g + j
                    for kc in range(8):
                        MM(bank[:, j * 128:(j + 1) * 128], hT[:, kc, tt * 128:(tt + 1) * 128], wsl[:, sl, kc, :], kc == 0, kc == 7,
                           ["ws%d" % sl] + hreg(g), [br])
                CP('act', vaug[:, 4 * g:4 * g + 4, :, 0:64], bank[:].rearrange("p (a e d) -> p a e d", a=4, e=2), [br], ['vaug'])
                if g == 0:
                    CP('dve', vst[:].rearrange("p a b -> p (a b)"), bank[:], [br], ['vst'])
                    for j in range(4):
                        for e in range(2):
                            DMA('sp', nv[j // 2, l, 2 * hp + e, (j % 2) * 128:(j % 2) * 128 + 128, :],
                                vst[:, j, e * 64:(e + 1) * 64], ['vst'], ['nv'], 'vst')

            def att_p1(q0, nq, tiles_fn):
                st = []
                tpb = 512 // nq
                for e in range(2):
                    tiles = tiles_fn(e)
                    nt_ = len(tiles)
                    banks = [nbank() for _ in range((nt_ + tpb - 1) // tpb)]
                    qa = qT[e * 64:(e + 1) * 64, q0:q0 + nq]
                    for t, (ka, va, ba) in enumerate(tiles):
                        bs, bsr = banks[t // tpb]
                        c0 = (t % tpb) * nq
                        MM(bs[:, c0:c0 + nq], ka, qa, True, True, ['qT', 'kT', rkc], [bsr])
                    k_ = cnt['P'] % 6
                    cnt['P'] += 1
                    p_, pr = Pbig[k_], "Pb%d" % k_
                    for j, (bs, bsr) in enumerate(banks):
                        ncol = min(tpb, nt_ - j * tpb) * nq
                        ACT(p_[:, j * 512:j * 512 + ncol], bs[:, 0:ncol], AF.Exp, [bsr], [pr])
                    if nq == 128 and tiles[0][2] is not None:
                        TT('dve', p_[:, 0:640], p_[:, 0:640], B2i[:, e, :], ALU.mult, [pr, rb2i], [pr])
                    else:
                        for t, (ka, va, ba) in enumerate(tiles):
                            if ba is not None:
                                TT('dve', p_[:, t * nq:(t + 1) * nq], p_[:, t * nq:(t + 1) * nq], ba, ALU.mult, [pr, rb2], [pr])
                    st.append((tiles, p_, pr))
                return (q0, nq, st)

            def att_p2(state, bi):
                q0, nq, st = state
                bo, bor = nbank(6)
                on = onb[bi]
                onr = "on%d" % bi
                rc = rcb[bi]
                rcr = "rc%d" % bi
                for e in range(2):
                    tiles, p_, pr = st[e]
                    nt_ = len(tiles)
                    for t, (ka, va, ba) in enumerate(tiles):
                        MM(bo[0:nq, e * 65:(e + 1) * 65], p_[:, t * nq:(t + 1) * nq], va, t == 0, t == nt_ - 1,
                           [pr, 'vaug', rvc], [bor])
                for e in range(2):
                    RCP(rc[0:nq, e:e + 1], bo[0:nq, e * 65 + 64:e * 65 + 65], [bor], ["%s_%d" % (rcr, e)])
                    TS('dve', on[0:nq, e * 64:(e + 1) * 64], bo[0:nq, e * 65:e * 65 + 64], rc[0:nq, e:e + 1], None, ALU.mult, None,
                       [bor, "%s_%d" % (rcr, e)], ["%s_%d" % (onr, e)])
                return (q0, nq, bi)

            def att_p3(st3):
                q0, nq, bi = st3
                on = onb[bi]
                onr = "on%d" % bi
                pt, ptr = ntbank(7)
                TR(pt[:, 0:nq], on[0:nq, :], identb[0:nq, 0:nq], [onr + "_0", onr + "_1", 'cstb'], [ptr])
                CP('dve', yT[:, 4 + hp, q0:q0 + nq], pt[:, 0:nq], [ptr], ["y%d_%d" % (4 + hp, q0 // 512)])

            blocks = []
            for s in range(2):
                for qb in range(2):
                    def tf_(e, s=s):
                        return [(kT[e * 64:(e + 1) * 64, s * 256 + kt * 128:s * 256 + (kt + 1) * 128], vaug[:, s * 2 + kt, e, :], None)
                                for kt in range(2)]
                    blocks.append((s * 256 + qb * 128, 128, tf_))

            def ctx_tiles(e):
                return [(kctxT[e * 64:(e + 1) * 64, t * 128:(t + 1) * 128], vctx[:, t, e, :], None) for t in range(4)]

            def single_row(r):
                rs_ = min(max(r - 4, 0), 24)

                def tf_(e):
                    tl = []
                    for t in range(4):
                        kr0 = rs_ + 2 * t
                        tl.append((kT[e * 64:(e + 1) * 64, 512 + kr0 * 64:512 + kr0 * 64 + 128], vaug[:, 4 + kr0 // 2, e, :],
                                   B2[:, e, kr0 - r + 8, :]))
                    return tl + ctx_tiles(e)
                return (512 + r * 64, 64, tf_)

            def pair_rows(r):
                def tf_(e):
                    tl = []
                    for t in range(5):
                        kr0 = r - 4 + 2 * t
                        w = 11 - 2 * t
                        tl.append((kT[e * 64:(e + 1) * 64, 512 + kr0 * 64:512 + kr0 * 64 + 128], vaug[:, 4 + kr0 // 2, e, :],
                                   B2i[:, e, t * 128:(t + 1) * 128]))
                    return tl + ctx_tiles(e)
                return (512 + r * 64, 128, tf_)

            for r in range(4):
                blocks.append(single_row(r))
            for r in range(4, 28, 2):
                blocks.append(pair_rows(r))
            for r in range(28, 32):
                blocks.append(single_row(r))
            if hp + 1 < 4:
                na_prefetch(hp + 1)
            stq = [att_p1(*blocks[0]), att_p1(*blocks[1])]
            prev3 = None
            for i in range(len(blocks)):
                if i + 2 < len(blocks):
                    stq.append(att_p1(*blocks[i + 2]))
                cur3 = att_p2(stq.pop(0), i % 2)
                if prev3 is not None:
                    att_p3(prev3)
                prev3 = cur3
            att_p3(prev3)
            gate_apply(2048 + 1536 + hp * 128, 4 + hp)
        wpre(('C', l), w_in[l][:, 4096:4096 + 128])
        barrier()
        if stop == (l, 'B'):
            raise _Stop()

        cv_ = Carve()
        qrT = cv_.get([128, NT], BF16)
        krT = cv_.get([128, NT], BF16)
        vtok = cv_.get([128, NTT, 256], BF16)
        ofb = cv_.get([128, NTT, 256], BF16)
        lrT = cv_.get([33, NT], BF16)
        rcs = [cv_.get([128, 2, 512], BF16) for _ in range(2)]
        DP = 6
        spb = [cv_.get([128, 128], F32) for _ in range(DP)]
        epb = [cv_.get([128, 128], F32) for _ in range(DP)]
        emb = [cv_.get([128, 128], F32) for _ in range(DP)]
        qtb = [cv_.get([128, 128], BF16) for _ in range(DP)]
        ktb = [cv_.get([128, 128], BF16) for _ in range(DP)]
        kdb = [cv_.get([128, 128], BF16) for _ in range(DP)]
        khb = [cv_.get([128, 128], BF16) for _ in range(DP)]
        Ab = [cv_.get([128, 256], BF16) for _ in range(DP)]
        Sm = cv_.get([128, 128], F32)
        Sbfs = [cv_.get([128, 128], BF16) for _ in range(2)]
        osb = [cv_.get([128, 256], F32) for _ in range(DP)]
        onb2 = [cv_.get([128, 256], BF16) for _ in range(DP)]
        junkb = cv_.get([128, 128], BF16)
        MSET('pool', lrT[32:33, :], 1.0, ['lrT'])
        for gp in range(2):
            cols = [4096 + gp * 128, 4352 + gp * 128, 5664 + gp * 128, 5920 + gp * 128]
            for which in range(2):
                dst = qrT if which == 0 else krT
                dn = 'qr' if which == 0 else 'kr'
                s1 = wload(w_in[l][:, cols[which]:cols[which] + 128], tag=('C', l) if (which == 0 and gp == 0) else None)
                s2 = wload(w_in[l][:, cols[2 + which]:cols[2 + which] + 128])
                for g in range(NG):
                    b1, r1 = proj(s1, g)
                    if g == 0:
                        CP('act', dst[:, 0:512], b1[:], [r1], ["%s0" % dn, dn + 'all'])
                        continue
                    b2_, r2 = proj(s2, g)
                    rb = g % 2
                    DMA('pool', rcs[rb], rope[:, :, (g - 1) * 512:g * 512].rearrange("a p n -> p a n"), (), ["rcs%d" % rb],
                        "rcs%d" % rb)
                    t1, t1r = ntf()
                    t2, t2r = ntf()
                    TT('dve', t1, b1[:], rcs[rb][:, 0, :], ALU.mult, [r1, "rcs%d" % rb], [t1r])
                    TT('dve', t2, b2_[:], rcs[rb][:, 1, :], ALU.mult, [r2, "rcs%d" % rb], [t2r])
                    TT('dve', dst[:, g * 512:(g + 1) * 512], t1, t2, ALU.add, [t1r, t2r], ["%s%d" % (dn, g), dn + 'all'])
            if gp == 0:
                sl = wload(w_in[l][:, 5632:5664], 8, 32)
                for g in range(NG):
                    bank, br = proj(sl, g, 32)
                    CP('act', lrT[0:32, g * 512:(g + 1) * 512], bank[0:32, :], [br], ['lrT'])
            sv = [wload(w_in[l][:, 4608 + gp * 256 + i * 128:4608 + gp * 256 + (i + 1) * 128]) for i in range(2)]
            for i in range(2):
                for g in range(NG):
                    bank, br = proj(sv[i], g)
                    t, tr = ntb()
                    CP('act', t, bank[:], [br], [tr])
                    pt, ptr = ntbank()
                    for j in range(4):
                        TR(pt[:, j * 128:(j + 1) * 128], t[:, j * 128:(j + 1) * 128], identb, [tr, 'cstb'], [ptr])
                    CP('dve', vtok[:, 4 * g:4 * g + 4, i * 128:(i + 1) * 128], pt[:, 0:512].rearrange("p (a b) -> p a b", a=4),
                       [ptr], ['vtok'])
            MSET('pool', sml[:, 61:62], 0.0, ['qrall', 'krall'] + ["qr%d" % g for g in range(NG)] + ["kr%d" % g for g in range(NG)])

            order = []
            for d in range(2):
                for si, (t0, ln, lat) in enumerate(SEQS):
                    tiles = list(range(t0 // 128, (t0 + ln) // 128))
                    if d == 1:
                        tiles = tiles[::-1]
                    for j, tt in enumerate(tiles):
                        order.append((d, si, lat, tt, j == 0, j == len(tiles) - 1))
            no_ = len(order)

            def S1(ii):
                d, si, lat, tt, first, last = order[ii]
                b = ii % DP
                tok = slice(tt * 128, (tt + 1) * 128)
                Z, Zr = nbank(ii % 2)
                MM(Z[:, 256:384], lrT[0:33, tok], wal[0:33, gp * 256 + d * 128:gp * 256 + (d + 1) * 128], True, True,
                   ['lrT', 'wal'], [Zr])
                sp, spr = spb[b], "sp%d" % b
                ACT(sp, Z[:, 256:384], AF.Exp, [Zr], [spr], scale=-1.0)
                ACT(sp, sp, AF.Ln, [spr], [spr], bias=1.0)

            def S2(ii):
                d, si, lat, tt, first, last = order[ii]
                b = ii % DP
                Z, Zr = nbank(ii % 2)
                sp, spr = spb[b], "sp%d" % b
                MM(Z[:, 0:128], sp, Umat[d], True, True, [spr, 'cstf'], [Zr])
                ACT(epb[b], Z[:, 0:128], AF.Exp, [Zr], ["ep%d" % b])
                ACT(emb[b], Z[:, 0:128], AF.Exp, [Zr], ["em%d" % b], scale=-1.0)

            def S3(ii):
                d, si, lat, tt, first, last = order[ii]
                b = ii % DP
                tok = slice(tt * 128, (tt + 1) * 128)
                ep, epr, em, emr = epb[b], "ep%d" % b, emb[b], "em%d" % b
                qt, qtr, kt, ktr, kd, kdr = qtb[b], "qt%d" % b, ktb[b], "kt%d" % b, kdb[b], "kd%d" % b
                dc = 127 if d == 0 else 0
                STT(qt, qrT[:, tok], 0.125, ep, ALU.mult, ALU.mult, ['qrall', epr], [qtr])
                TT('dve', kt, krT[:, tok], em, ALU.mult, ['krall', emr], [ktr])
                TS('dve', kd, kt, ep[:, dc:dc + 1], None, ALU.mult, None, [ktr, epr], [kdr])

            def S3b(ii):
                d, si, lat, tt, first, last = order[ii]
                b = ii % DP
                qt, qtr, kt, ktr, kd, kdr = qtb[b], "qt%d" % b, ktb[b], "kt%d" % b, kdb[b], "kd%d" % b
                pt, ptr = ntbank(7)
                TR(pt[:, 0:128], kd, identb, [kdr, 'cstb'], [ptr])
                for e in range(2):
                    ba, bar = nbank(2 + e)
                    MM(ba[:, 0:128], kt[e * 64:(e + 1) * 64, :], qt[e * 64:(e + 1) * 64, :], True, True, [ktr, qtr], [bar])

            def S4(ii):
                d, si, lat, tt, first, last = order[ii]
                b = ii % DP
                pt, ptr = ntbank(7)
                CP('act', khb[b], pt[:, 0:128], [ptr], ["kh%d" % b])
                for e in range(2):
                    ba, bar = nbank(2 + e)
                    TT('dve', Ab[b][:, e * 128:(e + 1) * 128], ba[:, 0:128], Mmask[d], ALU.mult, [bar, 'cstb'], ["A%d_%d" % (b, e)])

            def S5(ii):
                d, si, lat, tt, first, last = order[ii]
                b = ii % DP
                bd, bdr = nbank(6)
                MM(bd[:, 0:256], khb[b], vtok[:, tt, :], True, True, ["kh%d" % b, 'vtok'], [bdr])

            def SB(ii):
                d, si, lat, tt, first, last = order[ii]
                b = ii % DP
                ep, epr = epb[b], "ep%d" % b
                qt, qtr, kh, khr, A, Ar = qtb[b], "qt%d" % b, khb[b], "kh%d" % b, Ab[b], "A%d" % b
                if first:
                    if lat:
                        src = (sf if d == 0 else sb)[l, 2 * gp:2 * gp + 2].rearrange("e k v -> (e k) v")
                        DMA('sp', Sm, src, (), ['Sm0', 'Sm1'], 'Sm')
                    else:
                        MSET('pool', Sm, 0.0, ['Sm0', 'Sm1'])
                    CP('act', Sbfs[(ii + 1) % 2], Sm, ['Sm0', 'Sm1'], ["Sbf%d" % ((ii + 1) % 2)])
                Sold, Soldr = Sbfs[(ii + 1) % 2], "Sbf%d" % ((ii + 1) % 2)
                Snew, Snewr = Sbfs[ii % 2], "Sbf%d" % (ii % 2)
                bd, bdr = nbank(6)
                bos = [nbank(4 + e) for e in range(2)]
                for e in range(2):
                    MM(bos[e][0][:, 0:128], A[:, e * 128:(e + 1) * 128], vtok[:, tt, e * 128:(e + 1) * 128], True, False,
                       ["%s_%d" % (Ar, e), 'vtok'], [bos[e][1]])
                for e in range(2):
                    MM(bos[e][0][:, 0:128], qt[e * 64:(e + 1) * 64, :], Sold[e * 64:(e + 1) * 64, :], False, True,
                       [qtr, Soldr], [bos[e][1]])
                dc = 127 if d == 0 else 0
                for e in range(2):
                    rows = slice(e * 64, (e + 1) * 64)
                    STT(Sm[rows, :], Sm[rows, :], ep[rows, dc:dc + 1], bd[rows, e * 128:(e + 1) * 128], ALU.mult, ALU.add,
                        ['Sm%d' % e, epr, bdr], ['Sm%d' % e])
                CP('act', Snew, Sm, ['Sm0', 'Sm1'], [Snewr])
                if last and not lat:
                    dst = (nsf if d == 0 else nsb)[si, l, 2 * gp:2 * gp + 2].rearrange("e k v -> (e k) v")
                    DMA('sp', dst, Sm, ['Sm0', 'Sm1'], ['nso'], 'Smo')

            def T1(ii):
                d, si, lat, tt, first, last = order[ii]
                b = ii % DP
                for e in range(2):
                    bo, bor = nbank(4 + e)
                    if d == 0:
                        CP('act', ofb[:, tt, e * 128:(e + 1) * 128], bo[:, 0:128], [bor], ["of%d" % tt])
                    else:
                        TT('dve', osb[b][:, e * 128:(e + 1) * 128], bo[:, 0:128], ofb[:, tt, e * 128:(e + 1) * 128], ALU.add,
                           [bor, "of%d" % tt], ["os%d_%d" % (b, e)])

            def T2(ii):
                d, si, lat, tt, first, last = order[ii]
                if d == 0:
                    return
                b = ii % DP
                c0 = 8 + 6 * (ii % 4)
                for e in range(2):
                    ACT(junkb, osb[b][:, e * 128:(e + 1) * 128], AF.Square, ["os%d_%d" % (b, e)], ['junkb', "gs%d" % (ii % 4)],
                        accum=sml[:, c0 + e:c0 + e + 1])
                ACT(sml[:, c0 + 2:c0 + 4], sml[:, c0:c0 + 2], AF.Ln, ["gs%d" % (ii % 4)], ["gl%d" % (ii % 4)], bias=EPS, scale=1.0 / 128)
                ACT(sml[:, c0 + 4:c0 + 6], sml[:, c0 + 2:c0 + 4], AF.Exp, ["gl%d" % (ii % 4)], ["gr%d" % (ii % 4)], scale=-0.5)

            def T3(ii):
                d, si, lat, tt, first, last = order[ii]
                if d == 0:
                    return
                b = ii % DP
                c0 = 8 + 6 * (ii % 4)
                on, onr = onb2[b], "on2%d" % b
                for e in range(2):
                    TS('dve', on[:, e * 128:(e + 1) * 128], osb[b][:, e * 128:(e + 1) * 128], sml[:, c0 + 4 + e:c0 + 5 + e],
                       None, ALU.mult, None, ["os%d_%d" % (b, e), "gr%d" % (ii % 4)], ["%s_%d" % (onr, e)])
                pt, ptr = ntbank(7)
                for e in range(2):
                    TR(pt[:, 256 + e * 128:256 + (e + 1) * 128], on[:, e * 128:(e + 1) * 128], identb, ["%s_%d" % (onr, e), 'cstb'], [ptr])

            def T4(ii):
                d, si, lat, tt, first, last = order[ii]
                if d == 0:
                    return
                tok = slice(tt * 128, (tt + 1) * 128)
                pt, ptr = ntbank(7)
                for e in range(2):
                    ACT(yT[:, 8 + 2 * gp + e, tok], pt[:, 256 + e * 128:256 + (e + 1) * 128], AF.Copy,
                        [ptr, 'pv'], ["y%d_%d" % (8 + 2 * gp + e, tt // 4)], scale=pv[:, 70:71])

            stages = [(T4, -4), (T3, -3), (T2, -2), (T1, -1), (S2, 5), (S1, 6), (SB, 0), (S5, 1), (S4, 2), (S3b, 3), (S3, 4)]
            for j in range(-6, no_ + 4):
                for fn_, off in stages:
                    ii = j + off
                    if 0 <= ii < no_:
                        fn_(ii)
            for e in range(2):
                gate_apply(5120 + (2 * gp + e) * 128, 8 + 2 * gp + e)
        wpre(('D', l), w_br[l, 0][:, 0:128], 4)
        barrier()
        if stop == (l, 'C'):
            raise _Stop()

        cv_ = Carve()
        mrg = cv_.get([128, 8, NT], BF16)
        macc = cv_.get([128, NT], F32)
        og = cv_.get([128, 8, 512], F32)
        xt = [cv_.get([128, 1024], F32) for _ in range(2)]
        if l + 1 < NL:
            mod_begin(l + 1)
            pmn, pmnr = nbank(7)
        for fc in range(8):
            for b in range(3):
                if l + 1 < NL:
                    mod_chunk(l + 1, fc * 3 + b, pmn, pmnr)
                swb = wload(w_br[l, b][:, fc * 128:(fc + 1) * 128], 4, tag=('D', l) if (fc == 0 and b == 0) else None)
                swg = wload(w_gate[l][:, b * 1024 + fc * 128:b * 1024 + (fc + 1) * 128])
                for g in range(NG):
                    zb, zr = nbank()
                    for kc in range(4):
                        MM(zb[:], wsl[:, swb, kc, :], yT[:, b * 4 + kc, g * 512:(g + 1) * 512], kc == 0, kc == 3,
                           ["ws%d" % swb, "y%d_%d" % (b * 4 + kc, g)], [zr])
                    gb, gr = proj(swg, g)
                    sg, sgr = ntf()
                    ACT(sg, gb[:], AF.Sigmoid, [gr, 'pv'], [sgr], bias=pv[:, 24 + b * 8 + fc:25 + b * 8 + fc])
                    mv = macc[:, g * 512:(g + 1) * 512]
                    mr = "macc%d" % g
                    if b == 0:
                        TT('dve', mv, zb[:], sg, ALU.mult, [zr, sgr], [mr])
                    else:
                        TT('dve', sg, zb[:], sg, ALU.mult, [zr, sgr], [sgr])
                        if b == 1:
                            TT('dve', mv, mv, sg, ALU.add, [mr, sgr], [mr])
                        else:
                            TT('dve', mrg[:, fc, g * 512:(g + 1) * 512], mv, sg, ALU.add, [mr, sgr], ["mrg%d" % g])
        if l + 1 < NL:
            mod_finish(l + 1, pmn, pmnr)
        hres = ["hT%d" % t for t in range(8)] + ['wres']
        DMA('pool', hT[:, :, 0:1024], w_out[l].rearrange("(kc p) n -> p kc n", p=128), (), hres, 'wres')
        for g in range(NG):
            s = 0 if g == 0 else 1
            for fc in range(8):
                bank, br = nbank()
                for kc in range(8):
                    MM(bank[:], hT[:, kc, fc * 128:(fc + 1) * 128], mrg[:, kc, g * 512:(g + 1) * 512], kc == 0, kc == 7,
                       hres + ["mrg%d" % g], [br])
                TS('dve', og[:, fc, :], bank[:], modT[:, 16 + fc, s:s + 1], None, ALU.mult, None, [br, modTr], ['og'])
            for j in range(4):
                tt = 4 * g + j
                b = tt % 2
                DMA('sp', xt[b], xcur[tt * 128:(tt + 1) * 128, :], ["%s%d" % (xcn, tt)], ["xo%d" % b], "xo%d" % b)
                for hh in range(2):
                    bank, br = nbank()
                    for c in range(4):
                        fc = hh * 4 + c
                        TR(bank[:, c * 128:(c + 1) * 128], og[:, fc, j * 128:(j + 1) * 128], identf, ['og', 'cstf'], [br])
                    TT('dve', xt[b][:, hh * 512:(hh + 1) * 512], bank[:], xt[b][:, hh * 512:(hh + 1) * 512], ALU.add,
                       [br, "xo%d" % b], ["xo%d" % b])
                DMA('sp', xnxt[tt * 128:(tt + 1) * 128, :], xt[b], ["xo%d" % b], ["%s%d" % (xnn, tt)], "xo%d" % b)

    try:
        for l_ in range(NL):
            layer(l_)
            if stop == (l_, 'D'):
                raise _Stop()
    except _Stop:
        barrier()
        dbg_h = nc.dram_tensor("dbg_h", [128, 8 * NT], BF16, kind="ExternalOutput").ap()
        dbg_y = nc.dram_tensor("dbg_y", [128, 12 * NT], BF16, kind="ExternalOutput").ap()
        DMA('sp', dbg_h, hT[:].rearrange("p a b -> p (a b)"), ['bar'], ['dbgh'], 'dbgh')
        DMA('sp', dbg_y, yT[:].rearrange("p a b -> p (a b)"), ['bar'], ['dbgy'], 'dbgy')
    S.emit(nc)
    es.close()
    return nc


def _consts():
    s = np.arange(128)[:, None]
    t = np.arange(128)[None, :]
    c = -1.0 / 16.0
    identf = np.eye(128, dtype=np.float32)
    Uf = np.where(s <= t, c, 0.0).astype(np.float32)
    Ub = np.where(s >= t, c, 0.0).astype(np.float32)
    SU = np.where(s > t, c, 0.0).astype(np.float32)
    SL = np.where(s < t, c, 0.0).astype(np.float32)
    Mf = (s <= t).astype(np.float32)
    Mb = (s >= t).astype(np.float32)
    bd = ((s // 64) == (t // 64)).astype(np.float32) / 64.0
    return np.concatenate([identf, Uf, Ub, SU, SL, Mf, Mb, bd], axis=1).astype(np.float32)


def _rope_tables():
    pos = np.arange(2048)
    n_f = 16
    inv = 10000.0 ** (-np.arange(n_f) / n_f)
    ang_r = (pos // 64)[:, None] * inv
    ang_c = (pos % 64)[:, None] * inv
    cos = np.zeros((64, 2048), np.float32)
    sin = np.zeros((64, 2048), np.float32)
    for i in range(64):
        ang = ang_r if i < 32 else ang_c
        j = i % 16
        a_part = (i % 32) < 16
        cos[i] = np.cos(ang[:, j].astype(np.float32))
        sn = np.sin(ang[:, j].astype(np.float32))
        sin[i] = -sn if a_part else sn
    return np.stack([np.tile(cos, (2, 1)), np.tile(sin, (2, 1))]).astype(np.float32)


def _swap_perm():
    i = np.arange(64)
    return np.where((i % 32) < 16, i + 16, i - 16)


def _b2_table(rpb, interior=False):
    L = rpb.shape[0]
    pad = np.concatenate([rpb.reshape(L, 8, -1), np.full((L, 8, 1), NEG, np.float32)], axis=2)
    p = np.arange(128)
    ph = (p // 64)[:, None, None]
    kc = (p % 64)[:, None, None]
    u = np.arange(16)[None, :, None]
    qc = np.arange(64)[None, None, :]
    if interior:
        dr = 7 - u + ph
        lo, hi = -4, 3
    else:
        dr = u - 8 + ph
        lo, hi = -7, 7
    csq = np.clip(qc - 8, 0, 48)
    valid = (dr >= lo) & (dr <= hi) & (kc >= csq) & (kc < csq + 16)
    dc = np.clip(kc - qc + 15, 0, 30)
    idx = np.where(valid, (np.clip(dr, -7, 7) + 7) * 31 + dc, 15 * 31)
    tab = pad[:, :, idx]
    return np.ascontiguousarray(tab.transpose(0, 2, 1, 3, 4)).reshape(L, 128, 8 * 1024).astype(np.float32)


_NC_CACHE = {}


def kernel(x_prompt, x_sample, c, cache_k, cache_v, state_fwd, state_bwd, c_ctx,
           norm_w, w_ada, b_ada, w_in, conv_w, q_norm_w, k_norm_w, rpb,
           w_alpha, b_alpha, gla_norm_w, w_branch, w_gate, b_gate, w_out, _ret_maps=False):
    f = lambda a: np.ascontiguousarray(np.asarray(a, dtype=np.float32))
    x_prompt, x_sample, c, cache_k, cache_v = map(f, (x_prompt, x_sample, c, cache_k, cache_v))
    state_fwd, state_bwd, c_ctx, norm_w, w_ada, b_ada, w_in = map(f, (state_fwd, state_bwd, c_ctx, norm_w, w_ada, b_ada, w_in))
    conv_w, q_norm_w, k_norm_w, rpb, w_alpha, b_alpha = map(f, (conv_w, q_norm_w, k_norm_w, rpb, w_alpha, b_alpha))
    gla_norm_w, w_branch, w_gate, b_gate, w_out = map(f, (gla_norm_w, w_branch, w_gate, b_gate, w_out))

    perm = _swap_perm()
    qcols = np.concatenate([4096 + h * 64 + perm for h in range(4)])
    kcols = np.concatenate([4352 + h * 64 + perm for h in range(4)])
    w_in_ext = np.ascontiguousarray(np.concatenate([w_in, w_in[:, :, qcols], w_in[:, :, kcols]], axis=2))
    pvec = np.zeros((NL, 128, 72), np.float32)
    pvec[:, :, 0:24] = b_ada.reshape(NL, 24, 128).transpose(0, 2, 1)
    pvec[:, :, 24:48] = b_gate.reshape(NL, 24, 128).transpose(0, 2, 1)
    pvec[:, :, 48:56] = norm_w.reshape(NL, 8, 128).transpose(0, 2, 1)
    pvec[:, :, 56:68] = conv_w.reshape(NL, 3, 4, 128).transpose(0, 3, 2, 1).reshape(NL, 128, 12)
    pvec[:, :, 68] = np.tile(q_norm_w, (1, 2))
    pvec[:, :, 69] = np.tile(k_norm_w, (1, 2))
    pvec[:, :, 70] = gla_norm_w
    walpha = np.zeros((NL, 33, 2, 2, 128), np.float32)
    for d in range(2):
        wa = w_alpha[:, d].reshape(NL, 16, 2, 128)
        walpha[:, d * 16:(d + 1) * 16, :, d, :] = wa
        walpha[:, 32, :, d, :] = b_alpha[:, d].reshape(NL, 2, 128)
    walpha = walpha.reshape(NL, 33, 512)
    b2tab = _b2_table(rpb)
    tfull = _b2_table(rpb, interior=True).reshape(NL, 128, 8, 16, 64)
    b2tabi = np.ascontiguousarray(np.stack(
        [np.concatenate([tfull[:, :, :, 11 - 2 * t, :], tfull[:, :, :, 12 - 2 * t, :]], axis=-1) for t in range(5)],
        axis=3)).reshape(NL, 128, 8 * 640)
    cst = _consts()
    rope = _rope_tables()

    in_maps = []
    for i in range(8):
        xin = np.ascontiguousarray(np.concatenate([x_prompt[2 * i], x_prompt[2 * i + 1], x_sample[i]], axis=0))
        cond = np.zeros((128, 8, 2), np.float32)
        cond[:, :, 0] = c_ctx.reshape(8, 128).T
        cond[:, :, 1] = c[i].reshape(8, 128).T
        in_maps.append(dict(
            xin=xin, w_in=w_in_ext, w_ada=w_ada, w_gate=w_gate, w_br=w_branch, w_out=w_out,
            cond=np.ascontiguousarray(cond.reshape(128, 16)), pvec=pvec, walpha=walpha, b2tab=b2tab, b2tabi=b2tabi,
            ck=np.ascontiguousarray(cache_k[i]), cv=np.ascontiguousarray(cache_v[i]),
            sf=np.ascontiguousarray(state_fwd[i]), sb=np.ascontiguousarray(state_bwd[i]),
            cst=cst, rope=rope))
    if _ret_maps:
        return in_maps
    if 'nc' not in _NC_CACHE:
        _NC_CACHE['nc'] = build_nc()
    nc = _NC_CACHE['nc']
    res = run_bass_kernel_spmd(nc, in_maps, core_ids=list(range(8)))
    rs = res.results
    y_p = np.zeros((16, 256, D), np.float32)
    y_s = np.zeros((8, 2048, D), np.float32)
    nk = np.zeros((16, NL, 8, 256, 64), np.float32)
    nv = np.zeros((16, NL, 8, 256, 64), np.float32)
    nsf = np.zeros((16, NL, 4, 64, 128), np.float32)
    nsb = np.zeros((16, NL, 4, 64, 128), np.float32)
    for i in range(8):
        y = rs[i]["y"]
        y_p[2 * i] = y[0:256]
        y_p[2 * i + 1] = y[256:512]
        y_s[i] = y[512:]
        nk[2 * i:2 * i + 2] = rs[i]["nk"]
        nv[2 * i:2 * i + 2] = rs[i]["nv"]
        nsf[2 * i:2 * i + 2] = rs[i]["nsf"]
        nsb[2 * i:2 * i + 2] = rs[i]["nsb"]
    return (y_p, y_s, nk, nv, nsf, nsb)
```

```python
import numpy as np
from contextlib import ExitStack
import concourse.bass as bass
import concourse.mybir as mybir
from concourse.bass_utils import run_bass_kernel_spmd

F32 = mybir.dt.float32
BF16 = mybir.dt.bfloat16
AF = mybir.ActivationFunctionType
ALU = mybir.AluOpType

NL = 4
D = 1024
NT = 2560
NTT = 20
NG = 5
WEXT = 6176
EPS = 1e-6
NEG = -1e30
ENG = ['pe', 'act', 'dve', 'pool', 'sp']


class Sch:
    def __init__(self):
        self.ops = {e: [] for e in ENG}
        self.lastw = {}
        self.readers = {}
        self.dma_count = {}
        self.regions = set()
        self.barrier_op = None

    def add(self, eng, fn, R=(), W=(), dma_key=None):
        op = dict(eng=eng, fn=fn, deps=[], signal=False, dma_key=dma_key, dma_ord=0)
        if dma_key is not None:
            self.dma_count[dma_key] = self.dma_count.get(dma_key, 0) + 1
            op['dma_ord'] = self.dma_count[dma_key]
        deps = {}

        def dep(o, raw):
            if o is None:
                return
            if o['dma_key'] is None and o['eng'] == eng and dma_key is None:
                if eng == 'pe':
                    return
            deps[id(o)] = o

        for r in R:
            self.regions.add(r)
            dep(self.lastw.get(r, self.barrier_op), True)
            if r.startswith('pb') or r.startswith('pt'):
                for o in self.readers.get(r, {}).values():
                    if o['eng'] != eng:
                        dep(o, True)
        for w in W:
            self.regions.add(w)
            dep(self.lastw.get(w, self.barrier_op), False)
            for o in self.readers.get(w, {}).values():
                dep(o, False)
        for r in R:
            self.readers.setdefault(r, {})[(eng, dma_key)] = op
        for w in W:
            self.lastw[w] = op
            self.readers[w] = {}
        op['deps'] = list(deps.values())
        for o in op['deps']:
            if o['dma_key'] is None:
                o['signal'] = True
        self.ops[eng].append(op)
        return op

    def emit(self, nc):
        for e in ENG:
            c = 0
            for op in self.ops[e]:
                if op['signal']:
                    c += 1
                op['ord'] = c
        with ExitStack() as es:
            sems = {e: es.enter_context(nc.semaphore('s_' + e)) for e in ENG}
            dsems = {k: es.enter_context(nc.semaphore('d%d' % i)) for i, k in enumerate(self.dma_count)}
            block = es.enter_context(nc.Block())

            def run(e, h):
                seen = {}
                for op in self.ops[e]:
                    need = {}
                    for o in op['deps']:
                        if o['dma_key'] is not None:
                            key = ('d', o['dma_key'])
                            val = 16 * o['dma_ord']
                        else:
                            key = ('e', o['eng'])
                            val = o['ord']
                        need[key] = max(need.get(key, 0), val)
                    for key, val in need.items():
                        if seen.get(key, 0) >= val:
                            continue
                        seen[key] = val
                        h.wait_ge(dsems[key[1]] if key[0] == 'd' else sems[key[1]], val)
                    ins = op['fn'](h)
                    if op['dma_key'] is not None:
                        ins.then_inc(dsems[op['dma_key']], 16)
                    elif op['signal']:
                        ins.then_inc(sems[e], 1)
                if e == 'sp':
                    for k, c in self.dma_count.items():
                        h.wait_ge(dsems[k], 16 * c)

            @block.tensor
            def _(h):
                run('pe', h)

            @block.scalar
            def _(h):
                run('act', h)

            @block.vector
            def _(h):
                run('dve', h)

            @block.gpsimd
            def _(h):
                run('pool', h)

            @block.sync
            def _(h):
                run('sp', h)


class _Stop(Exception):
    pass


def build_nc(stop=None):
    nc = bass.Bass("TRN2", target_bir_lowering=False)
    S = Sch()

    def din(name, shape):
        return nc.dram_tensor(name, list(shape), F32, kind="ExternalInput").ap()

    def dout(name, shape):
        return nc.dram_tensor(name, list(shape), F32, kind="ExternalOutput").ap()

    xin = din("xin", [NT, D])
    w_in = din("w_in", [NL, D, WEXT])
    w_ada = din("w_ada", [NL, D, 3 * D])
    w_gate = din("w_gate", [NL, D, 3 * D])
    w_br = din("w_br", [NL, 3, 512, D])
    w_out = din("w_out", [NL, D, D])
    cond = din("cond", [128, 16])
    pvec = din("pvec", [NL, 128, 72])
    walpha = din("walpha", [NL, 33, 512])
    b2tab = din("b2tab", [NL, 128, 8 * 1024])
    b2tabi = din("b2tabi", [NL, 128, 8 * 640])
    ck = din("ck", [NL, 8, 512, 64])
    cv = din("cv", [NL, 8, 512, 64])
    sf = din("sf", [NL, 4, 64, 128])
    sb = din("sb", [NL, 4, 64, 128])
    cst = din("cst", [128, 8 * 128])
    rope = din("rope", [2, 128, 2048])
    yout = dout("y", [NT, D])
    nk = dout("nk", [2, NL, 8, 256, 64])
    nv = dout("nv", [2, NL, 8, 256, 64])
    nsf = dout("nsf", [2, NL, 4, 64, 128])
    nsb = dout("nsb", [2, NL, 4, 64, 128])
    xsA = nc.dram_tensor("xsA", [NT, D], F32).ap()
    xsB = nc.dram_tensor("xsB", [NT, D], F32).ap()

    ARENA = 74 * 1024
    es = ExitStack()
    sb_ = lambda n, sh, dt: es.enter_context(nc.sbuf_tensor(n, sh, dt))
    hT = sb_("hT", [128, 8, NT], BF16)
    yT = sb_("yT", [128, 12, NT], BF16)
    NWS = 6
    wsl = sb_("wsl", [128, NWS, 8, 128], BF16)
    arena = sb_("arena", [128, ARENA // 4], F32)
    cstf = sb_("cstf", [128, 5, 128], F32)
    cstb = sb_("cstb", [128, 5, 128], BF16)
    condt = sb_("condt", [128, 16], F32)
    scb = sb_("scb", [128, 16], BF16)
    pv = sb_("pv", [128, 72], F32)
    modTs = [sb_("modT%d" % i, [128, 24, 2], F32) for i in range(2)]
    Amods = [sb_("Amod%d" % i, [128, 8, 2], F32) for i in range(2)]
    pvn = sb_("pvn", [128, 32], F32)
    wq8 = sb_("wq8", [128, 1], F32)
    wal = sb_("wal", [33, 512], BF16)
    tf = sb_("tf", [128, 4, 512], F32)
    tb = sb_("tb", [128, 4, 512], BF16)
    sml = sb_("sml", [128, 64], F32)
    NPB = 8
    pbs = [es.enter_context(nc.psum_tensor("pb%d" % i, [128, 512], F32)) for i in range(NPB)]

    identf, Uf, Ub, SU, SL = [cstf[:, i, :] for i in range(5)]
    identb, Mf, Mb, bd64, ones1k = [cstb[:, i, :] for i in range(5)]
    Umat = [Uf, Ub]
    SUL = [SU, SL]
    Mmask = [Mf, Mb]

    cnt = dict(pb=0, pt=0, ws=0, tf=0, tb=0, P=0)

    def nbank(idx=None):
        if idx is None:
            idx = cnt['pb'] % 6
            cnt['pb'] += 1
        return pbs[idx][:], "pb%d" % idx

    def ntbank(idx=None):
        if idx is None:
            idx = 6 + cnt['pt'] % 2
            cnt['pt'] += 1
        return pbs[idx][:].bitcast(BF16), "pb%d" % idx

    def ntf():
        i = cnt['tf'] % 4
        cnt['tf'] += 1
        return tf[:, i, :], "tf%d" % i

    def ntb():
        i = cnt['tb'] % 4
        cnt['tb'] += 1
        return tb[:, i, :], "tb%d" % i

    def MM(out, lhsT, rhs, start, stop, R, W):
        S.add('pe', lambda h: h.matmul(out, lhsT, rhs, start=start, stop=stop), R, W)

    def TR(out, in_, ident, R, W):
        S.add('pe', lambda h: h.transpose(out, in_, ident), R, W)

    def ACT(out, in_, func, R, W, bias=None, scale=None, accum=None):
        kw = {}
        if bias is not None:
            kw['bias'] = bias
        if scale is not None:
            kw['scale'] = scale
        if accum is not None:
            kw['accum_out'] = accum
        S.add('act', lambda h: h.activation(out, in_, func, **kw), R, W)

    def engname(e):
        return e

    def TT(e, out, in0, in1, op, R, W):
        S.add(e, lambda h: h.tensor_tensor(out, in0, in1, op), R, W)

    def TS(e, out, in0, s1, s2, op0, op1, R, W):
        if s2 is None:
            S.add(e, lambda h: h.tensor_scalar(out, in0, s1, None, op0), R, W)
        else:
            S.add(e, lambda h: h.tensor_scalar(out, in0, s1, s2, op0, op1), R, W)

    def STT(out, in0, sc, in1, op0, op1, R, W):
        S.add('dve', lambda h: h.scalar_tensor_tensor(out, in0, sc, in1, op0, op1), R, W)

    def CP(e, out, in_, R, W):
        if e == 'act':
            S.add('act', lambda h: h.copy(out, in_), R, W)
        else:
            S.add(e, lambda h: h.tensor_copy(out, in_), R, W)

    def MSET(e, ap, val, W):
        S.add(e, lambda h: h.memset(ap, val), (), W)

    def RCP(out, in_, R, W):
        S.add('dve', lambda h: h.reciprocal(out, in_), R, W)

    def DMA(q, out, in_, R, W, key):
        S.add(q, lambda h: h.dma_start(out=out, in_=in_), R, W, dma_key=key)

    def barrier():
        regs = sorted(S.regions)
        MSET('pool', sml[:, 63:64], 0.0, regs + ['bar'])
        S.barrier_op = S.ops['pool'][-1]

    pre = {}

    def wpre(tag, src, nk_=8, ncols=128):
        pre[tag] = wload(src, nk_, ncols)

    def wload(src, nk_=8, ncols=128, tag=None):
        if tag is not None and tag in pre:
            return pre.pop(tag)
        i = cnt['ws'] % NWS
        cnt['ws'] += 1
        DMA('pool', wsl[:, i, 0:nk_, 0:ncols], src.rearrange("(kc p) n -> p kc n", p=128), (), ["ws%d" % i], "ws%d" % i)
        return i

    def hreg(g):
        return ["hT%d" % t for t in range(4 * g, 4 * g + 4)]

    def proj(slot, g, ncols=128):
        bank, br = nbank()
        for kc in range(8):
            MM(bank[0:ncols, :], wsl[:, slot, kc, 0:ncols], hT[:, kc, g * 512:(g + 1) * 512], kc == 0, kc == 7,
               ["ws%d" % slot] + hreg(g), [br])
        return bank, br

    class Carve:
        def __init__(self):
            self.off = 0

        def get(self, shape, dt):
            n = int(np.prod(shape[1:])) * (4 if dt == F32 else 2)
            n = (n + 31) // 32 * 32
            a = self.off // 4
            self.off += n
            assert self.off <= ARENA, ("arena overflow", self.off)
            v = arena[0:shape[0], a:a + n // 4]
            if dt == BF16:
                v = v.bitcast(BF16)
            ne = int(np.prod(shape[1:]))
            v = v[:, 0:ne]
            if len(shape) == 3:
                v = v.rearrange("p (a b) -> p a b", a=shape[1])
            elif len(shape) == 4:
                v = v.rearrange("p (a b c) -> p a b c", a=shape[1], b=shape[2])
            return v

    DMA('sp', cstf[:], cst[:, 0:640].rearrange("p (a b) -> p a b", a=5), (), ['cstf'], 'cstf')
    DMA('pool', cstb[:, 0, :], cst[:, 0:128], (), ['cstb'], 'cstb')
    DMA('pool', cstb[:, 1:4, :], cst[:, 640:1024].rearrange("p (a b) -> p a b", a=3), (), ['cstb'], 'cstb')
    MSET('pool', cstb[:, 4, :], 1.0 / 1024.0, ['cstb'])
    DMA('sp', condt[:], cond, (), ['condt'], 'condt')
    ACT(scb[:], condt[:], AF.Silu, ['condt'], ['scb'])

    xbufs = [(xin, 'xin'), (xsA, 'xsA'), (xsB, 'xsB'), (xsA, 'xsA'), (yout, 'yo')]
    if stop is not None:
        MSET('pool', yT[:].rearrange("p a b -> p (a b)"), 0.0, ['yTinit'])
    SEQS = [(0, 256, 0), (256, 256, 0), (512, 2048, 1)]

    def ucol(tok):
        return tok + 1 + 2 * (0 if tok < 256 else (1 if tok < 512 else 2))

    def mod_begin(lm):
        DMA('sp', pvn[:, 0:24], pvec[lm][:, 0:24], (), ['pvn'], 'pvn')
        DMA('sp', pvn[:, 24:32], pvec[lm][:, 48:56], (), ['pvn'], 'pvn')

    def mod_chunk(lm, j, pm, pmr):
        sl = wload(w_ada[lm][:, j * 128:(j + 1) * 128])
        for kc in range(8):
            MM(pm[:, 2 * j:2 * j + 2], wsl[:, sl, kc, :], scb[:, 2 * kc:2 * kc + 2], kc == 0, kc == 7,
               ["ws%d" % sl, 'scb'], [pmr])

    def mod_finish(lm, pm, pmr):
        mT, mr = modTs[lm % 2], "modT%d" % (lm % 2)
        aT, ar = Amods[lm % 2], "Amod%d" % (lm % 2)
        TT('dve', mT[:], pm[:, 0:48].rearrange("p (a b) -> p a b", b=2),
           pvn[:, 0:24].unsqueeze(2).broadcast_to([128, 24, 2]), ALU.add, [pmr, 'pvn'], [mr])
        TS('dve', aT[:], mT[:, 8:16, :], 1.0, None, ALU.add, None, [mr], [ar])
        TT('dve', aT[:], aT[:], pvn[:, 24:32].unsqueeze(2).broadcast_to([128, 8, 2]), ALU.mult, [ar, 'pvn'], [ar])

    mod_begin(0)
    pm0, pm0r = nbank(7)
    for j0 in range(24):
        mod_chunk(0, j0, pm0, pm0r)
    mod_finish(0, pm0, pm0r)

    def layer(l):
        modT, modTr = modTs[l % 2], "modT%d" % (l % 2)
        Amod, Amodr = Amods[l % 2], "Amod%d" % (l % 2)
        xcur, xcn = xbufs[l]
        xnxt, xnn = xbufs[l + 1]
        barrier()
        DMA('sp', pv[:], pvec[l], (), ['pv'], 'pv')
        DMA('pool', wal[:], walpha[l], (), ['wal'], 'wal')
        TS('dve', wq8[:], pv[:, 68:69], 0.125, None, ALU.mult, None, ['pv'], ['wq8'])

        cv_ = Carve()
        xt = [cv_.get([128, 1024], F32) for _ in range(4)]
        xn = [cv_.get([128, 1024], BF16) for _ in range(3)]
        sqj = cv_.get([128, 1024], BF16)

        def X1(tt):
            b = tt % 4
            DMA('sp', xt[b], xcur[tt * 128:(tt + 1) * 128, :], ["%s%d" % (xcn, tt)], ["xt%d" % b], "xt%d" % b)

        def X2(tt):
            b = tt % 4
            ACT(sqj, xt[b], AF.Square, ["xt%d" % b], ['sqj', "ss%d" % b], accum=sml[:, b:b + 1])
            ACT(sml[:, 4 + b:5 + b], sml[:, b:b + 1], AF.Ln, ["ss%d" % b], ["ln%d" % b], bias=EPS, scale=1.0 / D)
            ACT(sml[:, 8 + b:9 + b], sml[:, 4 + b:5 + b], AF.Exp, ["ln%d" % b], ["rs%d" % b], scale=-0.5)

        def X3(tt):
            b = tt % 4
            n3 = tt % 3
            TS('dve', xn[n3], xt[b], sml[:, 8 + b:9 + b], None, ALU.mult, None, ["xt%d" % b, "rs%d" % b], ["xn%d" % n3])
            pt, ptr = ntbank(6 + tt % 2)
            for c in range(8):
                TR(pt[:, c * 128:(c + 1) * 128], xn[n3][:, c * 128:(c + 1) * 128], identb, ["xn%d" % n3, 'cstb'], [ptr])

        def X4(tt):
            s = 0 if tt < 4 else 1
            pt, ptr = ntbank(6 + tt % 2)
            for c in range(8):
                o = hT[:, c, tt * 128:(tt + 1) * 128]
                i = pt[:, c * 128:(c + 1) * 128]
                if tt % 2 == 0:
                    ACT(o, i, AF.Identity, [ptr, Amodr, modTr], ["hT%d" % tt], bias=modT[:, c, s:s + 1],
                        scale=Amod[:, c, s:s + 1])
                else:
                    TS('dve', o, i, Amod[:, c, s:s + 1], modT[:, c, s:s + 1], ALU.mult, ALU.add,
                       [ptr, Amodr, modTr], ["hT%d" % tt])

        for j in range(-2, NTT + 1):
            for fn_, off in ((X4, -1), (X3, 0), (X2, 1), (X1, 2)):
                if 0 <= j + off < NTT:
                    fn_(j + off)
        wpre(('A', l), w_in[l][:, 0:128])
        barrier()
        if stop == (l, 'X'):
            raise _Stop()

        def gate_apply(col0, ych):
            sl = wload(w_in[l][:, col0:col0 + 128])
            for g in range(NG):
                bank, br = proj(sl, g)
                t, tr = ntb()
                ACT(t, bank[:], AF.Silu, [br], [tr])
                yv = yT[:, ych, g * 512:(g + 1) * 512]
                TT('dve', yv, yv, t, ALU.mult, [tr, "y%d_%d" % (ych, g)], ["y%d_%d" % (ych, g)])

        cv_ = Carve()
        ubuf = cv_.get([128, NT + 6], F32)
        MSET('pool', ubuf, 0.0, ["u%d" % g for g in range(NG)])
        for cc in range(4):
            s_xa = wload(w_in[l][:, cc * 128:(cc + 1) * 128], tag=('A', l) if cc == 0 else None)
            s_ca = wload(w_in[l][:, 1024 + cc * 128:1024 + (cc + 1) * 128])
            for g in range(NG):
                b1, r1 = proj(s_xa, g)
                b2, r2 = proj(s_ca, g)
                t, tr = ntf()
                CP('act', t, b1[:], [r1], [tr])
                if g == 0:
                    for hh in range(2):
                        TT('dve', ubuf[:, ucol(hh * 256):ucol(hh * 256) + 256], b2[:, hh * 256:(hh + 1) * 256],
                           t[:, hh * 256:(hh + 1) * 256], ALU.mult, [r2, tr], ["u0"])
                else:
                    TT('dve', ubuf[:, ucol(g * 512):ucol(g * 512) + 512], b2[:], t, ALU.mult, [r2, tr], ["u%d" % g])
            s_ba = wload(w_in[l][:, 512 + cc * 128:512 + (cc + 1) * 128])
            for g in range(NG):
                b3, r3 = proj(s_ba, g)
                t, tr = ntf()
                segs = [(0, 256), (256, 256)] if g == 0 else [(g * 512, 512)]
                ur = ["u%d" % gg for gg in range(max(0, g - 1), min(NG, g + 2))]
                for (t0, n) in segs:
                    u0 = ucol(t0)
                    o = t[:, t0 - g * 512:t0 - g * 512 + n]
                    TS('dve', o, ubuf[:, u0 - 1:u0 - 1 + n], pv[:, 56 + cc * 3:57 + cc * 3], None, ALU.mult, None,
                       ur + ['pv'], [tr])
                    STT(o, ubuf[:, u0:u0 + n], pv[:, 57 + cc * 3:58 + cc * 3], o, ALU.mult, ALU.add, ur + ['pv', tr], [tr])
                    STT(o, ubuf[:, u0 + 1:u0 + 1 + n], pv[:, 58 + cc * 3:59 + cc * 3], o, ALU.mult, ALU.add,
                        ur + ['pv', tr], [tr])
                TT('dve', yT[:, cc, g * 512:(g + 1) * 512], b3[:], t, ALU.mult, [r3, tr], ["y%d_%d" % (cc, g)])
            gate_apply(1536 + cc * 128, cc)
        wpre(('B', l), w_in[l][:, 2048:2048 + 128])
        barrier()
        if stop == (l, 'A'):
            raise _Stop()

        cv_ = Carve()
        qT = cv_.get([128, NT], BF16)
        kT = cv_.get([128, NT], BF16)
        vaug = cv_.get([128, NTT, 2, 65], BF16)
        vodd = cv_.get([128, 15, 2, 65], BF16)
        kcfs = [cv_.get([128, 4, 2, 64], F32) for _ in range(2)]
        kctxTs = [cv_.get([128, 512], BF16) for _ in range(2)]
        vctxs = [cv_.get([128, 4, 2, 65], BF16) for _ in range(2)]
        B2s = [cv_.get([128, 2, 16, 64], BF16) for _ in range(2)]
        B2is = [cv_.get([128, 2, 640], BF16) for _ in range(2)]
        Pbig = [cv_.get([128, 1152], BF16) for _ in range(6)]
        onb = [cv_.get([128, 128], BF16) for _ in range(2)]
        kst = cv_.get([128, 4, 128], F32)
        knb = cv_.get([128, 512], F32)
        vst = cv_.get([128, 4, 128], F32)
        rcb = [cv_.get([128, 2], F32) for _ in range(2)]
        for i2 in range(2):
            MSET('pool', vctxs[i2][:, :, :, 64:65], 1.0, ["vctx%d" % i2])

        def na_prefetch(hp_):
            i2 = hp_ % 2
            for e in range(2):
                DMA('sp', kcfs[i2][:, :, e, :], ck[l, 2 * hp_ + e].rearrange("(kt p) d -> p kt d", p=128), (), ["kcf%d" % i2],
                    'kcf%d_%d' % (i2, e))
                DMA('pool', vctxs[i2][:, :, e, 0:64], cv[l, 2 * hp_ + e].rearrange("(kt p) d -> p kt d", p=128), (),
                    ["vctx%d" % i2], 'vctx%d_%d' % (i2, e))
            DMA('pool', B2s[i2][:], b2tab[l][:, 2 * hp_ * 1024:(2 * hp_ + 2) * 1024].rearrange("p (e u q) -> p e u q", e=2, u=16),
                (), ["B2%d" % i2], 'B2%d' % i2)
            DMA('pool', B2is[i2][:], b2tabi[l][:, 2 * hp_ * 640:(2 * hp_ + 2) * 640].rearrange("p (e q) -> p e q", e=2),
                (), ["B2i%d" % i2], 'B2i%d' % i2)
            v2 = B2s[i2][:].rearrange("p e u q -> p (e u q)")
            ACT(v2, v2, AF.Exp, ["B2%d" % i2], ["B2%d" % i2])
            v2 = B2is[i2][:].rearrange("p e q -> p (e q)")
            ACT(v2, v2, AF.Exp, ["B2i%d" % i2], ["B2i%d" % i2])

        na_prefetch(0)
        for hp in range(4):
            i2 = hp % 2
            kcf, kctxT, vctx, B2, B2i = kcfs[i2], kctxTs[i2], vctxs[i2], B2s[i2], B2is[i2]
            rkc, rvc, rb2, rb2i = "kctxT%d" % i2, "vctx%d" % i2, "B2%d" % i2, "B2i%d" % i2
            MSET('pool', vaug[:, :, :, 64:65], 1.0, ['vaug'])
            bk, bkr = nbank()
            for kt in range(4):
                TR(bk[:, kt * 128:(kt + 1) * 128], kcf[:, kt].rearrange("p e d -> p (e d)"), identf, ["kcf%d" % i2, 'cstf'], [bkr])
            CP('dve', kctxT, bk[:], [bkr], [rkc])
            for which in range(2):
                sl = wload(w_in[l][:, 2048 + which * 512 + hp * 128:2048 + which * 512 + (hp + 1) * 128],
                           tag=('B', l) if (which == 0 and hp == 0) else None)
                dst = qT if which == 0 else kT
                dn = 'qT' if which == 0 else 'kT'
                wcol = wq8[:, 0:1] if which == 0 else pv[:, 69:70]
                def qk1(g):
                    bank, br = proj(sl, g)
                    f, fr = ntf()
                    CP('act', f, bank[:], [br], [fr])
                    sq, sqr = ntb()
                    ACT(sq, bank[:], AF.Square, [br], [sqr])
                    return (f, fr, sq, sqr)

                def qk2(g, st_):
                    f, fr, sq, sqr = st_
                    b2_, b2r = nbank()
                    MM(b2_[:], bd64, sq, True, True, [sqr, 'cstb'], [b2r])
                    rs, rsr = ntf()
                    ACT(rs, b2_[:], AF.Ln, [b2r], [rsr], bias=EPS)
                    ACT(rs, rs, AF.Exp, [rsr], [rsr], scale=-0.5)
                    STT(dst[:, g * 512:(g + 1) * 512], f, wcol, rs, ALU.mult, ALU.mult, [fr, rsr, 'pv', 'wq8'],
                        [dn])
                    if which == 1 and g == 0:
                        kn, knr = knb, 'knb'
                        STT(kn, f, wcol, rs, ALU.mult, ALU.mult, [fr, rsr, 'pv'], [knr])
                        b3, b3r = nbank()
                        for j in range(4):
                            TR(b3[:, j * 128:(j + 1) * 128], kn[:, j * 128:(j + 1) * 128], identf, [knr, 'cstf'], [b3r])
                        CP('dve', kst[:].rearrange("p a b -> p (a b)"), b3[:], [b3r], ['kst'])
                        for j in range(4):
                            for e in range(2):
                                DMA('sp', nk[j // 2, l, 2 * hp + e, (j % 2) * 128:(j % 2) * 128 + 128, :],
                                    kst[:, j, e * 64:(e + 1) * 64], ['kst'], ['nk'], 'kst')

                st_ = qk1(0)
                for g in range(NG):
                    nx_ = qk1(g + 1) if g + 1 < NG else None
                    qk2(g, st_)
                    st_ = nx_
            sl = wload(w_in[l][:, 3072 + hp * 128:3072 + (hp + 1) * 128])
            for g in range(NG):
                bank, br = nbank()
                for j in range(4):
                    tt = 4 * g + j
                    for kc in range(8):
                        MM(bank[:, j * 128:(j + 1) * 128], hT[:, kc, tt * 128:(tt + 1) * 128], wsl[:, sl, kc, :], kc == 0, kc == 7,
                           ["ws%d" % sl] + hreg(g), [br])
                CP('act', vaug[:, 4 * g:4 * g + 4, :, 0:64], bank[:].rearrange("p (a e d) -> p a e d", a=4, e=2), [br], ['vaug'])
                if g == 0:
                    CP('dve', vst[:].rearrange("p a b -> p (a b)"), bank[:], [br], ['vst'])
                    for j in range(4):
                        for e in range(2):
                            DMA('sp', nv[j // 2, l, 2 * hp + e, (j % 2) * 128:(j % 2) * 128 + 128, :],
                                vst[:, j, e * 64:(e + 1) * 64], ['vst'], ['nv'], 'vst')

            def att_p1(q0, nq, tiles_fn):
                st = []
                tpb = 512 // nq
                for e in range(2):
                    tiles = tiles_fn(e)
                    nt_ = len(tiles)
                    banks = [nbank() for _ in range((nt_ + tpb - 1) // tpb)]
                    qa = qT[e * 64:(e + 1) * 64, q0:q0 + nq]
                    for t, (ka, va, ba) in enumerate(tiles):
                        bs, bsr = banks[t // tpb]
                        c0 = (t % tpb) * nq
                        MM(bs[:, c0:c0 + nq], ka, qa, True, True, ['qT', 'kT', rkc], [bsr])
                    k_ = cnt['P'] % 6
                    cnt['P'] += 1
                    p_, pr = Pbig[k_], "Pb%d" % k_
                    for j, (bs, bsr) in enumerate(banks):
                        ncol = min(tpb, nt_ - j * tpb) * nq
                        ACT(p_[:, j * 512:j * 512 + ncol], bs[:, 0:ncol], AF.Exp, [bsr], [pr])
                    if nq == 128 and tiles[0][2] is not None:
                        TT('dve', p_[:, 0:640], p_[:, 0:640], B2i[:, e, :], ALU.mult, [pr, rb2i], [pr])
                    else:
                        for t, (ka, va, ba) in enumerate(tiles):
                            if ba is not None:
                                TT('dve', p_[:, t * nq:(t + 1) * nq], p_[:, t * nq:(t + 1) * nq], ba, ALU.mult, [pr, rb2], [pr])
                    st.append((tiles, p_, pr))
                return (q0, nq, st)

            def att_p2(state, bi):
                q0, nq, st = state
                bo, bor = nbank(6)
                on = onb[bi]
                onr = "on%d" % bi
                rc = rcb[bi]
                rcr = "rc%d" % bi
                for e in range(2):
                    tiles, p_, pr = st[e]
                    nt_ = len(tiles)
                    for t, (ka, va, ba) in enumerate(tiles):
                        MM(bo[0:nq, e * 65:(e + 1) * 65], p_[:, t * nq:(t + 1) * nq], va, t == 0, t == nt_ - 1,
                           [pr, 'vaug', rvc], [bor])
                for e in range(2):
                    RCP(rc[0:nq, e:e + 1], bo[0:nq, e * 65 + 64:e * 65 + 65], [bor], ["%s_%d" % (rcr, e)])
                    TS('dve', on[0:nq, e * 64:(e + 1) * 64], bo[0:nq, e * 65:e * 65 + 64], rc[0:nq, e:e + 1], None, ALU.mult, None,
                       [bor, "%s_%d" % (rcr, e)], ["%s_%d" % (onr, e)])
                return (q0, nq, bi)

            def att_p3(st3):
                q0, nq, bi = st3
                on = onb[bi]
                onr = "on%d" % bi
                pt, ptr = ntbank(7)
                TR(pt[:, 0:nq], on[0:nq, :], identb[0:nq, 0:nq], [onr + "_0", onr + "_1", 'cstb'], [ptr])
                CP('dve', yT[:, 4 + hp, q0:q0 + nq], pt[:, 0:nq], [ptr], ["y%d_%d" % (4 + hp, q0 // 512)])

            blocks = []
            for s in range(2):
                for qb in range(2):
                    def tf_(e, s=s):
                        return [(kT[e * 64:(e + 1) * 64, s * 256 + kt * 128:s * 256 + (kt + 1) * 128], vaug[:, s * 2 + kt, e, :], None)
                                for kt in range(2)]
                    blocks.append((s * 256 + qb * 128, 128, tf_))

            def ctx_tiles(e):
                return [(kctxT[e * 64:(e + 1) * 64, t * 128:(t + 1) * 128], vctx[:, t, e, :], None) for t in range(4)]

            def single_row(r):
                rs_ = min(max(r - 4, 0), 24)

                def tf_(e):
                    tl = []
                    for t in range(4):
                        kr0 = rs_ + 2 * t
                        tl.append((kT[e * 64:(e + 1) * 64, 512 + kr0 * 64:512 + kr0 * 64 + 128], vaug[:, 4 + kr0 // 2, e, :],
                                   B2[:, e, kr0 - r + 8, :]))
                    return tl + ctx_tiles(e)
                return (512 + r * 64, 64, tf_)

            def pair_rows(r):
                def tf_(e):
                    tl = []
                    for t in range(5):
                        kr0 = r - 4 + 2 * t
                        w = 11 - 2 * t
                        tl.append((kT[e * 64:(e + 1) * 64, 512 + kr0 * 64:512 + kr0 * 64 + 128], vaug[:, 4 + kr0 // 2, e, :],
                                   B2i[:, e, t * 128:(t + 1) * 128]))
                    return tl + ctx_tiles(e)
                return (512 + r * 64, 128, tf_)

            for r in range(4):
                blocks.append(single_row(r))
            for r in range(4, 28, 2):
                blocks.append(pair_rows(r))
            for r in range(28, 32):
                blocks.append(single_row(r))
            if hp + 1 < 4:
                na_prefetch(hp + 1)
            stq = [att_p1(*blocks[0]), att_p1(*blocks[1])]
            prev3 = None
            for i in range(len(blocks)):
                if i + 2 < len(blocks):
                    stq.append(att_p1(*blocks[i + 2]))
                cur3 = att_p2(stq.pop(0), i % 2)
                if prev3 is not None:
                    att_p3(prev3)
                prev3 = cur3
            att_p3(prev3)
            gate_apply(2048 + 1536 + hp * 128, 4 + hp)
        wpre(('C', l), w_in[l][:, 4096:4096 + 128])
        barrier()
        if stop == (l, 'B'):
            raise _Stop()

        cv_ = Carve()
        qrT = cv_.get([128, NT], BF16)
        krT = cv_.get([128, NT], BF16)
        vtok = cv_.get([128, NTT, 256], BF16)
        ofb = cv_.get([128, NTT, 256], BF16)
        lrT = cv_.get([33, NT], BF16)
        rcs = [cv_.get([128, 2, 512], BF16) for _ in range(2)]
        DP = 6
        spb = [cv_.get([128, 128], F32) for _ in range(DP)]
        epb = [cv_.get([128, 128], F32) for _ in range(DP)]
        emb = [cv_.get([128, 128], F32) for _ in range(DP)]
        qtb = [cv_.get([128, 128], BF16) for _ in range(DP)]
        ktb = [cv_.get([128, 128], BF16) for _ in range(DP)]
        kdb = [cv_.get([128, 128], BF16) for _ in range(DP)]
        khb = [cv_.get([128, 128], BF16) for _ in range(DP)]
        Ab = [cv_.get([128, 256], BF16) for _ in range(DP)]
        Sm = cv_.get([128, 128], F32)
        Sbfs = [cv_.get([128, 128], BF16) for _ in range(2)]
        osb = [cv_.get([128, 256], F32) for _ in range(DP)]
        onb2 = [cv_.get([128, 256], BF16) for _ in range(DP)]
        junkb = cv_.get([128, 128], BF16)
        MSET('pool', lrT[32:33, :], 1.0, ['lrT'])
        for gp in range(2):
            cols = [4096 + gp * 128, 4352 + gp * 128, 5664 + gp * 128, 5920 + gp * 128]
            for which in range(2):
                dst = qrT if which == 0 else krT
                dn = 'qr' if which == 0 else 'kr'
                s1 = wload(w_in[l][:, cols[which]:cols[which] + 128], tag=('C', l) if (which == 0 and gp == 0) else None)
                s2 = wload(w_in[l][:, cols[2 + which]:cols[2 + which] + 128])
                for g in range(NG):
                    b1, r1 = proj(s1, g)
                    if g == 0:
                        CP('act', dst[:, 0:512], b1[:], [r1], ["%s0" % dn, dn + 'all'])
                        continue
                    b2_, r2 = proj(s2, g)
                    rb = g % 2
                    DMA('pool', rcs[rb], rope[:, :, (g - 1) * 512:g * 512].rearrange("a p n -> p a n"), (), ["rcs%d" % rb],
                        "rcs%d" % rb)
                    t1, t1r = ntf()
                    t2, t2r = ntf()
                    TT('dve', t1, b1[:], rcs[rb][:, 0, :], ALU.mult, [r1, "rcs%d" % rb], [t1r])
                    TT('dve', t2, b2_[:], rcs[rb][:, 1, :], ALU.mult, [r2, "rcs%d" % rb], [t2r])
                    TT('dve', dst[:, g * 512:(g + 1) * 512], t1, t2, ALU.add, [t1r, t2r], ["%s%d" % (dn, g), dn + 'all'])
            if gp == 0:
                sl = wload(w_in[l][:, 5632:5664], 8, 32)
                for g in range(NG):
                    bank, br = proj(sl, g, 32)
                    CP('act', lrT[0:32, g * 512:(g + 1) * 512], bank[0:32, :], [br], ['lrT'])
            sv = [wload(w_in[l][:, 4608 + gp * 256 + i * 128:4608 + gp * 256 + (i + 1) * 128]) for i in range(2)]
            for i in range(2):
                for g in range(NG):
                    bank, br = proj(sv[i], g)
                    t, tr = ntb()
                    CP('act', t, bank[:], [br], [tr])
                    pt, ptr = ntbank()
                    for j in range(4):
                        TR(pt[:, j * 128:(j + 1) * 128], t[:, j * 128:(j + 1) * 128], identb, [tr, 'cstb'], [ptr])
                    CP('dve', vtok[:, 4 * g:4 * g + 4, i * 128:(i + 1) * 128], pt[:, 0:512].rearrange("p (a b) -> p a b", a=4),
                       [ptr], ['vtok'])
            MSET('pool', sml[:, 61:62], 0.0, ['qrall', 'krall'] + ["qr%d" % g for g in range(NG)] + ["kr%d" % g for g in range(NG)])

            order = []
            for d in range(2):
                for si, (t0, ln, lat) in enumerate(SEQS):
                    tiles = list(range(t0 // 128, (t0 + ln) // 128))
                    if d == 1:
                        tiles = tiles[::-1]
                    for j, tt in enumerate(tiles):
                        order.append((d, si, lat, tt, j == 0, j == len(tiles) - 1))
            no_ = len(order)

            def S1(ii):
                d, si, lat, tt, first, last = order[ii]
                b = ii % DP
                tok = slice(tt * 128, (tt + 1) * 128)
                Z, Zr = nbank(ii % 2)
                MM(Z[:, 256:384], lrT[0:33, tok], wal[0:33, gp * 256 + d * 128:gp * 256 + (d + 1) * 128], True, True,
                   ['lrT', 'wal'], [Zr])
                sp, spr = spb[b], "sp%d" % b
                ACT(sp, Z[:, 256:384], AF.Exp, [Zr], [spr], scale=-1.0)
                ACT(sp, sp, AF.Ln, [spr], [spr], bias=1.0)

            def S2(ii):
                d, si, lat, tt, first, last = order[ii]
                b = ii % DP
                Z, Zr = nbank(ii % 2)
                sp, spr = spb[b], "sp%d" % b
                MM(Z[:, 0:128], sp, Umat[d], True, True, [spr, 'cstf'], [Zr])
                ACT(epb[b], Z[:, 0:128], AF.Exp, [Zr], ["ep%d" % b])
                ACT(emb[b], Z[:, 0:128], AF.Exp, [Zr], ["em%d" % b], scale=-1.0)

            def S3(ii):
                d, si, lat, tt, first, last = order[ii]
                b = ii % DP
                tok = slice(tt * 128, (tt + 1) * 128)
                ep, epr, em, emr = epb[b], "ep%d" % b, emb[b], "em%d" % b
                qt, qtr, kt, ktr, kd, kdr = qtb[b], "qt%d" % b, ktb[b], "kt%d" % b, kdb[b], "kd%d" % b
                dc = 127 if d == 0 else 0
                STT(qt, qrT[:, tok], 0.125, ep, ALU.mult, ALU.mult, ['qrall', epr], [qtr])
                TT('dve', kt, krT[:, tok], em, ALU.mult, ['krall', emr], [ktr])
                TS('dve', kd, kt, ep[:, dc:dc + 1], None, ALU.mult, None, [ktr, epr], [kdr])

            def S3b(ii):
                d, si, lat, tt, first, last = order[ii]
                b = ii % DP
                qt, qtr, kt, ktr, kd, kdr = qtb[b], "qt%d" % b, ktb[b], "kt%d" % b, kdb[b], "kd%d" % b
                pt, ptr = ntbank(7)
                TR(pt[:, 0:128], kd, identb, [kdr, 'cstb'], [ptr])
                for e in range(2):
                    ba, bar = nbank(2 + e)
                    MM(ba[:, 0:128], kt[e * 64:(e + 1) * 64, :], qt[e * 64:(e + 1) * 64, :], True, True, [ktr, qtr], [bar])

            def S4(ii):
                d, si, lat, tt, first, last = order[ii]
                b = ii % DP
                pt, ptr = ntbank(7)
                CP('act', khb[b], pt[:, 0:128], [ptr], ["kh%d" % b])
                for e in range(2):
                    ba, bar = nbank(2 + e)
                    TT('dve', Ab[b][:, e * 128:(e + 1) * 128], ba[:, 0:128], Mmask[d], ALU.mult, [bar, 'cstb'], ["A%d_%d" % (b, e)])

            def S5(ii):
                d, si, lat, tt, first, last = order[ii]
                b = ii % DP
                bd, bdr = nbank(6)
                MM(bd[:, 0:256], khb[b], vtok[:, tt, :], True, True, ["kh%d" % b, 'vtok'], [bdr])

            def SB(ii):
                d, si, lat, tt, first, last = order[ii]
                b = ii % DP
                ep, epr = epb[b], "ep%d" % b
                qt, qtr, kh, khr, A, Ar = qtb[b], "qt%d" % b, khb[b], "kh%d" % b, Ab[b], "A%d" % b
                if first:
                    if lat:
                        src = (sf if d == 0 else sb)[l, 2 * gp:2 * gp + 2].rearrange("e k v -> (e k) v")
                        DMA('sp', Sm, src, (), ['Sm0', 'Sm1'], 'Sm')
                    else:
                        MSET('pool', Sm, 0.0, ['Sm0', 'Sm1'])
                    CP('act', Sbfs[(ii + 1) % 2], Sm, ['Sm0', 'Sm1'], ["Sbf%d" % ((ii + 1) % 2)])
                Sold, Soldr = Sbfs[(ii + 1) % 2], "Sbf%d" % ((ii + 1) % 2)
                Snew, Snewr = Sbfs[ii % 2], "Sbf%d" % (ii % 2)
                bd, bdr = nbank(6)
                bos = [nbank(4 + e) for e in range(2)]
                for e in range(2):
                    MM(bos[e][0][:, 0:128], A[:, e * 128:(e + 1) * 128], vtok[:, tt, e * 128:(e + 1) * 128], True, False,
                       ["%s_%d" % (Ar, e), 'vtok'], [bos[e][1]])
                for e in range(2):
                    MM(bos[e][0][:, 0:128], qt[e * 64:(e + 1) * 64, :], Sold[e * 64:(e + 1) * 64, :], False, True,
                       [qtr, Soldr], [bos[e][1]])
                dc = 127 if d == 0 else 0
                for e in range(2):
                    rows = slice(e * 64, (e + 1) * 64)
                    STT(Sm[rows, :], Sm[rows, :], ep[rows, dc:dc + 1], bd[rows, e * 128:(e + 1) * 128], ALU.mult, ALU.add,
                        ['Sm%d' % e, epr, bdr], ['Sm%d' % e])
                CP('act', Snew, Sm, ['Sm0', 'Sm1'], [Snewr])
                if last and not lat:
                    dst = (nsf if d == 0 else nsb)[si, l, 2 * gp:2 * gp + 2].rearrange("e k v -> (e k) v")
                    DMA('sp', dst, Sm, ['Sm0', 'Sm1'], ['nso'], 'Smo')

            def T1(ii):
                d, si, lat, tt, first, last = order[ii]
                b = ii % DP
                for e in range(2):
                    bo, bor = nbank(4 + e)
                    if d == 0:
                        CP('act', ofb[:, tt, e * 128:(e + 1) * 128], bo[:, 0:128], [bor], ["of%d" % tt])
                    else:
                        TT('dve', osb[b][:, e * 128:(e + 1) * 128], bo[:, 0:128], ofb[:, tt, e * 128:(e + 1) * 128], ALU.add,
                           [bor, "of%d" % tt], ["os%d_%d" % (b, e)])

            def T2(ii):
                d, si, lat, tt, first, last = order[ii]
                if d == 0:
                    return
                b = ii % DP
                c0 = 8 + 6 * (ii % 4)
                for e in range(2):
                    ACT(junkb, osb[b][:, e * 128:(e + 1) * 128], AF.Square, ["os%d_%d" % (b, e)], ['junkb', "gs%d" % (ii % 4)],
                        accum=sml[:, c0 + e:c0 + e + 1])
                ACT(sml[:, c0 + 2:c0 + 4], sml[:, c0:c0 + 2], AF.Ln, ["gs%d" % (ii % 4)], ["gl%d" % (ii % 4)], bias=EPS, scale=1.0 / 128)
                ACT(sml[:, c0 + 4:c0 + 6], sml[:, c0 + 2:c0 + 4], AF.Exp, ["gl%d" % (ii % 4)], ["gr%d" % (ii % 4)], scale=-0.5)

            def T3(ii):
                d, si, lat, tt, first, last = order[ii]
                if d == 0:
                    return
                b = ii % DP
                c0 = 8 + 6 * (ii % 4)
                on, onr = onb2[b], "on2%d" % b
                for e in range(2):
                    TS('dve', on[:, e * 128:(e + 1) * 128], osb[b][:, e * 128:(e + 1) * 128], sml[:, c0 + 4 + e:c0 + 5 + e],
                       None, ALU.mult, None, ["os%d_%d" % (b, e), "gr%d" % (ii % 4)], ["%s_%d" % (onr, e)])
                pt, ptr = ntbank(7)
                for e in range(2):
                    TR(pt[:, 256 + e * 128:256 + (e + 1) * 128], on[:, e * 128:(e + 1) * 128], identb, ["%s_%d" % (onr, e), 'cstb'], [ptr])

            def T4(ii):
                d, si, lat, tt, first, last = order[ii]
                if d == 0:
                    return
                tok = slice(tt * 128, (tt + 1) * 128)
                pt, ptr = ntbank(7)
                for e in range(2):
                    ACT(yT[:, 8 + 2 * gp + e, tok], pt[:, 256 + e * 128:256 + (e + 1) * 128], AF.Copy,
                        [ptr, 'pv'], ["y%d_%d" % (8 + 2 * gp + e, tt // 4)], scale=pv[:, 70:71])

            stages = [(T4, -4), (T3, -3), (T2, -2), (T1, -1), (S2, 5), (S1, 6), (SB, 0), (S5, 1), (S4, 2), (S3b, 3), (S3, 4)]
            for j in range(-6, no_ + 4):
                for fn_, off in stages:
                    ii = j + off
                    if 0 <= ii < no_:
                        fn_(ii)
            for e in range(2):
                gate_apply(5120 + (2 * gp + e) * 128, 8 + 2 * gp + e)
        wpre(('D', l), w_br[l, 0][:, 0:128], 4)
        barrier()
        if stop == (l, 'C'):
            raise _Stop()

        cv_ = Carve()
        mrg = cv_.get([128, 8, NT], BF16)
        macc = cv_.get([128, NT], F32)
        og = cv_.get([128, 8, 512], F32)
        xt = [cv_.get([128, 1024], F32) for _ in range(2)] + [macc[:, 0:1024], macc[:, 1024:2048]]
        maccr = ["macc%d" % g_ for g_ in range(NG)]
        if l + 1 < NL:
            mod_begin(l + 1)
            pmn, pmnr = nbank(7)
        for fc in range(8):
            for b in range(3):
                if l + 1 < NL:
                    mod_chunk(l + 1, fc * 3 + b, pmn, pmnr)
                swb = wload(w_br[l, b][:, fc * 128:(fc + 1) * 128], 4, tag=('D', l) if (fc == 0 and b == 0) else None)
                swg = wload(w_gate[l][:, b * 1024 + fc * 128:b * 1024 + (fc + 1) * 128])
                for g in range(NG):
                    zb, zr = nbank()
                    for kc in range(4):
                        MM(zb[:], wsl[:, swb, kc, :], yT[:, b * 4 + kc, g * 512:(g + 1) * 512], kc == 0, kc == 3,
                           ["ws%d" % swb, "y%d_%d" % (b * 4 + kc, g)], [zr])
                    gb, gr = proj(swg, g)
                    sg, sgr = ntf()
                    ACT(sg, gb[:], AF.Sigmoid, [gr, 'pv'], [sgr], bias=pv[:, 24 + b * 8 + fc:25 + b * 8 + fc])
                    mv = macc[:, g * 512:(g + 1) * 512]
                    mr = "macc%d" % g
                    if b == 0:
                        TT('dve', mv, zb[:], sg, ALU.mult, [zr, sgr], [mr])
                    else:
                        TT('dve', sg, zb[:], sg, ALU.mult, [zr, sgr], [sgr])
                        if b == 1:
                            TT('dve', mv, mv, sg, ALU.add, [mr, sgr], [mr])
                        else:
                            TT('dve', mrg[:, fc, g * 512:(g + 1) * 512], mv, sg, ALU.add, [mr, sgr], ["mrg%d" % g])
        if l + 1 < NL:
            mod_finish(l + 1, pmn, pmnr)
        for g in range(NG):
            s = 0 if g == 0 else 1
            for j in range(4):
                tt = 4 * g + j
                DMA('sp', xt[j], xcur[tt * 128:(tt + 1) * 128, :], ["%s%d" % (xcn, tt)],
                    ["xo%d" % j] + (maccr if j >= 2 else []), "xo%d" % j)
            for fc in range(8):
                sw = wload(w_out[l][:, fc * 128:(fc + 1) * 128])
                bank, br = nbank()
                for kc in range(8):
                    MM(bank[:], wsl[:, sw, kc, :], mrg[:, kc, g * 512:(g + 1) * 512], kc == 0, kc == 7, ["ws%d" % sw, "mrg%d" % g], [br])
                TS('dve', og[:, fc, :], bank[:], modT[:, 16 + fc, s:s + 1], None, ALU.mult, None, [br, modTr], ['og'])
            for j in range(4):
                tt = 4 * g + j
                b = j
                for hh in range(2):
                    bank, br = nbank()
                    for c in range(4):
                        fc = hh * 4 + c
                        TR(bank[:, c * 128:(c + 1) * 128], og[:, fc, j * 128:(j + 1) * 128], identf, ['og', 'cstf'], [br])
                    TT('dve', xt[b][:, hh * 512:(hh + 1) * 512], bank[:], xt[b][:, hh * 512:(hh + 1) * 512], ALU.add,
                       [br, "xo%d" % b], ["xo%d" % b])
                DMA('sp', xnxt[tt * 128:(tt + 1) * 128, :], xt[b], ["xo%d" % b], ["%s%d" % (xnn, tt)], "xo%d" % b)

    try:
        for l_ in range(NL):
            layer(l_)
            if stop == (l_, 'D'):
                raise _Stop()
    except _Stop:
        barrier()
        dbg_h = nc.dram_tensor("dbg_h", [128, 8 * NT], BF16, kind="ExternalOutput").ap()
        dbg_y = nc.dram_tensor("dbg_y", [128, 12 * NT], BF16, kind="ExternalOutput").ap()
        DMA('sp', dbg_h, hT[:].rearrange("p a b -> p (a b)"), ['bar'], ['dbgh'], 'dbgh')
        DMA('sp', dbg_y, yT[:].rearrange("p a b -> p (a b)"), ['bar'], ['dbgy'], 'dbgy')
    S.emit(nc)
    es.close()
    return nc


def _consts():
    s = np.arange(128)[:, None]
    t = np.arange(128)[None, :]
    c = -1.0 / 16.0
    identf = np.eye(128, dtype=np.float32)
    Uf = np.where(s <= t, c, 0.0).astype(np.float32)
    Ub = np.where(s >= t, c, 0.0).astype(np.float32)
    SU = np.where(s > t, c, 0.0).astype(np.float32)
    SL = np.where(s < t, c, 0.0).astype(np.float32)
    Mf = (s <= t).astype(np.float32)
    Mb = (s >= t).astype(np.float32)
    bd = ((s // 64) == (t // 64)).astype(np.float32) / 64.0
    return np.concatenate([identf, Uf, Ub, SU, SL, Mf, Mb, bd], axis=1).astype(np.float32)


def _rope_tables():
    pos = np.arange(2048)
    n_f = 16
    inv = 10000.0 ** (-np.arange(n_f) / n_f)
    ang_r = (pos // 64)[:, None] * inv
    ang_c = (pos % 64)[:, None] * inv
    cos = np.zeros((64, 2048), np.float32)
    sin = np.zeros((64, 2048), np.float32)
    for i in range(64):
        ang = ang_r if i < 32 else ang_c
        j = i % 16
        a_part = (i % 32) < 16
        cos[i] = np.cos(ang[:, j].astype(np.float32))
        sn = np.sin(ang[:, j].astype(np.float32))
        sin[i] = -sn if a_part else sn
    return np.stack([np.tile(cos, (2, 1)), np.tile(sin, (2, 1))]).astype(np.float32)


def _swap_perm():
    i = np.arange(64)
    return np.where((i % 32) < 16, i + 16, i - 16)


def _b2_table(rpb, interior=False):
    L = rpb.shape[0]
    pad = np.concatenate([rpb.reshape(L, 8, -1), np.full((L, 8, 1), NEG, np.float32)], axis=2)
    p = np.arange(128)
    ph = (p // 64)[:, None, None]
    kc = (p % 64)[:, None, None]
    u = np.arange(16)[None, :, None]
    qc = np.arange(64)[None, None, :]
    if interior:
        dr = 7 - u + ph
        lo, hi = -4, 3
    else:
        dr = u - 8 + ph
        lo, hi = -7, 7
    csq = np.clip(qc - 8, 0, 48)
    valid = (dr >= lo) & (dr <= hi) & (kc >= csq) & (kc < csq + 16)
    dc = np.clip(kc - qc + 15, 0, 30)
    idx = np.where(valid, (np.clip(dr, -7, 7) + 7) * 31 + dc, 15 * 31)
    tab = pad[:, :, idx]
    return np.ascontiguousarray(tab.transpose(0, 2, 1, 3, 4)).reshape(L, 128, 8 * 1024).astype(np.float32)


_NC_CACHE = {}


def kernel(x_prompt, x_sample, c, cache_k, cache_v, state_fwd, state_bwd, c_ctx,
           norm_w, w_ada, b_ada, w_in, conv_w, q_norm_w, k_norm_w, rpb,
           w_alpha, b_alpha, gla_norm_w, w_branch, w_gate, b_gate, w_out, _ret_maps=False):
    f = lambda a: np.ascontiguousarray(np.asarray(a, dtype=np.float32))
    x_prompt, x_sample, c, cache_k, cache_v = map(f, (x_prompt, x_sample, c, cache_k, cache_v))
    state_fwd, state_bwd, c_ctx, norm_w, w_ada, b_ada, w_in = map(f, (state_fwd, state_bwd, c_ctx, norm_w, w_ada, b_ada, w_in))
    conv_w, q_norm_w, k_norm_w, rpb, w_alpha, b_alpha = map(f, (conv_w, q_norm_w, k_norm_w, rpb, w_alpha, b_alpha))
    gla_norm_w, w_branch, w_gate, b_gate, w_out = map(f, (gla_norm_w, w_branch, w_gate, b_gate, w_out))

    perm = _swap_perm()
    qcols = np.concatenate([4096 + h * 64 + perm for h in range(4)])
    kcols = np.concatenate([4352 + h * 64 + perm for h in range(4)])
    w_in_ext = np.ascontiguousarray(np.concatenate([w_in, w_in[:, :, qcols], w_in[:, :, kcols]], axis=2))
    pvec = np.zeros((NL, 128, 72), np.float32)
    pvec[:, :, 0:24] = b_ada.reshape(NL, 24, 128).transpose(0, 2, 1)
    pvec[:, :, 24:48] = b_gate.reshape(NL, 24, 128).transpose(0, 2, 1)
    pvec[:, :, 48:56] = norm_w.reshape(NL, 8, 128).transpose(0, 2, 1)
    pvec[:, :, 56:68] = conv_w.reshape(NL, 3, 4, 128).transpose(0, 3, 2, 1).reshape(NL, 128, 12)
    pvec[:, :, 68] = np.tile(q_norm_w, (1, 2))
    pvec[:, :, 69] = np.tile(k_norm_w, (1, 2))
    pvec[:, :, 70] = gla_norm_w
    walpha = np.zeros((NL, 33, 2, 2, 128), np.float32)
    for d in range(2):
        wa = w_alpha[:, d].reshape(NL, 16, 2, 128)
        walpha[:, d * 16:(d + 1) * 16, :, d, :] = wa
        walpha[:, 32, :, d, :] = b_alpha[:, d].reshape(NL, 2, 128)
    walpha = walpha.reshape(NL, 33, 512)
    b2tab = _b2_table(rpb)
    tfull = _b2_table(rpb, interior=True).reshape(NL, 128, 8, 16, 64)
    b2tabi = np.ascontiguousarray(np.stack(
        [np.concatenate([tfull[:, :, :, 11 - 2 * t, :], tfull[:, :, :, 12 - 2 * t, :]], axis=-1) for t in range(5)],
        axis=3)).reshape(NL, 128, 8 * 640)
    cst = _consts()
    rope = _rope_tables()

    in_maps = []
    for i in range(8):
        xin = np.ascontiguousarray(np.concatenate([x_prompt[2 * i], x_prompt[2 * i + 1], x_sample[i]], axis=0))
        cond = np.zeros((128, 8, 2), np.float32)
        cond[:, :, 0] = c_ctx.reshape(8, 128).T
        cond[:, :, 1] = c[i].reshape(8, 128).T
        in_maps.append(dict(
            xin=xin, w_in=w_in_ext, w_ada=w_ada, w_gate=w_gate, w_br=w_branch, w_out=w_out,
            cond=np.ascontiguousarray(cond.reshape(128, 16)), pvec=pvec, walpha=walpha, b2tab=b2tab, b2tabi=b2tabi,
            ck=np.ascontiguousarray(cache_k[i]), cv=np.ascontiguousarray(cache_v[i]),
            sf=np.ascontiguousarray(state_fwd[i]), sb=np.ascontiguousarray(state_bwd[i]),
            cst=cst, rope=rope))
    if _ret_maps:
        return in_maps
    if 'nc' not in _NC_CACHE:
        _NC_CACHE['nc'] = build_nc()
    nc = _NC_CACHE['nc']
    res = run_bass_kernel_spmd(nc, in_maps, core_ids=list(range(8)))
    rs = res.results
    y_p = np.zeros((16, 256, D), np.float32)
    y_s = np.zeros((8, 2048, D), np.float32)
    nk = np.zeros((16, NL, 8, 256, 64), np.float32)
    nv = np.zeros((16, NL, 8, 256, 64), np.float32)
    nsf = np.zeros((16, NL, 4, 64, 128), np.float32)
    nsb = np.zeros((16, NL, 4, 64, 128), np.float32)
    for i in range(8):
        y = rs[i]["y"]
        y_p[2 * i] = y[0:256]
        y_p[2 * i + 1] = y[256:512]
        y_s[i] = y[512:]
        nk[2 * i:2 * i + 2] = rs[i]["nk"]
        nv[2 * i:2 * i + 2] = rs[i]["nv"]
        nsf[2 * i:2 * i + 2] = rs[i]["nsf"]
        nsb[2 * i:2 * i + 2] = rs[i]["nsb"]
    return (y_p, y_s, nk, nv, nsf, nsb)
```

```python
import numpy as np
from contextlib import ExitStack
import concourse.bass as bass
import concourse.mybir as mybir
from concourse.bass_utils import run_bass_kernel_spmd

F32 = mybir.dt.float32
BF16 = mybir.dt.bfloat16
AF = mybir.ActivationFunctionType
ALU = mybir.AluOpType

NL = 4
D = 1024
NT = 2560
NTT = 20
NG = 5
WEXT = 6176
EPS = 1e-6
NEG = -1e30
ENG = ['pe', 'act', 'dve', 'pool', 'sp']


class Sch:
    def __init__(self):
        self.ops = {e: [] for e in ENG}
        self.lastw = {}
        self.readers = {}
        self.dma_count = {}
        self.regions = set()
        self.barrier_op = None

    def add(self, eng, fn, R=(), W=(), dma_key=None):
        op = dict(eng=eng, fn=fn, deps=[], signal=False, dma_key=dma_key, dma_ord=0)
        if dma_key is not None:
            self.dma_count[dma_key] = self.dma_count.get(dma_key, 0) + 1
            op['dma_ord'] = self.dma_count[dma_key]
        deps = {}

        def dep(o, raw):
            if o is None:
                return
            if o['dma_key'] is None and o['eng'] == eng and dma_key is None:
                if eng == 'pe':
                    return
            deps[id(o)] = o

        for r in R:
            self.regions.add(r)
            dep(self.lastw.get(r, self.barrier_op), True)
            if r.startswith('pb') or r.startswith('pt'):
                for o in self.readers.get(r, {}).values():
                    if o['eng'] != eng:
                        dep(o, True)
        for w in W:
            self.regions.add(w)
            dep(self.lastw.get(w, self.barrier_op), False)
            for o in self.readers.get(w, {}).values():
                dep(o, False)
        for r in R:
            self.readers.setdefault(r, {})[(eng, dma_key)] = op
        for w in W:
            self.lastw[w] = op
            self.readers[w] = {}
        op['deps'] = list(deps.values())
        for o in op['deps']:
            if o['dma_key'] is None:
                o['signal'] = True
        self.ops[eng].append(op)
        return op

    def emit(self, nc):
        for e in ENG:
            c = 0
            for op in self.ops[e]:
                if op['signal']:
                    c += 1
                op['ord'] = c
        with ExitStack() as es:
            sems = {e: es.enter_context(nc.semaphore('s_' + e)) for e in ENG}
            dsems = {k: es.enter_context(nc.semaphore('d%d' % i)) for i, k in enumerate(self.dma_count)}
            block = es.enter_context(nc.Block())

            def run(e, h):
                seen = {}
                for op in self.ops[e]:
                    need = {}
                    for o in op['deps']:
                        if o['dma_key'] is not None:
                            key = ('d', o['dma_key'])
                            val = 16 * o['dma_ord']
                        else:
                            key = ('e', o['eng'])
                            val = o['ord']
                        need[key] = max(need.get(key, 0), val)
                    for key, val in need.items():
                        if seen.get(key, 0) >= val:
                            continue
                        seen[key] = val
                        h.wait_ge(dsems[key[1]] if key[0] == 'd' else sems[key[1]], val)
                    ins = op['fn'](h)
                    if op['dma_key'] is not None:
                        ins.then_inc(dsems[op['dma_key']], 16)
                    elif op['signal']:
                        ins.then_inc(sems[e], 1)
                if e == 'sp':
                    for k, c in self.dma_count.items():
                        h.wait_ge(dsems[k], 16 * c)

            @block.tensor
            def _(h):
                run('pe', h)

            @block.scalar
            def _(h):
                run('act', h)

            @block.vector
            def _(h):
                run('dve', h)

            @block.gpsimd
            def _(h):
                run('pool', h)

            @block.sync
            def _(h):
                run('sp', h)


class _Stop(Exception):
    pass


def build_nc(stop=None):
    nc = bass.Bass("TRN2", target_bir_lowering=False)
    S = Sch()

    def din(name, shape):
        return nc.dram_tensor(name, list(shape), F32, kind="ExternalInput").ap()

    def dout(name, shape):
        return nc.dram_tensor(name, list(shape), F32, kind="ExternalOutput").ap()

    xin = din("xin", [NT, D])
    w_in = din("w_in", [NL, D, WEXT])
    w_ada = din("w_ada", [NL, D, 3 * D])
    w_gate = din("w_gate", [NL, D, 3 * D])
    w_br = din("w_br", [NL, 3, 512, D])
    w_out = din("w_out", [NL, D, D])
    cond = din("cond", [128, 16])
    pvec = din("pvec", [NL, 128, 72])
    walpha = din("walpha", [NL, 33, 512])
    b2tab = din("b2tab", [NL, 128, 8 * 1024])
    b2tabi = din("b2tabi", [NL, 128, 8 * 640])
    ck = din("ck", [NL, 8, 512, 64])
    cv = din("cv", [NL, 8, 512, 64])
    sf = din("sf", [NL, 4, 64, 128])
    sb = din("sb", [NL, 4, 64, 128])
    cst = din("cst", [128, 8 * 128])
    rope = din("rope", [2, 128, 2048])
    yout = dout("y", [NT, D])
    nk = dout("nk", [2, NL, 8, 256, 64])
    nv = dout("nv", [2, NL, 8, 256, 64])
    nsf = dout("nsf", [2, NL, 4, 64, 128])
    nsb = dout("nsb", [2, NL, 4, 64, 128])
    xsA = nc.dram_tensor("xsA", [NT, D], F32).ap()
    xsB = nc.dram_tensor("xsB", [NT, D], F32).ap()

    ARENA = 74 * 1024
    es = ExitStack()
    sb_ = lambda n, sh, dt: es.enter_context(nc.sbuf_tensor(n, sh, dt))
    hT = sb_("hT", [128, 8, NT], BF16)
    yT = sb_("yT", [128, 12, NT], BF16)
    NWS = 6
    wsl = sb_("wsl", [128, NWS, 8, 128], BF16)
    arena = sb_("arena", [128, ARENA // 4], F32)
    cstf = sb_("cstf", [128, 5, 128], F32)
    cstb = sb_("cstb", [128, 5, 128], BF16)
    condt = sb_("condt", [128, 16], F32)
    scb = sb_("scb", [128, 16], BF16)
    pv = sb_("pv", [128, 72], F32)
    modTs = [sb_("modT%d" % i, [128, 24, 2], F32) for i in range(2)]
    Amods = [sb_("Amod%d" % i, [128, 8, 2], F32) for i in range(2)]
    pvn = sb_("pvn", [128, 32], F32)
    wq8 = sb_("wq8", [128, 1], F32)
    wal = sb_("wal", [33, 512], BF16)
    tf = sb_("tf", [128, 4, 512], F32)
    tb = sb_("tb", [128, 4, 512], BF16)
    sml = sb_("sml", [128, 64], F32)
    NPB = 8
    pbs = [es.enter_context(nc.psum_tensor("pb%d" % i, [128, 512], F32)) for i in range(NPB)]

    identf, Uf, Ub, SU, SL = [cstf[:, i, :] for i in range(5)]
    identb, Mf, Mb, bd64, ones1k = [cstb[:, i, :] for i in range(5)]
    Umat = [Uf, Ub]
    SUL = [SU, SL]
    Mmask = [Mf, Mb]

    cnt = dict(pb=0, pt=0, ws=0, tf=0, tb=0, P=0)

    def nbank(idx=None):
        if idx is None:
            idx = cnt['pb'] % 6
            cnt['pb'] += 1
        return pbs[idx][:], "pb%d" % idx

    def ntbank(idx=None):
        if idx is None:
            idx = 6 + cnt['pt'] % 2
            cnt['pt'] += 1
        return pbs[idx][:].bitcast(BF16), "pb%d" % idx

    def ntf():
        i = cnt['tf'] % 4
        cnt['tf'] += 1
        return tf[:, i, :], "tf%d" % i

    def ntb():
        i = cnt['tb'] % 4
        cnt['tb'] += 1
        return tb[:, i, :], "tb%d" % i

    def MM(out, lhsT, rhs, start, stop, R, W):
        S.add('pe', lambda h: h.matmul(out, lhsT, rhs, start=start, stop=stop), R, W)

    def TR(out, in_, ident, R, W):
        S.add('pe', lambda h: h.transpose(out, in_, ident), R, W)

    def ACT(out, in_, func, R, W, bias=None, scale=None, accum=None):
        kw = {}
        if bias is not None:
            kw['bias'] = bias
        if scale is not None:
            kw['scale'] = scale
        if accum is not None:
            kw['accum_out'] = accum
        S.add('act', lambda h: h.activation(out, in_, func, **kw), R, W)

    def engname(e):
        return e

    def TT(e, out, in0, in1, op, R, W):
        S.add(e, lambda h: h.tensor_tensor(out, in0, in1, op), R, W)

    def TS(e, out, in0, s1, s2, op0, op1, R, W):
        if s2 is None:
            S.add(e, lambda h: h.tensor_scalar(out, in0, s1, None, op0), R, W)
        else:
            S.add(e, lambda h: h.tensor_scalar(out, in0, s1, s2, op0, op1), R, W)

    def STT(out, in0, sc, in1, op0, op1, R, W):
        S.add('dve', lambda h: h.scalar_tensor_tensor(out, in0, sc, in1, op0, op1), R, W)

    def CP(e, out, in_, R, W):
        if e == 'act':
            S.add('act', lambda h: h.copy(out, in_), R, W)
        else:
            S.add(e, lambda h: h.tensor_copy(out, in_), R, W)

    def MSET(e, ap, val, W):
        S.add(e, lambda h: h.memset(ap, val), (), W)

    def RCP(out, in_, R, W):
        S.add('dve', lambda h: h.reciprocal(out, in_), R, W)

    def DMA(q, out, in_, R, W, key):
        S.add(q, lambda h: h.dma_start(out=out, in_=in_), R, W, dma_key=key)

    def barrier():
        regs = sorted(S.regions)
        MSET('pool', sml[:, 63:64], 0.0, regs + ['bar'])
        S.barrier_op = S.ops['pool'][-1]

    pre = {}

    def wpre(tag, src, nk_=8, ncols=128):
        pre[tag] = wload(src, nk_, ncols)

    def wload(src, nk_=8, ncols=128, tag=None):
        if tag is not None and tag in pre:
            return pre.pop(tag)
        i = cnt['ws'] % NWS
        cnt['ws'] += 1
        DMA('pool', wsl[:, i, 0:nk_, 0:ncols], src.rearrange("(kc p) n -> p kc n", p=128), (), ["ws%d" % i], "ws%d" % i)
        return i

    def hreg(g):
        return ["hT%d" % t for t in range(4 * g, 4 * g + 4)]

    def proj(slot, g, ncols=128):
        bank, br = nbank()
        for kc in range(8):
            MM(bank[0:ncols, :], wsl[:, slot, kc, 0:ncols], hT[:, kc, g * 512:(g + 1) * 512], kc == 0, kc == 7,
               ["ws%d" % slot] + hreg(g), [br])
        return bank, br

    class Carve:
        def __init__(self):
            self.off = 0

        def get(self, shape, dt):
            n = int(np.prod(shape[1:])) * (4 if dt == F32 else 2)
            n = (n + 31) // 32 * 32
            a = self.off // 4
            self.off += n
            assert self.off <= ARENA, ("arena overflow", self.off)
            v = arena[0:shape[0], a:a + n // 4]
            if dt == BF16:
                v = v.bitcast(BF16)
            ne = int(np.prod(shape[1:]))
            v = v[:, 0:ne]
            if len(shape) == 3:
                v = v.rearrange("p (a b) -> p a b", a=shape[1])
            elif len(shape) == 4:
                v = v.rearrange("p (a b c) -> p a b c", a=shape[1], b=shape[2])
            return v

    DMA('sp', cstf[:], cst[:, 0:640].rearrange("p (a b) -> p a b", a=5), (), ['cstf'], 'cstf')
    DMA('pool', cstb[:, 0, :], cst[:, 0:128], (), ['cstb'], 'cstb')
    DMA('pool', cstb[:, 1:4, :], cst[:, 640:1024].rearrange("p (a b) -> p a b", a=3), (), ['cstb'], 'cstb')
    MSET('pool', cstb[:, 4, :], 1.0 / 1024.0, ['cstb'])
    DMA('sp', condt[:], cond, (), ['condt'], 'condt')
    ACT(scb[:], condt[:], AF.Silu, ['condt'], ['scb'])

    xbufs = [(xin, 'xin'), (xsA, 'xsA'), (xsB, 'xsB'), (xsA, 'xsA'), (yout, 'yo')]
    if stop is not None:
        MSET('pool', yT[:].rearrange("p a b -> p (a b)"), 0.0, ['yTinit'])
    SEQS = [(0, 256, 0), (256, 256, 0), (512, 2048, 1)]

    def ucol(tok):
        return tok + 1 + 2 * (0 if tok < 256 else (1 if tok < 512 else 2))

    def mod_begin(lm):
        DMA('sp', pvn[:, 0:24], pvec[lm][:, 0:24], (), ['pvn'], 'pvn')
        DMA('sp', pvn[:, 24:32], pvec[lm][:, 48:56], (), ['pvn'], 'pvn')

    def mod_chunk(lm, j, pm, pmr):
        sl = wload(w_ada[lm][:, j * 128:(j + 1) * 128])
        for kc in range(8):
            MM(pm[:, 2 * j:2 * j + 2], wsl[:, sl, kc, :], scb[:, 2 * kc:2 * kc + 2], kc == 0, kc == 7,
               ["ws%d" % sl, 'scb'], [pmr])

    def mod_finish(lm, pm, pmr):
        mT, mr = modTs[lm % 2], "modT%d" % (lm % 2)
        aT, ar = Amods[lm % 2], "Amod%d" % (lm % 2)
        TT('dve', mT[:], pm[:, 0:48].rearrange("p (a b) -> p a b", b=2),
           pvn[:, 0:24].unsqueeze(2).broadcast_to([128, 24, 2]), ALU.add, [pmr, 'pvn'], [mr])
        TS('dve', aT[:], mT[:, 8:16, :], 1.0, None, ALU.add, None, [mr], [ar])
        TT('dve', aT[:], aT[:], pvn[:, 24:32].unsqueeze(2).broadcast_to([128, 8, 2]), ALU.mult, [ar, 'pvn'], [ar])

    mod_begin(0)
    pm0, pm0r = nbank(7)
    for j0 in range(24):
        mod_chunk(0, j0, pm0, pm0r)
    mod_finish(0, pm0, pm0r)

    def layer(l):
        modT, modTr = modTs[l % 2], "modT%d" % (l % 2)
        Amod, Amodr = Amods[l % 2], "Amod%d" % (l % 2)
        xcur, xcn = xbufs[l]
        xnxt, xnn = xbufs[l + 1]
        barrier()
        DMA('sp', pv[:], pvec[l], (), ['pv'], 'pv')
        DMA('pool', wal[:], walpha[l], (), ['wal'], 'wal')
        TS('dve', wq8[:], pv[:, 68:69], 0.125, None, ALU.mult, None, ['pv'], ['wq8'])

        cv_ = Carve()
        xt = [cv_.get([128, 1024], F32) for _ in range(4)]
        xn = [cv_.get([128, 1024], BF16) for _ in range(3)]
        sqj = cv_.get([128, 1024], BF16)

        def X1(tt):
            b = tt % 4
            DMA('sp', xt[b], xcur[tt * 128:(tt + 1) * 128, :], ["%s%d" % (xcn, tt)], ["xt%d" % b], "xt%d" % b)

        def X2(tt):
            b = tt % 4
            ACT(sqj, xt[b], AF.Square, ["xt%d" % b], ['sqj', "ss%d" % b], accum=sml[:, b:b + 1])
            ACT(sml[:, 4 + b:5 + b], sml[:, b:b + 1], AF.Ln, ["ss%d" % b], ["ln%d" % b], bias=EPS, scale=1.0 / D)
            ACT(sml[:, 8 + b:9 + b], sml[:, 4 + b:5 + b], AF.Exp, ["ln%d" % b], ["rs%d" % b], scale=-0.5)

        def X3(tt):
            b = tt % 4
            n3 = tt % 3
            TS('dve', xn[n3], xt[b], sml[:, 8 + b:9 + b], None, ALU.mult, None, ["xt%d" % b, "rs%d" % b], ["xn%d" % n3])
            pt, ptr = ntbank(6 + tt % 2)
            for c in range(8):
                TR(pt[:, c * 128:(c + 1) * 128], xn[n3][:, c * 128:(c + 1) * 128], identb, ["xn%d" % n3, 'cstb'], [ptr])

        def X4(tt):
            s = 0 if tt < 4 else 1
            pt, ptr = ntbank(6 + tt % 2)
            for c in range(8):
                o = hT[:, c, tt * 128:(tt + 1) * 128]
                i = pt[:, c * 128:(c + 1) * 128]
                if tt % 2 == 0:
                    ACT(o, i, AF.Identity, [ptr, Amodr, modTr], ["hT%d" % tt], bias=modT[:, c, s:s + 1],
                        scale=Amod[:, c, s:s + 1])
                else:
                    TS('dve', o, i, Amod[:, c, s:s + 1], modT[:, c, s:s + 1], ALU.mult, ALU.add,
                       [ptr, Amodr, modTr], ["hT%d" % tt])

        for j in range(-2, NTT + 1):
            for fn_, off in ((X4, -1), (X3, 0), (X2, 1), (X1, 2)):
                if 0 <= j + off < NTT:
                    fn_(j + off)
        wpre(('A', l), w_in[l][:, 0:128])
        barrier()
        if stop == (l, 'X'):
            raise _Stop()

        def gate_apply(col0, ych):
            sl = wload(w_in[l][:, col0:col0 + 128])
            for g in range(NG):
                bank, br = proj(sl, g)
                t, tr = ntb()
                ACT(t, bank[:], AF.Silu, [br], [tr])
                yv = yT[:, ych, g * 512:(g + 1) * 512]
                TT('dve', yv, yv, t, ALU.mult, [tr, "y%d_%d" % (ych, g)], ["y%d_%d" % (ych, g)])

        cv_ = Carve()
        ubuf = cv_.get([128, NT + 6], F32)
        MSET('pool', ubuf, 0.0, ["u%d" % g for g in range(NG)])
        for cc in range(4):
            s_xa = wload(w_in[l][:, cc * 128:(cc + 1) * 128], tag=('A', l) if cc == 0 else None)
            s_ca = wload(w_in[l][:, 1024 + cc * 128:1024 + (cc + 1) * 128])
            for g in range(NG):
                b1, r1 = proj(s_xa, g)
                b2, r2 = proj(s_ca, g)
                t, tr = ntf()
                CP('act', t, b1[:], [r1], [tr])
                if g == 0:
                    for hh in range(2):
                        TT('dve', ubuf[:, ucol(hh * 256):ucol(hh * 256) + 256], b2[:, hh * 256:(hh + 1) * 256],
                           t[:, hh * 256:(hh + 1) * 256], ALU.mult, [r2, tr], ["u0"])
                else:
                    TT('dve', ubuf[:, ucol(g * 512):ucol(g * 512) + 512], b2[:], t, ALU.mult, [r2, tr], ["u%d" % g])
            s_ba = wload(w_in[l][:, 512 + cc * 128:512 + (cc + 1) * 128])
            for g in range(NG):
                b3, r3 = proj(s_ba, g)
                t, tr = ntf()
                segs = [(0, 256), (256, 256)] if g == 0 else [(g * 512, 512)]
                ur = ["u%d" % gg for gg in range(max(0, g - 1), min(NG, g + 2))]
                for (t0, n) in segs:
                    u0 = ucol(t0)
                    o = t[:, t0 - g * 512:t0 - g * 512 + n]
                    TS('dve', o, ubuf[:, u0 - 1:u0 - 1 + n], pv[:, 56 + cc * 3:57 + cc * 3], None, ALU.mult, None,
                       ur + ['pv'], [tr])
                    STT(o, ubuf[:, u0:u0 + n], pv[:, 57 + cc * 3:58 + cc * 3], o, ALU.mult, ALU.add, ur + ['pv', tr], [tr])
                    STT(o, ubuf[:, u0 + 1:u0 + 1 + n], pv[:, 58 + cc * 3:59 + cc * 3], o, ALU.mult, ALU.add,
                        ur + ['pv', tr], [tr])
                TT('dve', yT[:, cc, g * 512:(g + 1) * 512], b3[:], t, ALU.mult, [r3, tr], ["y%d_%d" % (cc, g)])
            gate_apply(1536 + cc * 128, cc)
        wpre(('B', l), w_in[l][:, 2048:2048 + 128])
        barrier()
        if stop == (l, 'A'):
            raise _Stop()

        cv_ = Carve()
        qT = cv_.get([128, NT], BF16)
        kT = cv_.get([128, NT], BF16)
        vaug = cv_.get([128, NTT, 2, 65], BF16)
        vodd = cv_.get([128, 15, 2, 65], BF16)
        kcfs = [cv_.get([128, 4, 2, 64], F32) for _ in range(2)]
        kctxTs = [cv_.get([128, 512], BF16) for _ in range(2)]
        vctxs = [cv_.get([128, 4, 2, 65], BF16) for _ in range(2)]
        B2s = [cv_.get([128, 2, 16, 64], BF16) for _ in range(2)]
        B2is = [cv_.get([128, 2, 640], BF16) for _ in range(2)]
        Pbig = [cv_.get([128, 1152], BF16) for _ in range(6)]
        onb = [cv_.get([128, 128], BF16) for _ in range(2)]
        kst = cv_.get([128, 4, 128], F32)
        knb = cv_.get([128, 512], F32)
        vst = cv_.get([128, 4, 128], F32)
        rcb = [cv_.get([128, 2], F32) for _ in range(2)]
        for i2 in range(2):
            MSET('pool', vctxs[i2][:, :, :, 64:65], 1.0, ["vctx%d" % i2])

        def na_prefetch(hp_):
            i2 = hp_ % 2
            for e in range(2):
                DMA('sp', kcfs[i2][:, :, e, :], ck[l, 2 * hp_ + e].rearrange("(kt p) d -> p kt d", p=128), (), ["kcf%d" % i2],
                    'kcf%d_%d' % (i2, e))
                DMA('pool', vctxs[i2][:, :, e, 0:64], cv[l, 2 * hp_ + e].rearrange("(kt p) d -> p kt d", p=128), (),
                    ["vctx%d" % i2], 'vctx%d_%d' % (i2, e))
            DMA('pool', B2s[i2][:], b2tab[l][:, 2 * hp_ * 1024:(2 * hp_ + 2) * 1024].rearrange("p (e u q) -> p e u q", e=2, u=16),
                (), ["B2%d" % i2], 'B2%d' % i2)
            DMA('pool', B2is[i2][:], b2tabi[l][:, 2 * hp_ * 640:(2 * hp_ + 2) * 640].rearrange("p (e q) -> p e q", e=2),
                (), ["B2i%d" % i2], 'B2i%d' % i2)
            v2 = B2s[i2][:].rearrange("p e u q -> p (e u q)")
            ACT(v2, v2, AF.Exp, ["B2%d" % i2], ["B2%d" % i2])
            v2 = B2is[i2][:].rearrange("p e q -> p (e q)")
            ACT(v2, v2, AF.Exp, ["B2i%d" % i2], ["B2i%d" % i2])

        na_prefetch(0)
        for hp in range(4):
            i2 = hp % 2
            kcf, kctxT, vctx, B2, B2i = kcfs[i2], kctxTs[i2], vctxs[i2], B2s[i2], B2is[i2]
            rkc, rvc, rb2, rb2i = "kctxT%d" % i2, "vctx%d" % i2, "B2%d" % i2, "B2i%d" % i2
            MSET('pool', vaug[:, :, :, 64:65], 1.0, ['vaug'])
            bk, bkr = nbank()
            for kt in range(4):
                TR(bk[:, kt * 128:(kt + 1) * 128], kcf[:, kt].rearrange("p e d -> p (e d)"), identf, ["kcf%d" % i2, 'cstf'], [bkr])
            CP('dve', kctxT, bk[:], [bkr], [rkc])
            for which in range(2):
                sl = wload(w_in[l][:, 2048 + which * 512 + hp * 128:2048 + which * 512 + (hp + 1) * 128],
                           tag=('B', l) if (which == 0 and hp == 0) else None)
                dst = qT if which == 0 else kT
                dn = 'qT' if which == 0 else 'kT'
                wcol = wq8[:, 0:1] if which == 0 else pv[:, 69:70]
                def qk1(g):
                    bank, br = proj(sl, g)
                    sq, sqr = ntb()
                    ACT(sq, bank[:], AF.Square, [br], [sqr])
                    f, fr = ntf()
                    CP('dve', f, bank[:], [br], [fr])
                    return (f, fr, sq, sqr)

                def qk2(g, st_):
                    f, fr, sq, sqr = st_
                    b2_, b2r = nbank()
                    MM(b2_[:], bd64, sq, True, True, [sqr, 'cstb'], [b2r])
                    rs, rsr = ntf()
                    ACT(rs, b2_[:], AF.Ln, [b2r], [rsr], bias=EPS)
                    ACT(rs, rs, AF.Exp, [rsr], [rsr], scale=-0.5)
                    STT(dst[:, g * 512:(g + 1) * 512], f, wcol, rs, ALU.mult, ALU.mult, [fr, rsr, 'pv', 'wq8'],
                        [dn])
                    if which == 1 and g == 0:
                        kn, knr = knb, 'knb'
                        STT(kn, f, wcol, rs, ALU.mult, ALU.mult, [fr, rsr, 'pv'], [knr])
                        b3, b3r = nbank()
                        for j in range(4):
                            TR(b3[:, j * 128:(j + 1) * 128], kn[:, j * 128:(j + 1) * 128], identf, [knr, 'cstf'], [b3r])
                        CP('dve', kst[:].rearrange("p a b -> p (a b)"), b3[:], [b3r], ['kst'])
                        for j in range(4):
                            for e in range(2):
                                DMA('sp', nk[j // 2, l, 2 * hp + e, (j % 2) * 128:(j % 2) * 128 + 128, :],
                                    kst[:, j, e * 64:(e + 1) * 64], ['kst'], ['nk'], 'kst')

                st_ = qk1(0)
                for g in range(NG):
                    nx_ = qk1(g + 1) if g + 1 < NG else None
                    qk2(g, st_)
                    st_ = nx_
            sl = wload(w_in[l][:, 3072 + hp * 128:3072 + (hp + 1) * 128])
            for g in range(NG):
                bank, br = nbank()
                for j in range(4):
                    tt = 4 * g + j
                    for kc in range(8):
                        MM(bank[:, j * 128:(j + 1) * 128], hT[:, kc, tt * 128:(tt + 1) * 128], wsl[:, sl, kc, :], kc == 0, kc == 7,
                           ["ws%d" % sl] + hreg(g), [br])
                CP('act', vaug[:, 4 * g:4 * g + 4, :, 0:64], bank[:].rearrange("p (a e d) -> p a e d", a=4, e=2), [br], ['vaug'])
                if g == 0:
                    CP('dve', vst[:].rearrange("p a b -> p (a b)"), bank[:], [br], ['vst'])
                    for j in range(4):
                        for e in range(2):
                            DMA('sp', nv[j // 2, l, 2 * hp + e, (j % 2) * 128:(j % 2) * 128 + 128, :],
                                vst[:, j, e * 64:(e + 1) * 64], ['vst'], ['nv'], 'vst')

            def att_p1(q0, nq, tiles_fn):
                st = []
                tpb = 512 // nq
                for e in range(2):
                    tiles = tiles_fn(e)
                    nt_ = len(tiles)
                    banks = [nbank() for _ in range((nt_ + tpb - 1) // tpb)]
                    qa = qT[e * 64:(e + 1) * 64, q0:q0 + nq]
                    for t, (ka, va, ba) in enumerate(tiles):
                        bs, bsr = banks[t // tpb]
                        c0 = (t % tpb) * nq
                        MM(bs[:, c0:c0 + nq], ka, qa, True, True, ['qT', 'kT', rkc], [bsr])
                    k_ = cnt['P'] % 6
                    cnt['P'] += 1
                    p_, pr = Pbig[k_], "Pb%d" % k_
                    for j, (bs, bsr) in enumerate(banks):
                        ncol = min(tpb, nt_ - j * tpb) * nq
                        ACT(p_[:, j * 512:j * 512 + ncol], bs[:, 0:ncol], AF.Exp, [bsr], [pr])
                    if nq == 128 and tiles[0][2] is not None:
                        TT('dve', p_[:, 0:640], p_[:, 0:640], B2i[:, e, :], ALU.mult, [pr, rb2i], [pr])
                    else:
                        for t, (ka, va, ba) in enumerate(tiles):
                            if ba is not None:
                                TT('dve', p_[:, t * nq:(t + 1) * nq], p_[:, t * nq:(t + 1) * nq], ba, ALU.mult, [pr, rb2], [pr])
                    st.append((tiles, p_, pr))
                return (q0, nq, st)

            def att_p2(state, bi):
                q0, nq, st = state
                bo, bor = nbank(6)
                on = onb[bi]
                onr = "on%d" % bi
                rc = rcb[bi]
                rcr = "rc%d" % bi
                for e in range(2):
                    tiles, p_, pr = st[e]
                    nt_ = len(tiles)
                    for t, (ka, va, ba) in enumerate(tiles):
                        MM(bo[0:nq, e * 65:(e + 1) * 65], p_[:, t * nq:(t + 1) * nq], va, t == 0, t == nt_ - 1,
                           [pr, 'vaug', rvc], [bor])
                for e in range(2):
                    RCP(rc[0:nq, e:e + 1], bo[0:nq, e * 65 + 64:e * 65 + 65], [bor], ["%s_%d" % (rcr, e)])
                    TS('dve', on[0:nq, e * 64:(e + 1) * 64], bo[0:nq, e * 65:e * 65 + 64], rc[0:nq, e:e + 1], None, ALU.mult, None,
                       [bor, "%s_%d" % (rcr, e)], ["%s_%d" % (onr, e)])
                return (q0, nq, bi)

            def att_p3(st3):
                q0, nq, bi = st3
                on = onb[bi]
                onr = "on%d" % bi
                pt, ptr = ntbank(7)
                TR(pt[:, 0:nq], on[0:nq, :], identb[0:nq, 0:nq], [onr + "_0", onr + "_1", 'cstb'], [ptr])
                CP('dve', yT[:, 4 + hp, q0:q0 + nq], pt[:, 0:nq], [ptr], ["y%d_%d" % (4 + hp, q0 // 512)])

            blocks = []
            for s in range(2):
                for qb in range(2):
                    def tf_(e, s=s):
                        return [(kT[e * 64:(e + 1) * 64, s * 256 + kt * 128:s * 256 + (kt + 1) * 128], vaug[:, s * 2 + kt, e, :], None)
                                for kt in range(2)]
                    blocks.append((s * 256 + qb * 128, 128, tf_))

            def ctx_tiles(e):
                return [(kctxT[e * 64:(e + 1) * 64, t * 128:(t + 1) * 128], vctx[:, t, e, :], None) for t in range(4)]

            def single_row(r):
                rs_ = min(max(r - 4, 0), 24)

                def tf_(e):
                    tl = []
                    for t in range(4):
                        kr0 = rs_ + 2 * t
                        tl.append((kT[e * 64:(e + 1) * 64, 512 + kr0 * 64:512 + kr0 * 64 + 128], vaug[:, 4 + kr0 // 2, e, :],
                                   B2[:, e, kr0 - r + 8, :]))
                    return tl + ctx_tiles(e)
                return (512 + r * 64, 64, tf_)

            def pair_rows(r):
                def tf_(e):
                    tl = []
                    for t in range(5):
                        kr0 = r - 4 + 2 * t
                        w = 11 - 2 * t
                        tl.append((kT[e * 64:(e + 1) * 64, 512 + kr0 * 64:512 + kr0 * 64 + 128], vaug[:, 4 + kr0 // 2, e, :],
                                   B2i[:, e, t * 128:(t + 1) * 128]))
                    return tl + ctx_tiles(e)
                return (512 + r * 64, 128, tf_)

            for r in range(4):
                blocks.append(single_row(r))
            for r in range(4, 28, 2):
                blocks.append(pair_rows(r))
            for r in range(28, 32):
                blocks.append(single_row(r))
            if hp + 1 < 4:
                na_prefetch(hp + 1)
            stq = [att_p1(*blocks[0]), att_p1(*blocks[1])]
            prev3 = None
            for i in range(len(blocks)):
                if i + 2 < len(blocks):
                    stq.append(att_p1(*blocks[i + 2]))
                cur3 = att_p2(stq.pop(0), i % 2)
                if prev3 is not None:
                    att_p3(prev3)
                prev3 = cur3
            att_p3(prev3)
            gate_apply(2048 + 1536 + hp * 128, 4 + hp)
        wpre(('C', l), w_in[l][:, 4096:4096 + 128])
        barrier()
        if stop == (l, 'B'):
            raise _Stop()

        cv_ = Carve()
        qrT = cv_.get([128, NT], BF16)
        krT = cv_.get([128, NT], BF16)
        vtok = cv_.get([128, NTT, 256], BF16)
        ofb = cv_.get([128, NTT, 256], BF16)
        lrT = cv_.get([33, NT], BF16)
        rcs = [cv_.get([128, 2, 512], BF16) for _ in range(2)]
        DP = 6
        spb = [cv_.get([128, 128], F32) for _ in range(DP)]
        epb = [cv_.get([128, 128], F32) for _ in range(DP)]
        emb = [cv_.get([128, 128], F32) for _ in range(DP)]
        qtb = [cv_.get([128, 128], BF16) for _ in range(DP)]
        ktb = [cv_.get([128, 128], BF16) for _ in range(DP)]
        kdb = [cv_.get([128, 128], BF16) for _ in range(DP)]
        khb = [cv_.get([128, 128], BF16) for _ in range(DP)]
        Ab = [cv_.get([128, 256], BF16) for _ in range(DP)]
        Sm = cv_.get([128, 128], F32)
        Sbfs = [cv_.get([128, 128], BF16) for _ in range(2)]
        osb = [cv_.get([128, 256], F32) for _ in range(DP)]
        onb2 = [cv_.get([128, 256], BF16) for _ in range(DP)]
        junkb = cv_.get([128, 128], BF16)
        MSET('pool', lrT[32:33, :], 1.0, ['lrT'])
        for gp in range(2):
            cols = [4096 + gp * 128, 4352 + gp * 128, 5664 + gp * 128, 5920 + gp * 128]
            for which in range(2):
                dst = qrT if which == 0 else krT
                dn = 'qr' if which == 0 else 'kr'
                s1 = wload(w_in[l][:, cols[which]:cols[which] + 128], tag=('C', l) if (which == 0 and gp == 0) else None)
                s2 = wload(w_in[l][:, cols[2 + which]:cols[2 + which] + 128])
                for g in range(NG):
                    b1, r1 = proj(s1, g)
                    if g == 0:
                        CP('act', dst[:, 0:512], b1[:], [r1], ["%s0" % dn, dn + 'all'])
                        continue
                    b2_, r2 = proj(s2, g)
                    rb = g % 2
                    DMA('pool', rcs[rb], rope[:, :, (g - 1) * 512:g * 512].rearrange("a p n -> p a n"), (), ["rcs%d" % rb],
                        "rcs%d" % rb)
                    t1, t1r = ntf()
                    t2, t2r = ntf()
                    TT('dve', t1, b1[:], rcs[rb][:, 0, :], ALU.mult, [r1, "rcs%d" % rb], [t1r])
                    TT('dve', t2, b2_[:], rcs[rb][:, 1, :], ALU.mult, [r2, "rcs%d" % rb], [t2r])
                    TT('dve', dst[:, g * 512:(g + 1) * 512], t1, t2, ALU.add, [t1r, t2r], ["%s%d" % (dn, g), dn + 'all'])
            if gp == 0:
                sl = wload(w_in[l][:, 5632:5664], 8, 32)
                for g in range(NG):
                    bank, br = proj(sl, g, 32)
                    CP('act', lrT[0:32, g * 512:(g + 1) * 512], bank[0:32, :], [br], ['lrT'])
            sv = [wload(w_in[l][:, 4608 + gp * 256 + i * 128:4608 + gp * 256 + (i + 1) * 128]) for i in range(2)]
            for i in range(2):
                for g in range(NG):
                    bank, br = proj(sv[i], g)
                    t, tr = ntb()
                    CP('act', t, bank[:], [br], [tr])
                    pt, ptr = ntbank()
                    for j in range(4):
                        TR(pt[:, j * 128:(j + 1) * 128], t[:, j * 128:(j + 1) * 128], identb, [tr, 'cstb'], [ptr])
                    CP('dve', vtok[:, 4 * g:4 * g + 4, i * 128:(i + 1) * 128], pt[:, 0:512].rearrange("p (a b) -> p a b", a=4),
                       [ptr], ['vtok'])
            MSET('pool', sml[:, 61:62], 0.0, ['qrall', 'krall'] + ["qr%d" % g for g in range(NG)] + ["kr%d" % g for g in range(NG)])

            order = []
            for d in range(2):
                for si, (t0, ln, lat) in enumerate(SEQS):
                    tiles = list(range(t0 // 128, (t0 + ln) // 128))
                    if d == 1:
                        tiles = tiles[::-1]
                    for j, tt in enumerate(tiles):
                        order.append((d, si, lat, tt, j == 0, j == len(tiles) - 1))
            no_ = len(order)

            def S1(ii):
                d, si, lat, tt, first, last = order[ii]
                b = ii % DP
                tok = slice(tt * 128, (tt + 1) * 128)
                Z, Zr = nbank(ii % 2)
                MM(Z[:, 256:384], lrT[0:33, tok], wal[0:33, gp * 256 + d * 128:gp * 256 + (d + 1) * 128], True, True,
                   ['lrT', 'wal'], [Zr])
                sp, spr = spb[b], "sp%d" % b
                ACT(sp, Z[:, 256:384], AF.Exp, [Zr], [spr], scale=-1.0)
                ACT(sp, sp, AF.Ln, [spr], [spr], bias=1.0)

            def S2(ii):
                d, si, lat, tt, first, last = order[ii]
                b = ii % DP
                Z, Zr = nbank(ii % 2)
                sp, spr = spb[b], "sp%d" % b
                MM(Z[:, 0:128], sp, Umat[d], True, True, [spr, 'cstf'], [Zr])
                ACT(epb[b], Z[:, 0:128], AF.Exp, [Zr], ["ep%d" % b])
                ACT(emb[b], Z[:, 0:128], AF.Exp, [Zr], ["em%d" % b], scale=-1.0)

            def S3(ii):
                d, si, lat, tt, first, last = order[ii]
                b = ii % DP
                tok = slice(tt * 128, (tt + 1) * 128)
                ep, epr, em, emr = epb[b], "ep%d" % b, emb[b], "em%d" % b
                qt, qtr, kt, ktr, kd, kdr = qtb[b], "qt%d" % b, ktb[b], "kt%d" % b, kdb[b], "kd%d" % b
                dc = 127 if d == 0 else 0
                STT(qt, qrT[:, tok], 0.125, ep, ALU.mult, ALU.mult, ['qrall', epr], [qtr])
                TT('dve', kt, krT[:, tok], em, ALU.mult, ['krall', emr], [ktr])
                TS('dve', kd, kt, ep[:, dc:dc + 1], None, ALU.mult, None, [ktr, epr], [kdr])

            def S3b(ii):
                d, si, lat, tt, first, last = order[ii]
                b = ii % DP
                qt, qtr, kt, ktr, kd, kdr = qtb[b], "qt%d" % b, ktb[b], "kt%d" % b, kdb[b], "kd%d" % b
                pt, ptr = ntbank(7)
                TR(pt[:, 0:128], kd, identb, [kdr, 'cstb'], [ptr])
                for e in range(2):
                    ba, bar = nbank(2 + e)
                    MM(ba[:, 0:128], kt[e * 64:(e + 1) * 64, :], qt[e * 64:(e + 1) * 64, :], True, True, [ktr, qtr], [bar])

            def S4(ii):
                d, si, lat, tt, first, last = order[ii]
                b = ii % DP
                pt, ptr = ntbank(7)
                CP('act', khb[b], pt[:, 0:128], [ptr], ["kh%d" % b])
                for e in range(2):
                    ba, bar = nbank(2 + e)
                    TT('dve', Ab[b][:, e * 128:(e + 1) * 128], ba[:, 0:128], Mmask[d], ALU.mult, [bar, 'cstb'], ["A%d_%d" % (b, e)])

            def S5(ii):
                d, si, lat, tt, first, last = order[ii]
                b = ii % DP
                bd, bdr = nbank(6)
                MM(bd[:, 0:256], khb[b], vtok[:, tt, :], True, True, ["kh%d" % b, 'vtok'], [bdr])

            def SB(ii):
                d, si, lat, tt, first, last = order[ii]
                b = ii % DP
                ep, epr = epb[b], "ep%d" % b
                qt, qtr, kh, khr, A, Ar = qtb[b], "qt%d" % b, khb[b], "kh%d" % b, Ab[b], "A%d" % b
                if first:
                    if lat:
                        src = (sf if d == 0 else sb)[l, 2 * gp:2 * gp + 2].rearrange("e k v -> (e k) v")
                        DMA('sp', Sm, src, (), ['Sm0', 'Sm1'], 'Sm')
                    else:
                        MSET('pool', Sm, 0.0, ['Sm0', 'Sm1'])
                    CP('act', Sbfs[(ii + 1) % 2], Sm, ['Sm0', 'Sm1'], ["Sbf%d" % ((ii + 1) % 2)])
                Sold, Soldr = Sbfs[(ii + 1) % 2], "Sbf%d" % ((ii + 1) % 2)
                Snew, Snewr = Sbfs[ii % 2], "Sbf%d" % (ii % 2)
                bd, bdr = nbank(6)
                bos = [nbank(4 + e) for e in range(2)]
                for e in range(2):
                    MM(bos[e][0][:, 0:128], A[:, e * 128:(e + 1) * 128], vtok[:, tt, e * 128:(e + 1) * 128], True, False,
                       ["%s_%d" % (Ar, e), 'vtok'], [bos[e][1]])
                for e in range(2):
                    MM(bos[e][0][:, 0:128], qt[e * 64:(e + 1) * 64, :], Sold[e * 64:(e + 1) * 64, :], False, True,
                       [qtr, Soldr], [bos[e][1]])
                dc = 127 if d == 0 else 0
                for e in range(2):
                    rows = slice(e * 64, (e + 1) * 64)
                    STT(Sm[rows, :], Sm[rows, :], ep[rows, dc:dc + 1], bd[rows, e * 128:(e + 1) * 128], ALU.mult, ALU.add,
                        ['Sm%d' % e, epr, bdr], ['Sm%d' % e])
                CP('act', Snew, Sm, ['Sm0', 'Sm1'], [Snewr])
                if last and not lat:
                    dst = (nsf if d == 0 else nsb)[si, l, 2 * gp:2 * gp + 2].rearrange("e k v -> (e k) v")
                    DMA('sp', dst, Sm, ['Sm0', 'Sm1'], ['nso'], 'Smo')

            def T1(ii):
                d, si, lat, tt, first, last = order[ii]
                b = ii % DP
                for e in range(2):
                    bo, bor = nbank(4 + e)
                    if d == 0:
                        CP('act', ofb[:, tt, e * 128:(e + 1) * 128], bo[:, 0:128], [bor], ["of%d" % tt])
                    else:
                        TT('dve', osb[b][:, e * 128:(e + 1) * 128], bo[:, 0:128], ofb[:, tt, e * 128:(e + 1) * 128], ALU.add,
                           [bor, "of%d" % tt], ["os%d_%d" % (b, e)])

            def T2(ii):
                d, si, lat, tt, first, last = order[ii]
                if d == 0:
                    return
                b = ii % DP
                c0 = 8 + 6 * (ii % 4)
                for e in range(2):
                    ACT(junkb, osb[b][:, e * 128:(e + 1) * 128], AF.Square, ["os%d_%d" % (b, e)], ['junkb', "gs%d" % (ii % 4)],
                        accum=sml[:, c0 + e:c0 + e + 1])
                ACT(sml[:, c0 + 2:c0 + 4], sml[:, c0:c0 + 2], AF.Ln, ["gs%d" % (ii % 4)], ["gl%d" % (ii % 4)], bias=EPS, scale=1.0 / 128)
                ACT(sml[:, c0 + 4:c0 + 6], sml[:, c0 + 2:c0 + 4], AF.Exp, ["gl%d" % (ii % 4)], ["gr%d" % (ii % 4)], scale=-0.5)

            def T3(ii):
                d, si, lat, tt, first, last = order[ii]
                if d == 0:
                    return
                b = ii % DP
                c0 = 8 + 6 * (ii % 4)
                on, onr = onb2[b], "on2%d" % b
                for e in range(2):
                    TS('dve', on[:, e * 128:(e + 1) * 128], osb[b][:, e * 128:(e + 1) * 128], sml[:, c0 + 4 + e:c0 + 5 + e],
                       None, ALU.mult, None, ["os%d_%d" % (b, e), "gr%d" % (ii % 4)], ["%s_%d" % (onr, e)])
                pt, ptr = ntbank(7)
                for e in range(2):
                    TR(pt[:, 256 + e * 128:256 + (e + 1) * 128], on[:, e * 128:(e + 1) * 128], identb, ["%s_%d" % (onr, e), 'cstb'], [ptr])

            def T4(ii):
                d, si, lat, tt, first, last = order[ii]
                if d == 0:
                    return
                tok = slice(tt * 128, (tt + 1) * 128)
                pt, ptr = ntbank(7)
                for e in range(2):
                    ACT(yT[:, 8 + 2 * gp + e, tok], pt[:, 256 + e * 128:256 + (e + 1) * 128], AF.Copy,
                        [ptr, 'pv'], ["y%d_%d" % (8 + 2 * gp + e, tt // 4)], scale=pv[:, 70:71])

            stages = [(T4, -4), (T3, -3), (T2, -2), (T1, -1), (S2, 5), (S1, 6), (SB, 0), (S5, 1), (S4, 2), (S3b, 3), (S3, 4)]
            for j in range(-6, no_ + 4):
                for fn_, off in stages:
                    ii = j + off
                    if 0 <= ii < no_:
                        fn_(ii)
            for e in range(2):
                gate_apply(5120 + (2 * gp + e) * 128, 8 + 2 * gp + e)
        wpre(('D', l), w_br[l, 0][:, 0:128], 4)
        barrier()
        if stop == (l, 'C'):
            raise _Stop()

        cv_ = Carve()
        mrg = cv_.get([128, 8, NT], BF16)
        macc = cv_.get([128, NT], F32)
        og = cv_.get([128, 8, 512], F32)
        xt = [cv_.get([128, 1024], F32) for _ in range(2)]
        if l + 1 < NL:
            mod_begin(l + 1)
            pmn, pmnr = nbank(7)
        for fc in range(8):
            for b in range(3):
                if l + 1 < NL:
                    mod_chunk(l + 1, fc * 3 + b, pmn, pmnr)
                swb = wload(w_br[l, b][:, fc * 128:(fc + 1) * 128], 4, tag=('D', l) if (fc == 0 and b == 0) else None)
                swg = wload(w_gate[l][:, b * 1024 + fc * 128:b * 1024 + (fc + 1) * 128])
                for g in range(NG):
                    zb, zr = nbank()
                    for kc in range(4):
                        MM(zb[:], wsl[:, swb, kc, :], yT[:, b * 4 + kc, g * 512:(g + 1) * 512], kc == 0, kc == 3,
                           ["ws%d" % swb, "y%d_%d" % (b * 4 + kc, g)], [zr])
                    gb, gr = proj(swg, g)
                    sg, sgr = ntf()
                    ACT(sg, gb[:], AF.Sigmoid, [gr, 'pv'], [sgr], bias=pv[:, 24 + b * 8 + fc:25 + b * 8 + fc])
                    mv = macc[:, g * 512:(g + 1) * 512]
                    mr = "macc%d" % g
                    if b == 0:
                        TT('dve', mv, zb[:], sg, ALU.mult, [zr, sgr], [mr])
                    else:
                        TT('dve', sg, zb[:], sg, ALU.mult, [zr, sgr], [sgr])
                        if b == 1:
                            TT('dve', mv, mv, sg, ALU.add, [mr, sgr], [mr])
                        else:
                            TT('dve', mrg[:, fc, g * 512:(g + 1) * 512], mv, sg, ALU.add, [mr, sgr], ["mrg%d" % g])
        if l + 1 < NL:
            mod_finish(l + 1, pmn, pmnr)
        for g in range(NG):
            s = 0 if g == 0 else 1
            for fc in range(8):
                sw = wload(w_out[l][:, fc * 128:(fc + 1) * 128])
                bank, br = nbank()
                for kc in range(8):
                    MM(bank[:], wsl[:, sw, kc, :], mrg[:, kc, g * 512:(g + 1) * 512], kc == 0, kc == 7, ["ws%d" % sw, "mrg%d" % g], [br])
                TS('dve', og[:, fc, :], bank[:], modT[:, 16 + fc, s:s + 1], None, ALU.mult, None, [br, modTr], ['og'])
            for j in range(4):
                tt = 4 * g + j
                b = tt % 2
                DMA('sp', xt[b], xcur[tt * 128:(tt + 1) * 128, :], ["%s%d" % (xcn, tt)], ["xo%d" % b], "xo%d" % b)
                for hh in range(2):
                    bank, br = nbank()
                    for c in range(4):
                        fc = hh * 4 + c
                        TR(bank[:, c * 128:(c + 1) * 128], og[:, fc, j * 128:(j + 1) * 128], identf, ['og', 'cstf'], [br])
                    TT('dve', xt[b][:, hh * 512:(hh + 1) * 512], bank[:], xt[b][:, hh * 512:(hh + 1) * 512], ALU.add,
                       [br, "xo%d" % b], ["xo%d" % b])
                DMA('sp', xnxt[tt * 128:(tt + 1) * 128, :], xt[b], ["xo%d" % b], ["%s%d" % (xnn, tt)], "xo%d" % b)

    try:
        for l_ in range(NL):
            layer(l_)
            if stop == (l_, 'D'):
                raise _Stop()
    except _Stop:
        barrier()
        dbg_h = nc.dram_tensor("dbg_h", [128, 8 * NT], BF16, kind="ExternalOutput").ap()
        dbg_y = nc.dram_tensor("dbg_y", [128, 12 * NT], BF16, kind="ExternalOutput").ap()
        DMA('sp', dbg_h, hT[:].rearrange("p a b -> p (a b)"), ['bar'], ['dbgh'], 'dbgh')
        DMA('sp', dbg_y, yT[:].rearrange("p a b -> p (a b)"), ['bar'], ['dbgy'], 'dbgy')
    S.emit(nc)
    es.close()
    return nc


def _consts():
    s = np.arange(128)[:, None]
    t = np.arange(128)[None, :]
    c = -1.0 / 16.0
    identf = np.eye(128, dtype=np.float32)
    Uf = np.where(s <= t, c, 0.0).astype(np.float32)
    Ub = np.where(s >= t, c, 0.0).astype(np.float32)
    SU = np.where(s > t, c, 0.0).astype(np.float32)
    SL = np.where(s < t, c, 0.0).astype(np.float32)
    Mf = (s <= t).astype(np.float32)
    Mb = (s >= t).astype(np.float32)
    bd = ((s // 64) == (t // 64)).astype(np.float32) / 64.0
    return np.concatenate([identf, Uf, Ub, SU, SL, Mf, Mb, bd], axis=1).astype(np.float32)


def _rope_tables():
    pos = np.arange(2048)
    n_f = 16
    inv = 10000.0 ** (-np.arange(n_f) / n_f)
    ang_r = (pos // 64)[:, None] * inv
    ang_c = (pos % 64)[:, None] * inv
    cos = np.zeros((64, 2048), np.float32)
    sin = np.zeros((64, 2048), np.float32)
    for i in range(64):
        ang = ang_r if i < 32 else ang_c
        j = i % 16
        a_part = (i % 32) < 16
        cos[i] = np.cos(ang[:, j].astype(np.float32))
        sn = np.sin(ang[:, j].astype(np.float32))
        sin[i] = -sn if a_part else sn
    return np.stack([np.tile(cos, (2, 1)), np.tile(sin, (2, 1))]).astype(np.float32)


def _swap_perm():
    i = np.arange(64)
    return np.where((i % 32) < 16, i + 16, i - 16)


def _b2_table(rpb, interior=False):
    L = rpb.shape[0]
    pad = np.concatenate([rpb.reshape(L, 8, -1), np.full((L, 8, 1), NEG, np.float32)], axis=2)
    p = np.arange(128)
    ph = (p // 64)[:, None, None]
    kc = (p % 64)[:, None, None]
    u = np.arange(16)[None, :, None]
    qc = np.arange(64)[None, None, :]
    if interior:
        dr = 7 - u + ph
        lo, hi = -4, 3
    else:
        dr = u - 8 + ph
        lo, hi = -7, 7
    csq = np.clip(qc - 8, 0, 48)
    valid = (dr >= lo) & (dr <= hi) & (kc >= csq) & (kc < csq + 16)
    dc = np.clip(kc - qc + 15, 0, 30)
    idx = np.where(valid, (np.clip(dr, -7, 7) + 7) * 31 + dc, 15 * 31)
    tab = pad[:, :, idx]
    return np.ascontiguousarray(tab.transpose(0, 2, 1, 3, 4)).reshape(L, 128, 8 * 1024).astype(np.float32)


_NC_CACHE = {}


def kernel(x_prompt, x_sample, c, cache_k, cache_v, state_fwd, state_bwd, c_ctx,
           norm_w, w_ada, b_ada, w_in, conv_w, q_norm_w, k_norm_w, rpb,
           w_alpha, b_alpha, gla_norm_w, w_branch, w_gate, b_gate, w_out, _ret_maps=False):
    f = lambda a: np.ascontiguousarray(np.asarray(a, dtype=np.float32))
    x_prompt, x_sample, c, cache_k, cache_v = map(f, (x_prompt, x_sample, c, cache_k, cache_v))
    state_fwd, state_bwd, c_ctx, norm_w, w_ada, b_ada, w_in = map(f, (state_fwd, state_bwd, c_ctx, norm_w, w_ada, b_ada, w_in))
    conv_w, q_norm_w, k_norm_w, rpb, w_alpha, b_alpha = map(f, (conv_w, q_norm_w, k_norm_w, rpb, w_alpha, b_alpha))
    gla_norm_w, w_branch, w_gate, b_gate, w_out = map(f, (gla_norm_w, w_branch, w_gate, b_gate, w_out))

    perm = _swap_perm()
    qcols = np.concatenate([4096 + h * 64 + perm for h in range(4)])
    kcols = np.concatenate([4352 + h * 64 + perm for h in range(4)])
    w_in_ext = np.ascontiguousarray(np.concatenate([w_in, w_in[:, :, qcols], w_in[:, :, kcols]], axis=2))
    pvec = np.zeros((NL, 128, 72), np.float32)
    pvec[:, :, 0:24] = b_ada.reshape(NL, 24, 128).transpose(0, 2, 1)
    pvec[:, :, 24:48] = b_gate.reshape(NL, 24, 128).transpose(0, 2, 1)
    pvec[:, :, 48:56] = norm_w.reshape(NL, 8, 128).transpose(0, 2, 1)
    pvec[:, :, 56:68] = conv_w.reshape(NL, 3, 4, 128).transpose(0, 3, 2, 1).reshape(NL, 128, 12)
    pvec[:, :, 68] = np.tile(q_norm_w, (1, 2))
    pvec[:, :, 69] = np.tile(k_norm_w, (1, 2))
    pvec[:, :, 70] = gla_norm_w
    walpha = np.zeros((NL, 33, 2, 2, 128), np.float32)
    for d in range(2):
        wa = w_alpha[:, d].reshape(NL, 16, 2, 128)
        walpha[:, d * 16:(d + 1) * 16, :, d, :] = wa
        walpha[:, 32, :, d, :] = b_alpha[:, d].reshape(NL, 2, 128)
    walpha = walpha.reshape(NL, 33, 512)
    b2tab = _b2_table(rpb)
    tfull = _b2_table(rpb, interior=True).reshape(NL, 128, 8, 16, 64)
    b2tabi = np.ascontiguousarray(np.stack(
        [np.concatenate([tfull[:, :, :, 11 - 2 * t, :], tfull[:, :, :, 12 - 2 * t, :]], axis=-1) for t in range(5)],
        axis=3)).reshape(NL, 128, 8 * 640)
    cst = _consts()
    rope = _rope_tables()

    in_maps = []
    for i in range(8):
        xin = np.ascontiguousarray(np.concatenate([x_prompt[2 * i], x_prompt[2 * i + 1], x_sample[i]], axis=0))
        cond = np.zeros((128, 8, 2), np.float32)
        cond[:, :, 0] = c_ctx.reshape(8, 128).T
        cond[:, :, 1] = c[i].reshape(8, 128).T
        in_maps.append(dict(
            xin=xin, w_in=w_in_ext, w_ada=w_ada, w_gate=w_gate, w_br=w_branch, w_out=w_out,
            cond=np.ascontiguousarray(cond.reshape(128, 16)), pvec=pvec, walpha=walpha, b2tab=b2tab, b2tabi=b2tabi,
            ck=np.ascontiguousarray(cache_k[i]), cv=np.ascontiguousarray(cache_v[i]),
            sf=np.ascontiguousarray(state_fwd[i]), sb=np.ascontiguousarray(state_bwd[i]),
            cst=cst, rope=rope))
    if _ret_maps:
        return in_maps
    if 'nc' not in _NC_CACHE:
        _NC_CACHE['nc'] = build_nc()
    nc = _NC_CACHE['nc']
    res = run_bass_kernel_spmd(nc, in_maps, core_ids=list(range(8)))
    rs = res.results
    y_p = np.zeros((16, 256, D), np.float32)
    y_s = np.zeros((8, 2048, D), np.float32)
    nk = np.zeros((16, NL, 8, 256, 64), np.float32)
    nv = np.zeros((16, NL, 8, 256, 64), np.float32)
    nsf = np.zeros((16, NL, 4, 64, 128), np.float32)
    nsb = np.zeros((16, NL, 4, 64, 128), np.float32)
    for i in range(8):
        y = rs[i]["y"]
        y_p[2 * i] = y[0:256]
        y_p[2 * i + 1] = y[256:512]
        y_s[i] = y[512:]
        nk[2 * i:2 * i + 2] = rs[i]["nk"]
        nv[2 * i:2 * i + 2] = rs[i]["nv"]
        nsf[2 * i:2 * i + 2] = rs[i]["nsf"]
        nsb[2 * i:2 * i + 2] = rs[i]["nsb"]
    return (y_p, y_s, nk, nv, nsf, nsb)
```

```python
import numpy as np
from contextlib import ExitStack
import concourse.bass as bass
import concourse.mybir as mybir
from concourse.bass_utils import run_bass_kernel_spmd

F32 = mybir.dt.float32
BF16 = mybir.dt.bfloat16
AF = mybir.ActivationFunctionType
ALU = mybir.AluOpType

NL = 4
D = 1024
NT = 2560
NTT = 20
NG = 5
WEXT = 6176
EPS = 1e-6
NEG = -1e30
ENG = ['pe', 'act', 'dve', 'pool', 'sp']


class Sch:
    def __init__(self):
        self.ops = {e: [] for e in ENG}
        self.lastw = {}
        self.readers = {}
        self.dma_count = {}
        self.regions = set()
        self.barrier_op = None

    def add(self, eng, fn, R=(), W=(), dma_key=None):
        op = dict(eng=eng, fn=fn, deps=[], signal=False, dma_key=dma_key, dma_ord=0)
        if dma_key is not None:
            self.dma_count[dma_key] = self.dma_count.get(dma_key, 0) + 1
            op['dma_ord'] = self.dma_count[dma_key]
        deps = {}

        def dep(o, raw):
            if o is None:
                return
            if o['dma_key'] is None and o['eng'] == eng and dma_key is None:
                if eng == 'pe':
                    return
            deps[id(o)] = o

        for r in R:
            self.regions.add(r)
            dep(self.lastw.get(r, self.barrier_op), True)
            if r.startswith('pb') or r.startswith('pt'):
                for o in self.readers.get(r, {}).values():
                    if o['eng'] != eng:
                        dep(o, True)
        for w in W:
            self.regions.add(w)
            dep(self.lastw.get(w, self.barrier_op), False)
            for o in self.readers.get(w, {}).values():
                dep(o, False)
        for r in R:
            self.readers.setdefault(r, {})[(eng, dma_key)] = op
        for w in W:
            self.lastw[w] = op
            self.readers[w] = {}
        op['deps'] = list(deps.values())
        for o in op['deps']:
            if o['dma_key'] is None:
                o['signal'] = True
        self.ops[eng].append(op)
        return op

    def emit(self, nc):
        for e in ENG:
            c = 0
            for op in self.ops[e]:
                if op['signal']:
                    c += 1
                op['ord'] = c
        with ExitStack() as es:
            sems = {e: es.enter_context(nc.semaphore('s_' + e)) for e in ENG}
            dsems = {k: es.enter_context(nc.semaphore('d%d' % i)) for i, k in enumerate(self.dma_count)}
            block = es.enter_context(nc.Block())

            def run(e, h):
                seen = {}
                for op in self.ops[e]:
                    need = {}
                    for o in op['deps']:
                        if o['dma_key'] is not None:
                            key = ('d', o['dma_key'])
                            val = 16 * o['dma_ord']
                        else:
                            key = ('e', o['eng'])
                            val = o['ord']
                        need[key] = max(need.get(key, 0), val)
                    for key, val in need.items():
                        if seen.get(key, 0) >= val:
                            continue
                        seen[key] = val
                        h.wait_ge(dsems[key[1]] if key[0] == 'd' else sems[key[1]], val)
                    ins = op['fn'](h)
                    if op['dma_key'] is not None:
                        ins.then_inc(dsems[op['dma_key']], 16)
                    elif op['signal']:
                        ins.then_inc(sems[e], 1)
                if e == 'sp':
                    for k, c in self.dma_count.items():
                        h.wait_ge(dsems[k], 16 * c)

            @block.tensor
            def _(h):
                run('pe', h)

            @block.scalar
            def _(h):
                run('act', h)

            @block.vector
            def _(h):
                run('dve', h)

            @block.gpsimd
            def _(h):
                run('pool', h)

            @block.sync
            def _(h):
                run('sp', h)


class _Stop(Exception):
    pass


def build_nc(stop=None):
    nc = bass.Bass("TRN2", target_bir_lowering=False)
    S = Sch()

    def din(name, shape):
        return nc.dram_tensor(name, list(shape), F32, kind="ExternalInput").ap()

    def dout(name, shape):
        return nc.dram_tensor(name, list(shape), F32, kind="ExternalOutput").ap()

    xin = din("xin", [NT, D])
    w_in = din("w_in", [NL, D, WEXT])
    w_ada = din("w_ada", [NL, D, 3 * D])
    w_gate = din("w_gate", [NL, D, 3 * D])
    w_br = din("w_br", [NL, 3, 512, D])
    w_out = din("w_out", [NL, D, D])
    cond = din("cond", [128, 16])
    pvec = din("pvec", [NL, 128, 72])
    walpha = din("walpha", [NL, 33, 512])
    b2tab = din("b2tab", [NL, 128, 8 * 1024])
    b2tabi = din("b2tabi", [NL, 128, 8 * 640])
    ck = din("ck", [NL, 8, 512, 64])
    cv = din("cv", [NL, 8, 512, 64])
    sf = din("sf", [NL, 4, 64, 128])
    sb = din("sb", [NL, 4, 64, 128])
    cst = din("cst", [128, 8 * 128])
    rope = din("rope", [2, 128, 2048])
    yout = dout("y", [NT, D])
    nk = dout("nk", [2, NL, 8, 256, 64])
    nv = dout("nv", [2, NL, 8, 256, 64])
    nsf = dout("nsf", [2, NL, 4, 64, 128])
    nsb = dout("nsb", [2, NL, 4, 64, 128])
    xsA = nc.dram_tensor("xsA", [NT, D], F32).ap()
    xsB = nc.dram_tensor("xsB", [NT, D], F32).ap()

    ARENA = 74 * 1024
    es = ExitStack()
    sb_ = lambda n, sh, dt: es.enter_context(nc.sbuf_tensor(n, sh, dt))
    hT = sb_("hT", [128, 8, NT], BF16)
    yT = sb_("yT", [128, 12, NT], BF16)
    NWS = 6
    wsl = sb_("wsl", [128, NWS, 8, 128], BF16)
    arena = sb_("arena", [128, ARENA // 4], F32)
    cstf = sb_("cstf", [128, 5, 128], F32)
    cstb = sb_("cstb", [128, 5, 128], BF16)
    condt = sb_("condt", [128, 16], F32)
    scb = sb_("scb", [128, 16], BF16)
    pv = sb_("pv", [128, 72], F32)
    modTs = [sb_("modT%d" % i, [128, 24, 2], F32) for i in range(2)]
    Amods = [sb_("Amod%d" % i, [128, 8, 2], F32) for i in range(2)]
    pvn = sb_("pvn", [128, 32], F32)
    wq8 = sb_("wq8", [128, 1], F32)
    wal = sb_("wal", [33, 512], BF16)
    tf = sb_("tf", [128, 4, 512], F32)
    tb = sb_("tb", [128, 4, 512], BF16)
    sml = sb_("sml", [128, 64], F32)
    NPB = 8
    pbs = [es.enter_context(nc.psum_tensor("pb%d" % i, [128, 512], F32)) for i in range(NPB)]

    identf, Uf, Ub, SU, SL = [cstf[:, i, :] for i in range(5)]
    identb, Mf, Mb, bd64, ones1k = [cstb[:, i, :] for i in range(5)]
    Umat = [Uf, Ub]
    SUL = [SU, SL]
    Mmask = [Mf, Mb]

    cnt = dict(pb=0, pt=0, ws=0, tf=0, tb=0, P=0)

    def nbank(idx=None):
        if idx is None:
            idx = cnt['pb'] % 6
            cnt['pb'] += 1
        return pbs[idx][:], "pb%d" % idx

    def ntbank(idx=None):
        if idx is None:
            idx = 6 + cnt['pt'] % 2
            cnt['pt'] += 1
        return pbs[idx][:].bitcast(BF16), "pb%d" % idx

    def ntf():
        i = cnt['tf'] % 4
        cnt['tf'] += 1
        return tf[:, i, :], "tf%d" % i

    def ntb():
        i = cnt['tb'] % 4
        cnt['tb'] += 1
        return tb[:, i, :], "tb%d" % i

    def MM(out, lhsT, rhs, start, stop, R, W):
        S.add('pe', lambda h: h.matmul(out, lhsT, rhs, start=start, stop=stop), R, W)

    def TR(out, in_, ident, R, W):
        S.add('pe', lambda h: h.transpose(out, in_, ident), R, W)

    def ACT(out, in_, func, R, W, bias=None, scale=None, accum=None):
        kw = {}
        if bias is not None:
            kw['bias'] = bias
        if scale is not None:
            kw['scale'] = scale
        if accum is not None:
            kw['accum_out'] = accum
        S.add('act', lambda h: h.activation(out, in_, func, **kw), R, W)

    def engname(e):
        return e

    def TT(e, out, in0, in1, op, R, W):
        S.add(e, lambda h: h.tensor_tensor(out, in0, in1, op), R, W)

    def TS(e, out, in0, s1, s2, op0, op1, R, W):
        if s2 is None:
            S.add(e, lambda h: h.tensor_scalar(out, in0, s1, None, op0), R, W)
        else:
            S.add(e, lambda h: h.tensor_scalar(out, in0, s1, s2, op0, op1), R, W)

    def STT(out, in0, sc, in1, op0, op1, R, W):
        S.add('dve', lambda h: h.scalar_tensor_tensor(out, in0, sc, in1, op0, op1), R, W)

    def CP(e, out, in_, R, W):
        if e == 'act':
            S.add('act', lambda h: h.copy(out, in_), R, W)
        else:
            S.add(e, lambda h: h.tensor_copy(out, in_), R, W)

    def MSET(e, ap, val, W):
        S.add(e, lambda h: h.memset(ap, val), (), W)

    def RCP(out, in_, R, W):
        S.add('dve', lambda h: h.reciprocal(out, in_), R, W)

    def DMA(q, out, in_, R, W, key):
        S.add(q, lambda h: h.dma_start(out=out, in_=in_), R, W, dma_key=key)

    def barrier():
        regs = sorted(S.regions)
        MSET('pool', sml[:, 63:64], 0.0, regs + ['bar'])
        S.barrier_op = S.ops['pool'][-1]

    pre = {}

    def wpre(tag, src, nk_=8, ncols=128):
        pre[tag] = wload(src, nk_, ncols)

    def wload(src, nk_=8, ncols=128, tag=None):
        if tag is not None and tag in pre:
            return pre.pop(tag)
        i = cnt['ws'] % NWS
        cnt['ws'] += 1
        DMA('pool', wsl[:, i, 0:nk_, 0:ncols], src.rearrange("(kc p) n -> p kc n", p=128), (), ["ws%d" % i], "ws%d" % i)
        return i

    def hreg(g):
        return ["hT%d" % t for t in range(4 * g, 4 * g + 4)]

    def proj(slot, g, ncols=128):
        bank, br = nbank()
        for kc in range(8):
            MM(bank[0:ncols, :], wsl[:, slot, kc, 0:ncols], hT[:, kc, g * 512:(g + 1) * 512], kc == 0, kc == 7,
               ["ws%d" % slot] + hreg(g), [br])
        return bank, br

    class Carve:
        def __init__(self):
            self.off = 0

        def get(self, shape, dt):
            n = int(np.prod(shape[1:])) * (4 if dt == F32 else 2)
            n = (n + 31) // 32 * 32
            a = self.off // 4
            self.off += n
            assert self.off <= ARENA, ("arena overflow", self.off)
            v = arena[0:shape[0], a:a + n // 4]
            if dt == BF16:
                v = v.bitcast(BF16)
            ne = int(np.prod(shape[1:]))
            v = v[:, 0:ne]
            if len(shape) == 3:
                v = v.rearrange("p (a b) -> p a b", a=shape[1])
            elif len(shape) == 4:
                v = v.rearrange("p (a b c) -> p a b c", a=shape[1], b=shape[2])
            return v

    DMA('sp', cstf[:], cst[:, 0:640].rearrange("p (a b) -> p a b", a=5), (), ['cstf'], 'cstf')
    DMA('pool', cstb[:, 0, :], cst[:, 0:128], (), ['cstb'], 'cstb')
    DMA('pool', cstb[:, 1:4, :], cst[:, 640:1024].rearrange("p (a b) -> p a b", a=3), (), ['cstb'], 'cstb')
    MSET('pool', cstb[:, 4, :], 1.0 / 1024.0, ['cstb'])
    DMA('sp', condt[:], cond, (), ['condt'], 'condt')
    ACT(scb[:], condt[:], AF.Silu, ['condt'], ['scb'])

    xbufs = [(xin, 'xin'), (xsA, 'xsA'), (xsB, 'xsB'), (xsA, 'xsA'), (yout, 'yo')]
    if stop is not None:
        MSET('pool', yT[:].rearrange("p a b -> p (a b)"), 0.0, ['yTinit'])
    SEQS = [(0, 256, 0), (256, 256, 0), (512, 2048, 1)]

    def ucol(tok):
        return tok + 1 + 2 * (0 if tok < 256 else (1 if tok < 512 else 2))

    def mod_begin(lm):
        DMA('sp', pvn[:, 0:24], pvec[lm][:, 0:24], (), ['pvn'], 'pvn')
        DMA('sp', pvn[:, 24:32], pvec[lm][:, 48:56], (), ['pvn'], 'pvn')

    def mod_chunk(lm, j, pm, pmr):
        sl = wload(w_ada[lm][:, j * 128:(j + 1) * 128])
        for kc in range(8):
            MM(pm[:, 2 * j:2 * j + 2], wsl[:, sl, kc, :], scb[:, 2 * kc:2 * kc + 2], kc == 0, kc == 7,
               ["ws%d" % sl, 'scb'], [pmr])

    def mod_finish(lm, pm, pmr):
        mT, mr = modTs[lm % 2], "modT%d" % (lm % 2)
        aT, ar = Amods[lm % 2], "Amod%d" % (lm % 2)
        TT('dve', mT[:], pm[:, 0:48].rearrange("p (a b) -> p a b", b=2),
           pvn[:, 0:24].unsqueeze(2).broadcast_to([128, 24, 2]), ALU.add, [pmr, 'pvn'], [mr])
        TS('dve', aT[:], mT[:, 8:16, :], 1.0, None, ALU.add, None, [mr], [ar])
        TT('dve', aT[:], aT[:], pvn[:, 24:32].unsqueeze(2).broadcast_to([128, 8, 2]), ALU.mult, [ar, 'pvn'], [ar])

    mod_begin(0)
    pm0, pm0r = nbank(7)
    for j0 in range(24):
        mod_chunk(0, j0, pm0, pm0r)
    mod_finish(0, pm0, pm0r)

    def layer(l):
        modT, modTr = modTs[l % 2], "modT%d" % (l % 2)
        Amod, Amodr = Amods[l % 2], "Amod%d" % (l % 2)
        xcur, xcn = xbufs[l]
        xnxt, xnn = xbufs[l + 1]
        barrier()
        DMA('sp', pv[:], pvec[l], (), ['pv'], 'pv')
        DMA('pool', wal[:], walpha[l], (), ['wal'], 'wal')
        TS('dve', wq8[:], pv[:, 68:69], 0.125, None, ALU.mult, None, ['pv'], ['wq8'])

        cv_ = Carve()
        xt = [cv_.get([128, 1024], F32) for _ in range(4)]
        xn = [cv_.get([128, 1024], BF16) for _ in range(3)]
        sqj = cv_.get([128, 1024], BF16)

        def X1(tt):
            b = tt % 4
            DMA('sp', xt[b], xcur[tt * 128:(tt + 1) * 128, :], ["%s%d" % (xcn, tt)], ["xt%d" % b], "xt%d" % b)

        def X2(tt):
            b = tt % 4
            ACT(sqj, xt[b], AF.Square, ["xt%d" % b], ['sqj', "ss%d" % b], accum=sml[:, b:b + 1])
            ACT(sml[:, 4 + b:5 + b], sml[:, b:b + 1], AF.Ln, ["ss%d" % b], ["ln%d" % b], bias=EPS, scale=1.0 / D)
            ACT(sml[:, 8 + b:9 + b], sml[:, 4 + b:5 + b], AF.Exp, ["ln%d" % b], ["rs%d" % b], scale=-0.5)

        def X3(tt):
            b = tt % 4
            n3 = tt % 3
            TS('dve', xn[n3], xt[b], sml[:, 8 + b:9 + b], None, ALU.mult, None, ["xt%d" % b, "rs%d" % b], ["xn%d" % n3])
            pt, ptr = ntbank(6 + tt % 2)
            for c in range(8):
                TR(pt[:, c * 128:(c + 1) * 128], xn[n3][:, c * 128:(c + 1) * 128], identb, ["xn%d" % n3, 'cstb'], [ptr])

        def X4(tt):
            s = 0 if tt < 4 else 1
            pt, ptr = ntbank(6 + tt % 2)
            for c in range(8):
                o = hT[:, c, tt * 128:(tt + 1) * 128]
                i = pt[:, c * 128:(c + 1) * 128]
                if tt % 4 == 0:
                    ACT(o, i, AF.Identity, [ptr, Amodr, modTr], ["hT%d" % tt], bias=modT[:, c, s:s + 1],
                        scale=Amod[:, c, s:s + 1])
                else:
                    TS('dve', o, i, Amod[:, c, s:s + 1], modT[:, c, s:s + 1], ALU.mult, ALU.add,
                       [ptr, Amodr, modTr], ["hT%d" % tt])

        for j in range(-2, NTT + 1):
            for fn_, off in ((X4, -1), (X3, 0), (X2, 1), (X1, 2)):
                if 0 <= j + off < NTT:
                    fn_(j + off)
        wpre(('A', l), w_in[l][:, 0:128])
        barrier()
        if stop == (l, 'X'):
            raise _Stop()

        def gate_apply(col0, ych):
            sl = wload(w_in[l][:, col0:col0 + 128])
            for g in range(NG):
                bank, br = proj(sl, g)
                t, tr = ntb()
                ACT(t, bank[:], AF.Silu, [br], [tr])
                yv = yT[:, ych, g * 512:(g + 1) * 512]
                TT('dve', yv, yv, t, ALU.mult, [tr, "y%d_%d" % (ych, g)], ["y%d_%d" % (ych, g)])

        cv_ = Carve()
        ubuf = cv_.get([128, NT + 6], F32)
        MSET('pool', ubuf, 0.0, ["u%d" % g for g in range(NG)])
        for cc in range(4):
            s_xa = wload(w_in[l][:, cc * 128:(cc + 1) * 128], tag=('A', l) if cc == 0 else None)
            s_ca = wload(w_in[l][:, 1024 + cc * 128:1024 + (cc + 1) * 128])
            for g in range(NG):
                b1, r1 = proj(s_xa, g)
                b2, r2 = proj(s_ca, g)
                t, tr = ntf()
                CP('act', t, b1[:], [r1], [tr])
                if g == 0:
                    for hh in range(2):
                        TT('dve', ubuf[:, ucol(hh * 256):ucol(hh * 256) + 256], b2[:, hh * 256:(hh + 1) * 256],
                           t[:, hh * 256:(hh + 1) * 256], ALU.mult, [r2, tr], ["u0"])
                else:
                    TT('dve', ubuf[:, ucol(g * 512):ucol(g * 512) + 512], b2[:], t, ALU.mult, [r2, tr], ["u%d" % g])
            s_ba = wload(w_in[l][:, 512 + cc * 128:512 + (cc + 1) * 128])
            for g in range(NG):
                b3, r3 = proj(s_ba, g)
                t, tr = ntf()
                segs = [(0, 256), (256, 256)] if g == 0 else [(g * 512, 512)]
                ur = ["u%d" % gg for gg in range(max(0, g - 1), min(NG, g + 2))]
                for (t0, n) in segs:
                    u0 = ucol(t0)
                    o = t[:, t0 - g * 512:t0 - g * 512 + n]
                    TS('dve', o, ubuf[:, u0 - 1:u0 - 1 + n], pv[:, 56 + cc * 3:57 + cc * 3], None, ALU.mult, None,
                       ur + ['pv'], [tr])
                    STT(o, ubuf[:, u0:u0 + n], pv[:, 57 + cc * 3:58 + cc * 3], o, ALU.mult, ALU.add, ur + ['pv', tr], [tr])
                    STT(o, ubuf[:, u0 + 1:u0 + 1 + n], pv[:, 58 + cc * 3:59 + cc * 3], o, ALU.mult, ALU.add,
                        ur + ['pv', tr], [tr])
                TT('dve', yT[:, cc, g * 512:(g + 1) * 512], b3[:], t, ALU.mult, [r3, tr], ["y%d_%d" % (cc, g)])
            gate_apply(1536 + cc * 128, cc)
        wpre(('B', l), w_in[l][:, 2048:2048 + 128])
        barrier()
        if stop == (l, 'A'):
            raise _Stop()

        cv_ = Carve()
        qT = cv_.get([128, NT], BF16)
        kT = cv_.get([128, NT], BF16)
        vaug = cv_.get([128, NTT, 2, 65], BF16)
        vodd = cv_.get([128, 15, 2, 65], BF16)
        kcfs = [cv_.get([128, 4, 2, 64], F32) for _ in range(2)]
        kctxTs = [cv_.get([128, 512], BF16) for _ in range(2)]
        vctxs = [cv_.get([128, 4, 2, 65], BF16) for _ in range(2)]
        B2s = [cv_.get([128, 2, 16, 64], BF16) for _ in range(2)]
        B2is = [cv_.get([128, 2, 640], BF16) for _ in range(2)]
        Pbig = [cv_.get([128, 1152], BF16) for _ in range(6)]
        onb = [cv_.get([128, 128], BF16) for _ in range(2)]
        kst = cv_.get([128, 4, 128], F32)
        knb = cv_.get([128, 512], F32)
        vst = cv_.get([128, 4, 128], F32)
        rcb = [cv_.get([128, 2], F32) for _ in range(2)]
        for i2 in range(2):
            MSET('pool', vctxs[i2][:, :, :, 64:65], 1.0, ["vctx%d" % i2])

        def na_prefetch(hp_):
            i2 = hp_ % 2
            for e in range(2):
                DMA('sp', kcfs[i2][:, :, e, :], ck[l, 2 * hp_ + e].rearrange("(kt p) d -> p kt d", p=128), (), ["kcf%d" % i2],
                    'kcf%d_%d' % (i2, e))
                DMA('pool', vctxs[i2][:, :, e, 0:64], cv[l, 2 * hp_ + e].rearrange("(kt p) d -> p kt d", p=128), (),
                    ["vctx%d" % i2], 'vctx%d_%d' % (i2, e))
            DMA('pool', B2s[i2][:], b2tab[l][:, 2 * hp_ * 1024:(2 * hp_ + 2) * 1024].rearrange("p (e u q) -> p e u q", e=2, u=16),
                (), ["B2%d" % i2], 'B2%d' % i2)
            DMA('pool', B2is[i2][:], b2tabi[l][:, 2 * hp_ * 640:(2 * hp_ + 2) * 640].rearrange("p (e q) -> p e q", e=2),
                (), ["B2i%d" % i2], 'B2i%d' % i2)
            v2 = B2s[i2][:].rearrange("p e u q -> p (e u q)")
            ACT(v2, v2, AF.Exp, ["B2%d" % i2], ["B2%d" % i2])
            v2 = B2is[i2][:].rearrange("p e q -> p (e q)")
            ACT(v2, v2, AF.Exp, ["B2i%d" % i2], ["B2i%d" % i2])

        na_prefetch(0)
        for hp in range(4):
            i2 = hp % 2
            kcf, kctxT, vctx, B2, B2i = kcfs[i2], kctxTs[i2], vctxs[i2], B2s[i2], B2is[i2]
            rkc, rvc, rb2, rb2i = "kctxT%d" % i2, "vctx%d" % i2, "B2%d" % i2, "B2i%d" % i2
            MSET('pool', vaug[:, :, :, 64:65], 1.0, ['vaug'])
            bk, bkr = nbank()
            for kt in range(4):
                TR(bk[:, kt * 128:(kt + 1) * 128], kcf[:, kt].rearrange("p e d -> p (e d)"), identf, ["kcf%d" % i2, 'cstf'], [bkr])
            CP('dve', kctxT, bk[:], [bkr], [rkc])
            for which in range(2):
                sl = wload(w_in[l][:, 2048 + which * 512 + hp * 128:2048 + which * 512 + (hp + 1) * 128],
                           tag=('B', l) if (which == 0 and hp == 0) else None)
                dst = qT if which == 0 else kT
                dn = 'qT' if which == 0 else 'kT'
                wcol = wq8[:, 0:1] if which == 0 else pv[:, 69:70]
                def qk1(g):
                    bank, br = proj(sl, g)
                    sq, sqr = ntb()
                    ACT(sq, bank[:], AF.Square, [br], [sqr])
                    f, fr = ntf()
                    CP('dve', f, bank[:], [br], [fr])
                    return (f, fr, sq, sqr)

                def qk2(g, st_):
                    f, fr, sq, sqr = st_
                    b2_, b2r = nbank()
                    MM(b2_[:], bd64, sq, True, True, [sqr, 'cstb'], [b2r])
                    rs, rsr = ntf()
                    ACT(rs, b2_[:], AF.Ln, [b2r], [rsr], bias=EPS)
                    ACT(rs, rs, AF.Exp, [rsr], [rsr], scale=-0.5)
                    STT(dst[:, g * 512:(g + 1) * 512], f, wcol, rs, ALU.mult, ALU.mult, [fr, rsr, 'pv', 'wq8'],
                        [dn])
                    if which == 1 and g == 0:
                        kn, knr = knb, 'knb'
                        STT(kn, f, wcol, rs, ALU.mult, ALU.mult, [fr, rsr, 'pv'], [knr])
                        b3, b3r = nbank()
                        for j in range(4):
                            TR(b3[:, j * 128:(j + 1) * 128], kn[:, j * 128:(j + 1) * 128], identf, [knr, 'cstf'], [b3r])
                        CP('dve', kst[:].rearrange("p a b -> p (a b)"), b3[:], [b3r], ['kst'])
                        for j in range(4):
                            for e in range(2):
                                DMA('sp', nk[j // 2, l, 2 * hp + e, (j % 2) * 128:(j % 2) * 128 + 128, :],
                                    kst[:, j, e * 64:(e + 1) * 64], ['kst'], ['nk'], 'kst')

                st_ = qk1(0)
                for g in range(NG):
                    nx_ = qk1(g + 1) if g + 1 < NG else None
                    qk2(g, st_)
                    st_ = nx_
            sl = wload(w_in[l][:, 3072 + hp * 128:3072 + (hp + 1) * 128])
            for g in range(NG):
                bank, br = nbank()
                for j in range(4):
                    tt = 4 * g + j
                    for kc in range(8):
                        MM(bank[:, j * 128:(j + 1) * 128], hT[:, kc, tt * 128:(tt + 1) * 128], wsl[:, sl, kc, :], kc == 0, kc == 7,
                           ["ws%d" % sl] + hreg(g), [br])
                CP('act', vaug[:, 4 * g:4 * g + 4, :, 0:64], bank[:].rearrange("p (a e d) -> p a e d", a=4, e=2), [br], ['vaug'])
                if g == 0:
                    CP('dve', vst[:].rearrange("p a b -> p (a b)"), bank[:], [br], ['vst'])
                    for j in range(4):
                        for e in range(2):
                            DMA('sp', nv[j // 2, l, 2 * hp + e, (j % 2) * 128:(j % 2) * 128 + 128, :],
                                vst[:, j, e * 64:(e + 1) * 64], ['vst'], ['nv'], 'vst')

            def att_p1(q0, nq, tiles_fn):
                st = []
                tpb = 512 // nq
                for e in range(2):
                    tiles = tiles_fn(e)
                    nt_ = len(tiles)
                    banks = [nbank() for _ in range((nt_ + tpb - 1) // tpb)]
                    qa = qT[e * 64:(e + 1) * 64, q0:q0 + nq]
                    for t, (ka, va, ba) in enumerate(tiles):
                        bs, bsr = banks[t // tpb]
                        c0 = (t % tpb) * nq
                        MM(bs[:, c0:c0 + nq], ka, qa, True, True, ['qT', 'kT', rkc], [bsr])
                    k_ = cnt['P'] % 6
                    cnt['P'] += 1
                    p_, pr = Pbig[k_], "Pb%d" % k_
                    for j, (bs, bsr) in enumerate(banks):
                        ncol = min(tpb, nt_ - j * tpb) * nq
                        ACT(p_[:, j * 512:j * 512 + ncol], bs[:, 0:ncol], AF.Exp, [bsr], [pr])
                    if nq == 128 and tiles[0][2] is not None:
                        TT('dve', p_[:, 0:640], p_[:, 0:640], B2i[:, e, :], ALU.mult, [pr, rb2i], [pr])
                    else:
                        for t, (ka, va, ba) in enumerate(tiles):
                            if ba is not None:
                                TT('dve', p_[:, t * nq:(t + 1) * nq], p_[:, t * nq:(t + 1) * nq], ba, ALU.mult, [pr, rb2], [pr])
                    st.append((tiles, p_, pr))
                return (q0, nq, st)

            def att_p2(state, bi):
                q0, nq, st = state
                bo, bor = nbank(6)
                on = onb[bi]
                onr = "on%d" % bi
                rc = rcb[bi]
                rcr = "rc%d" % bi
                for e in range(2):
                    tiles, p_, pr = st[e]
                    nt_ = len(tiles)
                    for t, (ka, va, ba) in enumerate(tiles):
                        MM(bo[0:nq, e * 65:(e + 1) * 65], p_[:, t * nq:(t + 1) * nq], va, t == 0, t == nt_ - 1,
                           [pr, 'vaug', rvc], [bor])
                for e in range(2):
                    RCP(rc[0:nq, e:e + 1], bo[0:nq, e * 65 + 64:e * 65 + 65], [bor], ["%s_%d" % (rcr, e)])
                    TS('dve', on[0:nq, e * 64:(e + 1) * 64], bo[0:nq, e * 65:e * 65 + 64], rc[0:nq, e:e + 1], None, ALU.mult, None,
                       [bor, "%s_%d" % (rcr, e)], ["%s_%d" % (onr, e)])
                return (q0, nq, bi)

            def att_p3(st3):
                q0, nq, bi = st3
                on = onb[bi]
                onr = "on%d" % bi
                pt, ptr = ntbank(7)
                TR(pt[:, 0:nq], on[0:nq, :], identb[0:nq, 0:nq], [onr + "_0", onr + "_1", 'cstb'], [ptr])
                CP('dve', yT[:, 4 + hp, q0:q0 + nq], pt[:, 0:nq], [ptr], ["y%d_%d" % (4 + hp, q0 // 512)])

            blocks = []
            for s in range(2):
                for qb in range(2):
                    def tf_(e, s=s):
                        return [(kT[e * 64:(e + 1) * 64, s * 256 + kt * 128:s * 256 + (kt + 1) * 128], vaug[:, s * 2 + kt, e, :], None)
                                for kt in range(2)]
                    blocks.append((s * 256 + qb * 128, 128, tf_))

            def ctx_tiles(e):
                return [(kctxT[e * 64:(e + 1) * 64, t * 128:(t + 1) * 128], vctx[:, t, e, :], None) for t in range(4)]

            def single_row(r):
                rs_ = min(max(r - 4, 0), 24)

                def tf_(e):
                    tl = []
                    for t in range(4):
                        kr0 = rs_ + 2 * t
                        tl.append((kT[e * 64:(e + 1) * 64, 512 + kr0 * 64:512 + kr0 * 64 + 128], vaug[:, 4 + kr0 // 2, e, :],
                                   B2[:, e, kr0 - r + 8, :]))
                    return tl + ctx_tiles(e)
                return (512 + r * 64, 64, tf_)

            def pair_rows(r):
                def tf_(e):
                    tl = []
                    for t in range(5):
                        kr0 = r - 4 + 2 * t
                        w = 11 - 2 * t
                        tl.append((kT[e * 64:(e + 1) * 64, 512 + kr0 * 64:512 + kr0 * 64 + 128], vaug[:, 4 + kr0 // 2, e, :],
                                   B2i[:, e, t * 128:(t + 1) * 128]))
                    return tl + ctx_tiles(e)
                return (512 + r * 64, 128, tf_)

            for r in range(4):
                blocks.append(single_row(r))
            for r in range(4, 28, 2):
                blocks.append(pair_rows(r))
            for r in range(28, 32):
                blocks.append(single_row(r))
            if hp + 1 < 4:
                na_prefetch(hp + 1)
            stq = [att_p1(*blocks[0]), att_p1(*blocks[1])]
            prev3 = None
            for i in range(len(blocks)):
                if i + 2 < len(blocks):
                    stq.append(att_p1(*blocks[i + 2]))
                cur3 = att_p2(stq.pop(0), i % 2)
                if prev3 is not None:
                    att_p3(prev3)
                prev3 = cur3
            att_p3(prev3)
            gate_apply(2048 + 1536 + hp * 128, 4 + hp)
        wpre(('C', l), w_in[l][:, 4096:4096 + 128])
        barrier()
        if stop == (l, 'B'):
            raise _Stop()

        cv_ = Carve()
        qrT = cv_.get([128, NT], BF16)
        krT = cv_.get([128, NT], BF16)
        vtok = cv_.get([128, NTT, 256], BF16)
        ofb = cv_.get([128, NTT, 256], BF16)
        lrT = cv_.get([33, NT], BF16)
        rcs = [cv_.get([128, 2, 512], BF16) for _ in range(2)]
        DP = 6
        spb = [cv_.get([128, 128], F32) for _ in range(DP)]
        epb = [cv_.get([128, 128], F32) for _ in range(DP)]
        emb = [cv_.get([128, 128], F32) for _ in range(DP)]
        qtb = [cv_.get([128, 128], BF16) for _ in range(DP)]
        ktb = [cv_.get([128, 128], BF16) for _ in range(DP)]
        kdb = [cv_.get([128, 128], BF16) for _ in range(DP)]
        khb = [cv_.get([128, 128], BF16) for _ in range(DP)]
        Ab = [cv_.get([128, 256], BF16) for _ in range(DP)]
        Sm = cv_.get([128, 128], F32)
        Sbfs = [cv_.get([128, 128], BF16) for _ in range(2)]
        osb = [cv_.get([128, 256], F32) for _ in range(DP)]
        onb2 = [cv_.get([128, 256], BF16) for _ in range(DP)]
        junkb = cv_.get([128, 128], BF16)
        MSET('pool', lrT[32:33, :], 1.0, ['lrT'])
        for gp in range(2):
            cols = [4096 + gp * 128, 4352 + gp * 128, 5664 + gp * 128, 5920 + gp * 128]
            for which in range(2):
                dst = qrT if which == 0 else krT
                dn = 'qr' if which == 0 else 'kr'
                s1 = wload(w_in[l][:, cols[which]:cols[which] + 128], tag=('C', l) if (which == 0 and gp == 0) else None)
                s2 = wload(w_in[l][:, cols[2 + which]:cols[2 + which] + 128])
                for g in range(NG):
                    b1, r1 = proj(s1, g)
                    if g == 0:
                        CP('act', dst[:, 0:512], b1[:], [r1], ["%s0" % dn, dn + 'all'])
                        continue
                    b2_, r2 = proj(s2, g)
                    rb = g % 2
                    DMA('pool', rcs[rb], rope[:, :, (g - 1) * 512:g * 512].rearrange("a p n -> p a n"), (), ["rcs%d" % rb],
                        "rcs%d" % rb)
                    t1, t1r = ntf()
                    t2, t2r = ntf()
                    TT('dve', t1, b1[:], rcs[rb][:, 0, :], ALU.mult, [r1, "rcs%d" % rb], [t1r])
                    TT('dve', t2, b2_[:], rcs[rb][:, 1, :], ALU.mult, [r2, "rcs%d" % rb], [t2r])
                    TT('dve', dst[:, g * 512:(g + 1) * 512], t1, t2, ALU.add, [t1r, t2r], ["%s%d" % (dn, g), dn + 'all'])
            if gp == 0:
                sl = wload(w_in[l][:, 5632:5664], 8, 32)
                for g in range(NG):
                    bank, br = proj(sl, g, 32)
                    CP('act', lrT[0:32, g * 512:(g + 1) * 512], bank[0:32, :], [br], ['lrT'])
            sv = [wload(w_in[l][:, 4608 + gp * 256 + i * 128:4608 + gp * 256 + (i + 1) * 128]) for i in range(2)]
            for i in range(2):
                for g in range(NG):
                    bank, br = proj(sv[i], g)
                    t, tr = ntb()
                    CP('act', t, bank[:], [br], [tr])
                    pt, ptr = ntbank()
                    for j in range(4):
                        TR(pt[:, j * 128:(j + 1) * 128], t[:, j * 128:(j + 1) * 128], identb, [tr, 'cstb'], [ptr])
                    CP('dve', vtok[:, 4 * g:4 * g + 4, i * 128:(i + 1) * 128], pt[:, 0:512].rearrange("p (a b) -> p a b", a=4),
                       [ptr], ['vtok'])
            MSET('pool', sml[:, 61:62], 0.0, ['qrall', 'krall'] + ["qr%d" % g for g in range(NG)] + ["kr%d" % g for g in range(NG)])

            order = []
            for d in range(2):
                for si, (t0, ln, lat) in enumerate(SEQS):
                    tiles = list(range(t0 // 128, (t0 + ln) // 128))
                    if d == 1:
                        tiles = tiles[::-1]
                    for j, tt in enumerate(tiles):
                        order.append((d, si, lat, tt, j == 0, j == len(tiles) - 1))
            no_ = len(order)

            def S1(ii):
                d, si, lat, tt, first, last = order[ii]
                b = ii % DP
                tok = slice(tt * 128, (tt + 1) * 128)
                Z, Zr = nbank(ii % 2)
                MM(Z[:, 256:384], lrT[0:33, tok], wal[0:33, gp * 256 + d * 128:gp * 256 + (d + 1) * 128], True, True,
                   ['lrT', 'wal'], [Zr])
                sp, spr = spb[b], "sp%d" % b
                ACT(sp, Z[:, 256:384], AF.Exp, [Zr], [spr], scale=-1.0)
                ACT(sp, sp, AF.Ln, [spr], [spr], bias=1.0)

            def S2(ii):
                d, si, lat, tt, first, last = order[ii]
                b = ii % DP
                Z, Zr = nbank(ii % 2)
                sp, spr = spb[b], "sp%d" % b
                MM(Z[:, 0:128], sp, Umat[d], True, True, [spr, 'cstf'], [Zr])
                ACT(epb[b], Z[:, 0:128], AF.Exp, [Zr], ["ep%d" % b])
                ACT(emb[b], Z[:, 0:128], AF.Exp, [Zr], ["em%d" % b], scale=-1.0)

            def S3(ii):
                d, si, lat, tt, first, last = order[ii]
                b = ii % DP
                tok = slice(tt * 128, (tt + 1) * 128)
                ep, epr, em, emr = epb[b], "ep%d" % b, emb[b], "em%d" % b
                qt, qtr, kt, ktr, kd, kdr = qtb[b], "qt%d" % b, ktb[b], "kt%d" % b, kdb[b], "kd%d" % b
                dc = 127 if d == 0 else 0
                STT(qt, qrT[:, tok], 0.125, ep, ALU.mult, ALU.mult, ['qrall', epr], [qtr])
                TT('dve', kt, krT[:, tok], em, ALU.mult, ['krall', emr], [ktr])
                TS('dve', kd, kt, ep[:, dc:dc + 1], None, ALU.mult, None, [ktr, epr], [kdr])

            def S3b(ii):
                d, si, lat, tt, first, last = order[ii]
                b = ii % DP
                qt, qtr, kt, ktr, kd, kdr = qtb[b], "qt%d" % b, ktb[b], "kt%d" % b, kdb[b], "kd%d" % b
                pt, ptr = ntbank(7)
                TR(pt[:, 0:128], kd, identb, [kdr, 'cstb'], [ptr])
                for e in range(2):
                    ba, bar = nbank(2 + e)
                    MM(ba[:, 0:128], kt[e * 64:(e + 1) * 64, :], qt[e * 64:(e + 1) * 64, :], True, True, [ktr, qtr], [bar])

            def S4(ii):
                d, si, lat, tt, first, last = order[ii]
                b = ii % DP
                pt, ptr = ntbank(7)
                CP('act', khb[b], pt[:, 0:128], [ptr], ["kh%d" % b])
                for e in range(2):
                    ba, bar = nbank(2 + e)
                    TT('dve', Ab[b][:, e * 128:(e + 1) * 128], ba[:, 0:128], Mmask[d], ALU.mult, [bar, 'cstb'], ["A%d_%d" % (b, e)])

            def S5(ii):
                d, si, lat, tt, first, last = order[ii]
                b = ii % DP
                bd, bdr = nbank(6)
                MM(bd[:, 0:256], khb[b], vtok[:, tt, :], True, True, ["kh%d" % b, 'vtok'], [bdr])

            def SB(ii):
                d, si, lat, tt, first, last = order[ii]
                b = ii % DP
                ep, epr = epb[b], "ep%d" % b
                qt, qtr, kh, khr, A, Ar = qtb[b], "qt%d" % b, khb[b], "kh%d" % b, Ab[b], "A%d" % b
                if first:
                    if lat:
                        src = (sf if d == 0 else sb)[l, 2 * gp:2 * gp + 2].rearrange("e k v -> (e k) v")
                        DMA('sp', Sm, src, (), ['Sm0', 'Sm1'], 'Sm')
                    else:
                        MSET('pool', Sm, 0.0, ['Sm0', 'Sm1'])
                    CP('act', Sbfs[(ii + 1) % 2], Sm, ['Sm0', 'Sm1'], ["Sbf%d" % ((ii + 1) % 2)])
                Sold, Soldr = Sbfs[(ii + 1) % 2], "Sbf%d" % ((ii + 1) % 2)
                Snew, Snewr = Sbfs[ii % 2], "Sbf%d" % (ii % 2)
                bd, bdr = nbank(6)
                bos = [nbank(4 + e) for e in range(2)]
                for e in range(2):
                    MM(bos[e][0][:, 0:128], A[:, e * 128:(e + 1) * 128], vtok[:, tt, e * 128:(e + 1) * 128], True, False,
                       ["%s_%d" % (Ar, e), 'vtok'], [bos[e][1]])
                for e in range(2):
                    MM(bos[e][0][:, 0:128], qt[e * 64:(e + 1) * 64, :], Sold[e * 64:(e + 1) * 64, :], False, True,
                       [qtr, Soldr], [bos[e][1]])
                dc = 127 if d == 0 else 0
                for e in range(2):
                    rows = slice(e * 64, (e + 1) * 64)
                    STT(Sm[rows, :], Sm[rows, :], ep[rows, dc:dc + 1], bd[rows, e * 128:(e + 1) * 128], ALU.mult, ALU.add,
                        ['Sm%d' % e, epr, bdr], ['Sm%d' % e])
                CP('act', Snew, Sm, ['Sm0', 'Sm1'], [Snewr])
                if last and not lat:
                    dst = (nsf if d == 0 else nsb)[si, l, 2 * gp:2 * gp + 2].rearrange("e k v -> (e k) v")
                    DMA('sp', dst, Sm, ['Sm0', 'Sm1'], ['nso'], 'Smo')

            def T1(ii):
                d, si, lat, tt, first, last = order[ii]
                b = ii % DP
                for e in range(2):
                    bo, bor = nbank(4 + e)
                    if d == 0:
                        CP('act', ofb[:, tt, e * 128:(e + 1) * 128], bo[:, 0:128], [bor], ["of%d" % tt])
                    else:
                        TT('dve', osb[b][:, e * 128:(e + 1) * 128], bo[:, 0:128], ofb[:, tt, e * 128:(e + 1) * 128], ALU.add,
                           [bor, "of%d" % tt], ["os%d_%d" % (b, e)])

            def T2(ii):
                d, si, lat, tt, first, last = order[ii]
                if d == 0:
                    return
                b = ii % DP
                c0 = 8 + 6 * (ii % 4)
                for e in range(2):
                    ACT(junkb, osb[b][:, e * 128:(e + 1) * 128], AF.Square, ["os%d_%d" % (b, e)], ['junkb', "gs%d" % (ii % 4)],
                        accum=sml[:, c0 + e:c0 + e + 1])
                ACT(sml[:, c0 + 2:c0 + 4], sml[:, c0:c0 + 2], AF.Ln, ["gs%d" % (ii % 4)], ["gl%d" % (ii % 4)], bias=EPS, scale=1.0 / 128)
                ACT(sml[:, c0 + 4:c0 + 6], sml[:, c0 + 2:c0 + 4], AF.Exp, ["gl%d" % (ii % 4)], ["gr%d" % (ii % 4)], scale=-0.5)

            def T3(ii):
                d, si, lat, tt, first, last = order[ii]
                if d == 0:
                    return
                b = ii % DP
                c0 = 8 + 6 * (ii % 4)
                on, onr = onb2[b], "on2%d" % b
                for e in range(2):
                    TS('dve', on[:, e * 128:(e + 1) * 128], osb[b][:, e * 128:(e + 1) * 128], sml[:, c0 + 4 + e:c0 + 5 + e],
                       None, ALU.mult, None, ["os%d_%d" % (b, e), "gr%d" % (ii % 4)], ["%s_%d" % (onr, e)])
                pt, ptr = ntbank(7)
                for e in range(2):
                    TR(pt[:, 256 + e * 128:256 + (e + 1) * 128], on[:, e * 128:(e + 1) * 128], identb, ["%s_%d" % (onr, e), 'cstb'], [ptr])

            def T4(ii):
                d, si, lat, tt, first, last = order[ii]
                if d == 0:
                    return
                tok = slice(tt * 128, (tt + 1) * 128)
                pt, ptr = ntbank(7)
                for e in range(2):
                    ACT(yT[:, 8 + 2 * gp + e, tok], pt[:, 256 + e * 128:256 + (e + 1) * 128], AF.Copy,
                        [ptr, 'pv'], ["y%d_%d" % (8 + 2 * gp + e, tt // 4)], scale=pv[:, 70:71])

            stages = [(T4, -4), (T3, -3), (T2, -2), (T1, -1), (S2, 5), (S1, 6), (SB, 0), (S5, 1), (S4, 2), (S3b, 3), (S3, 4)]
            for j in range(-6, no_ + 4):
                for fn_, off in stages:
                    ii = j + off
                    if 0 <= ii < no_:
                        fn_(ii)
            for e in range(2):
                gate_apply(5120 + (2 * gp + e) * 128, 8 + 2 * gp + e)
        wpre(('D', l), w_br[l, 0][:, 0:128], 4)
        barrier()
        if stop == (l, 'C'):
            raise _Stop()

        cv_ = Carve()
        mrg = cv_.get([128, 8, NT], BF16)
        macc = cv_.get([128, NT], F32)
        og = cv_.get([128, 8, 512], F32)
        xt = [cv_.get([128, 1024], F32) for _ in range(2)]
        if l + 1 < NL:
            mod_begin(l + 1)
            pmn, pmnr = nbank(7)
        for fc in range(8):
            for b in range(3):
                if l + 1 < NL:
                    mod_chunk(l + 1, fc * 3 + b, pmn, pmnr)
                swb = wload(w_br[l, b][:, fc * 128:(fc + 1) * 128], 4, tag=('D', l) if (fc == 0 and b == 0) else None)
                swg = wload(w_gate[l][:, b * 1024 + fc * 128:b * 1024 + (fc + 1) * 128])
                for g in range(NG):
                    zb, zr = nbank()
                    for kc in range(4):
                        MM(zb[:], wsl[:, swb, kc, :], yT[:, b * 4 + kc, g * 512:(g + 1) * 512], kc == 0, kc == 3,
                           ["ws%d" % swb, "y%d_%d" % (b * 4 + kc, g)], [zr])
                    gb, gr = proj(swg, g)
                    sg, sgr = ntf()
                    ACT(sg, gb[:], AF.Sigmoid, [gr, 'pv'], [sgr], bias=pv[:, 24 + b * 8 + fc:25 + b * 8 + fc])
                    mv = macc[:, g * 512:(g + 1) * 512]
                    mr = "macc%d" % g
                    if b == 0:
                        TT('dve', mv, zb[:], sg, ALU.mult, [zr, sgr], [mr])
                    else:
                        TT('dve', sg, zb[:], sg, ALU.mult, [zr, sgr], [sgr])
                        if b == 1:
                            TT('dve', mv, mv, sg, ALU.add, [mr, sgr], [mr])
                        else:
                            TT('dve', mrg[:, fc, g * 512:(g + 1) * 512], mv, sg, ALU.add, [mr, sgr], ["mrg%d" % g])
        if l + 1 < NL:
            mod_finish(l + 1, pmn, pmnr)
        for g in range(NG):
            s = 0 if g == 0 else 1
            for fc in range(8):
                sw = wload(w_out[l][:, fc * 128:(fc + 1) * 128])
                bank, br = nbank()
                for kc in range(8):
                    MM(bank[:], wsl[:, sw, kc, :], mrg[:, kc, g * 512:(g + 1) * 512], kc == 0, kc == 7, ["ws%d" % sw, "mrg%d" % g], [br])
                TS('dve', og[:, fc, :], bank[:], modT[:, 16 + fc, s:s + 1], None, ALU.mult, None, [br, modTr], ['og'])
            for j in range(4):
                tt = 4 * g + j
                b = tt % 2
                DMA('sp', xt[b], xcur[tt * 128:(tt + 1) * 128, :], ["%s%d" % (xcn, tt)], ["xo%d" % b], "xo%d" % b)
                for hh in range(2):
                    bank, br = nbank()
                    for c in range(4):
                        fc = hh * 4 + c
                        TR(bank[:, c * 128:(c + 1) * 128], og[:, fc, j * 128:(j + 1) * 128], identf, ['og', 'cstf'], [br])
                    TT('dve', xt[b][:, hh * 512:(hh + 1) * 512], bank[:], xt[b][:, hh * 512:(hh + 1) * 512], ALU.add,
                       [br, "xo%d" % b], ["xo%d" % b])
                DMA('sp', xnxt[tt * 128:(tt + 1) * 128, :], xt[b], ["xo%d" % b], ["%s%d" % (xnn, tt)], "xo%d" % b)

    try:
        for l_ in range(NL):
            layer(l_)
            if stop == (l_, 'D'):
                raise _Stop()
    except _Stop:
        barrier()
        dbg_h = nc.dram_tensor("dbg_h", [128, 8 * NT], BF16, kind="ExternalOutput").ap()
        dbg_y = nc.dram_tensor("dbg_y", [128, 12 * NT], BF16, kind="ExternalOutput").ap()
        DMA('sp', dbg_h, hT[:].rearrange("p a b -> p (a b)"), ['bar'], ['dbgh'], 'dbgh')
        DMA('sp', dbg_y, yT[:].rearrange("p a b -> p (a b)"), ['bar'], ['dbgy'], 'dbgy')
    S.emit(nc)
    es.close()
    return nc


def _consts():
    s = np.arange(128)[:, None]
    t = np.arange(128)[None, :]
    c = -1.0 / 16.0
    identf = np.eye(128, dtype=np.float32)
    Uf = np.where(s <= t, c, 0.0).astype(np.float32)
    Ub = np.where(s >= t, c, 0.0).astype(np.float32)
    SU = np.where(s > t, c, 0.0).astype(np.float32)
    SL = np.where(s < t, c, 0.0).astype(np.float32)
    Mf = (s <= t).astype(np.float32)
    Mb = (s >= t).astype(np.float32)
    bd = ((s // 64) == (t // 64)).astype(np.float32) / 64.0
    return np.concatenate([identf, Uf, Ub, SU, SL, Mf, Mb, bd], axis=1).astype(np.float32)


def _rope_tables():
    pos = np.arange(2048)
    n_f = 16
    inv = 10000.0 ** (-np.arange(n_f) / n_f)
    ang_r = (pos // 64)[:, None] * inv
    ang_c = (pos % 64)[:, None] * inv
    cos = np.zeros((64, 2048), np.float32)
    sin = np.zeros((64, 2048), np.float32)
    for i in range(64):
        ang = ang_r if i < 32 else ang_c
        j = i % 16
        a_part = (i % 32) < 16
        cos[i] = np.cos(ang[:, j].astype(np.float32))
        sn = np.sin(ang[:, j].astype(np.float32))
        sin[i] = -sn if a_part else sn
    return np.stack([np.tile(cos, (2, 1)), np.tile(sin, (2, 1))]).astype(np.float32)


def _swap_perm():
    i = np.arange(64)
    return np.where((i % 32) < 16, i + 16, i - 16)


def _b2_table(rpb, interior=False):
    L = rpb.shape[0]
    pad = np.concatenate([rpb.reshape(L, 8, -1), np.full((L, 8, 1), NEG, np.float32)], axis=2)
    p = np.arange(128)
    ph = (p // 64)[:, None, None]
    kc = (p % 64)[:, None, None]
    u = np.arange(16)[None, :, None]
    qc = np.arange(64)[None, None, :]
    if interior:
        dr = 7 - u + ph
        lo, hi = -4, 3
    else:
        dr = u - 8 + ph
        lo, hi = -7, 7
    csq = np.clip(qc - 8, 0, 48)
    valid = (dr >= lo) & (dr <= hi) & (kc >= csq) & (kc < csq + 16)
    dc = np.clip(kc - qc + 15, 0, 30)
    idx = np.where(valid, (np.clip(dr, -7, 7) + 7) * 31 + dc, 15 * 31)
    tab = pad[:, :, idx]
    return np.ascontiguousarray(tab.transpose(0, 2, 1, 3, 4)).reshape(L, 128, 8 * 1024).astype(np.float32)


_NC_CACHE = {}


def kernel(x_prompt, x_sample, c, cache_k, cache_v, state_fwd, state_bwd, c_ctx,
           norm_w, w_ada, b_ada, w_in, conv_w, q_norm_w, k_norm_w, rpb,
           w_alpha, b_alpha, gla_norm_w, w_branch, w_gate, b_gate, w_out, _ret_maps=False):
    f = lambda a: np.ascontiguousarray(np.asarray(a, dtype=np.float32))
    x_prompt, x_sample, c, cache_k, cache_v = map(f, (x_prompt, x_sample, c, cache_k, cache_v))
    state_fwd, state_bwd, c_ctx, norm_w, w_ada, b_ada, w_in = map(f, (state_fwd, state_bwd, c_ctx, norm_w, w_ada, b_ada, w_in))
    conv_w, q_norm_w, k_norm_w, rpb, w_alpha, b_alpha = map(f, (conv_w, q_norm_w, k_norm_w, rpb, w_alpha, b_alpha))
    gla_norm_w, w_branch, w_gate, b_gate, w_out = map(f, (gla_norm_w, w_branch, w_gate, b_gate, w_out))

    perm = _swap_perm()
    qcols = np.concatenate([4096 + h * 64 + perm for h in range(4)])
    kcols = np.concatenate([4352 + h * 64 + perm for h in range(4)])
    w_in_ext = np.ascontiguousarray(np.concatenate([w_in, w_in[:, :, qcols], w_in[:, :, kcols]], axis=2))
    pvec = np.zeros((NL, 128, 72), np.float32)
    pvec[:, :, 0:24] = b_ada.reshape(NL, 24, 128).transpose(0, 2, 1)
    pvec[:, :, 24:48] = b_gate.reshape(NL, 24, 128).transpose(0, 2, 1)
    pvec[:, :, 48:56] = norm_w.reshape(NL, 8, 128).transpose(0, 2, 1)
    pvec[:, :, 56:68] = conv_w.reshape(NL, 3, 4, 128).transpose(0, 3, 2, 1).reshape(NL, 128, 12)
    pvec[:, :, 68] = np.tile(q_norm_w, (1, 2))
    pvec[:, :, 69] = np.tile(k_norm_w, (1, 2))
    pvec[:, :, 70] = gla_norm_w
    walpha = np.zeros((NL, 33, 2, 2, 128), np.float32)
    for d in range(2):
        wa = w_alpha[:, d].reshape(NL, 16, 2, 128)
        walpha[:, d * 16:(d + 1) * 16, :, d, :] = wa
        walpha[:, 32, :, d, :] = b_alpha[:, d].reshape(NL, 2, 128)
    walpha = walpha.reshape(NL, 33, 512)
    b2tab = _b2_table(rpb)
    tfull = _b2_table(rpb, interior=True).reshape(NL, 128, 8, 16, 64)
    b2tabi = np.ascontiguousarray(np.stack(
        [np.concatenate([tfull[:, :, :, 11 - 2 * t, :], tfull[:, :, :, 12 - 2 * t, :]], axis=-1) for t in range(5)],
        axis=3)).reshape(NL, 128, 8 * 640)
    cst = _consts()
    rope = _rope_tables()

    in_maps = []
    for i in range(8):
        xin = np.ascontiguousarray(np.concatenate([x_prompt[2 * i], x_prompt[2 * i + 1], x_sample[i]], axis=0))
        cond = np.zeros((128, 8, 2), np.float32)
        cond[:, :, 0] = c_ctx.reshape(8, 128).T
        cond[:, :, 1] = c[i].reshape(8, 128).T
        in_maps.append(dict(
            xin=xin, w_in=w_in_ext, w_ada=w_ada, w_gate=w_gate, w_br=w_branch, w_out=w_out,
            cond=np.ascontiguousarray(cond.reshape(128, 16)), pvec=pvec, walpha=walpha, b2tab=b2tab, b2tabi=b2tabi,
            ck=np.ascontiguousarray(cache_k[i]), cv=np.ascontiguousarray(cache_v[i]),
            sf=np.ascontiguousarray(state_fwd[i]), sb=np.ascontiguousarray(state_bwd[i]),
            cst=cst, rope=rope))
    if _ret_maps:
        return in_maps
    if 'nc' not in _NC_CACHE:
        _NC_CACHE['nc'] = build_nc()
    nc = _NC_CACHE['nc']
    res = run_bass_kernel_spmd(nc, in_maps, core_ids=list(range(8)))
    rs = res.results
    y_p = np.zeros((16, 256, D), np.float32)
    y_s = np.zeros((8, 2048, D), np.float32)
    nk = np.zeros((16, NL, 8, 256, 64), np.float32)
    nv = np.zeros((16, NL, 8, 256, 64), np.float32)
    nsf = np.zeros((16, NL, 4, 64, 128), np.float32)
    nsb = np.zeros((16, NL, 4, 64, 128), np.float32)
    for i in range(8):
        y = rs[i]["y"]
        y_p[2 * i] = y[0:256]
        y_p[2 * i + 1] = y[256:512]
        y_s[i] = y[512:]
        nk[2 * i:2 * i + 2] = rs[i]["nk"]
        nv[2 * i:2 * i + 2] = rs[i]["nv"]
        nsf[2 * i:2 * i + 2] = rs[i]["nsf"]
        nsb[2 * i:2 * i + 2] = rs[i]["nsb"]
    return (y_p, y_s, nk, nv, nsf, nsb)
```
